# Optimizing a Trainium2 kernel written in Bass

```python
import math
import jax
import jax.numpy as jnp
from jax import lax
import numpy as np

D_MODEL = 1024
BATCH = 32
SEQ = 2048
DEPTH = 2
DEC_BATCH = 8
DEC_SEQ = 32
PAST_LEN = 4096

CHUNK = 64
N_META = 16
H_RET = 4
DK_RET = D_MODEL // 8
DV_RET = 2 * DK_RET
H_SB = 8
D_SB = D_MODEL // 8
Q_BLOCK = 128
POOL_WINDOWS = (2, 4, 8, 16)
N_POOL_GROUPS = 4
D_POOL = D_MODEL
DG_POOL = D_POOL // N_POOL_GROUPS
POOL_BUF = max(POOL_WINDOWS) - 1
N_BRANCH = 3
D_BRANCH = D_MODEL
D_FF = ((8 * D_MODEL // 3 + 127) // 128) * 128
ROPE_BASE = 10000.0
LN_EPS = 1e-5
ALPHA = (2 * DEPTH) ** 0.25
BETA = (8 * DEPTH) ** -0.25
IN_SPLITS = (H_RET * DK_RET, H_RET * DK_RET, H_RET * DV_RET, H_RET * DV_RET,
             H_SB * D_SB, H_SB * D_SB, H_SB * D_SB, D_POOL, N_BRANCH * D_MODEL)
D_IN = sum(IN_SPLITS)

kernel_name = 'hybrid_retention_stickbreak_pool_stream'


def layer_norm(x, g, b):
    xf = x.astype(jnp.float32)
    mu = xf.mean(-1, keepdims=True)
    var = jnp.square(xf - mu).mean(-1, keepdims=True)
    return ((xf - mu) * lax.rsqrt(var + LN_EPS) * g + b).astype(x.dtype)


def head_norm(x, g):
    xf = x.astype(jnp.float32)
    mu = xf.mean(-1, keepdims=True)
    var = jnp.square(xf - mu).mean(-1, keepdims=True)
    return ((xf - mu) * lax.rsqrt(var + LN_EPS) * g).astype(x.dtype)


def swiglu(x, w_up, w_down):
    gate, up = jnp.split(x @ w_up, 2, axis=-1)
    return (jax.nn.silu(gate) * up) @ w_down


def rotary(x, pos0):
    T = x.shape[1]
    half = x.shape[-1] // 2
    inv_freq = ROPE_BASE ** (-jnp.arange(half, dtype=jnp.float32) / half)
    ang = (pos0 + jnp.arange(T, dtype=jnp.float32))[:, None] * inv_freq[None, :]
    cos = jnp.cos(ang)[None, :, None, :]
    sin = jnp.sin(ang)[None, :, None, :]
    xf = x.astype(jnp.float32)
    x1, x2 = xf[..., :half], xf[..., half:]
    return jnp.concatenate([x1 * cos - x2 * sin, x2 * cos + x1 * sin], axis=-1)


def retention_log_decay():
    return jnp.log1p(-jnp.exp(jnp.linspace(math.log(1.0 / 32), math.log(1.0 / 512), H_RET,
                                           dtype=jnp.float32)))


def retention(q, k, v, s0):
    B, T, H, _ = q.shape
    L = min(CHUNK, T)
    n = T // L
    lg = retention_log_decay()
    i = jnp.arange(L, dtype=jnp.float32)
    diff = i[:, None] - i[None, :]
    d_intra = jnp.where(diff >= 0, jnp.exp(lg[:, None, None] * jnp.maximum(diff, 0.0)), 0.0)
    d_q = jnp.exp(lg[None, :] * (i[:, None] + 1.0))
    d_k = jnp.exp(lg[None, :] * (L - 1.0 - i[:, None]))
    d_c = jnp.exp(lg * L)

    def chunks(a):
        return a.astype(jnp.float32).reshape(B, n, L, H, a.shape[-1]).transpose(1, 0, 2, 3, 4)

    def step(s, xs):
        qc, kc, vc = xs
        att = jnp.einsum('blhk,bmhk->bhlm', qc, kc) * d_intra[None]
        o = (jnp.einsum('bhlm,bmhv->blhv', att, vc)
             + jnp.einsum('blhk,bhkv->blhv', qc, s) * d_q[None, :, :, None])
        s = s * d_c[None, :, None, None] + jnp.einsum('bmhk,bmhv->bhkv', kc * d_k[None, :, :, None], vc)
        return s, o

    s, o = lax.scan(step, s0.astype(jnp.float32), (chunks(q), chunks(k), chunks(v)))
    o = o.transpose(1, 0, 2, 3, 4).reshape(B, T, H, v.shape[-1])
    return o.astype(v.dtype), s.astype(s0.dtype)


def stick_breaking(q, k_all, v_all, q_start):
    B, T, H, D = q.shape
    S = k_all.shape[1]
    qb = min(Q_BLOCK, T)
    nb = T // qb
    scale = D ** -0.5
    key_idx = jnp.arange(S)
    q_blocks = q.reshape(B, nb, qb, H, D).transpose(1, 0, 2, 3, 4)

    def block(args):
        qblk, b = args
        z = jnp.einsum('bqhd,bshd->bhqs', qblk, k_all, preferred_element_type=jnp.float32) * scale
        q_idx = q_start + b * qb + jnp.arange(qb)
        visible = (key_idx[None, :] < q_idx[:, None])[None, None]
        log_stay = jnp.where(visible, jax.nn.log_sigmoid(-z), 0.0)
        later = lax.cumsum(log_stay, axis=3, reverse=True) - log_stay
        w = jnp.where(visible, jnp.exp(jax.nn.log_sigmoid(z) + later), 0.0)
        return jnp.einsum('bhqs,bshd->bqhd', w.astype(v_all.dtype), v_all,
                          preferred_element_type=jnp.float32).astype(v_all.dtype)

    out = lax.map(block, (q_blocks, jnp.arange(nb)))
    return out.transpose(1, 0, 2, 3, 4).reshape(B, T, H, D)


def pool_mixer(u, buf, mix_w, scale):
    B, T, C = u.shape
    Lb = buf.shape[1]
    z = jnp.concatenate([buf, u], axis=1)
    csum = jnp.concatenate([jnp.zeros((B, 1, C), jnp.float32),
                            jnp.cumsum(z.astype(jnp.float32), axis=1)], axis=1)
    idx = Lb + jnp.arange(T)
    hi = csum[:, Lb + 1:]
    uf = u.astype(jnp.float32)
    parts = []
    for g, w in enumerate(POOL_WINDOWS):
        sl = slice(g * DG_POOL, (g + 1) * DG_POOL)
        lo = jnp.maximum(idx + 1 - w, 0)
        cnt = jnp.minimum(idx + 1, w).astype(jnp.float32)
        parts.append((hi[..., sl] - csum[:, lo, sl]) / cnt[None, :, None] - uf[..., sl])
    pooled = jnp.stack(parts, axis=2).astype(u.dtype)
    mixed = jnp.einsum('btgc,gcd->btgd', pooled, mix_w).reshape(B, T, C) * scale
    return mixed, z[:, -POOL_BUF:]


def trunk_layer(h, past_k, past_v, s0, pool_buf, pos0, lp):
    (w_in, ret_g, pool_w, pool_scale, w_branch, w_out,
     up1, down1, up2, down2, ln_g, ln_b) = lp
    B, T, _ = h.shape
    h = layer_norm(ALPHA * h + 0.5 * swiglu(h, up1, down1), ln_g[0], ln_b[0])
    points = np.cumsum(IN_SPLITS)[:-1].tolist()
    q_r, k_r, v_r, g_r, q_s, k_s, v_s, u, gates = jnp.split(h @ w_in, points, axis=-1)
    q_r = rotary(q_r.reshape(B, T, H_RET, DK_RET), pos0)
    k_r = rotary(k_r.reshape(B, T, H_RET, DK_RET), pos0) * (DK_RET ** -0.5)
    o_r, s_new = retention(q_r, k_r, v_r.reshape(B, T, H_RET, DV_RET), s0)
    o_r = head_norm(o_r, ret_g).reshape(B, T, H_RET * DV_RET) * jax.nn.silu(g_r)
    k_s = k_s.reshape(B, T, H_SB, D_SB)
    v_s = v_s.reshape(B, T, H_SB, D_SB)
    k_all = jnp.concatenate([past_k, k_s], axis=1)
    v_all = jnp.concatenate([past_v, v_s], axis=1)
    o_s = stick_breaking(q_s.reshape(B, T, H_SB, D_SB), k_all, v_all, past_k.shape[1])
    o_s = o_s.reshape(B, T, H_SB * D_SB)
    o_p, buf_new = pool_mixer(u, pool_buf, pool_w, pool_scale)
    branch = jnp.stack([o_r, o_s, o_p], axis=2)
    proj_b = jnp.einsum('btnc,ncd->btnd', branch, w_branch)
    gate = jax.nn.sigmoid(gates.reshape(B, T, N_BRANCH, D_MODEL))
    mix = (gate * proj_b).sum(axis=2) @ w_out
    h = layer_norm(ALPHA * h + mix, ln_g[1], ln_b[1])
    h = layer_norm(ALPHA * h + 0.5 * swiglu(h, up2, down2), ln_g[2], ln_b[2])
    return h, (k_s, v_s, s_new, buf_new)


def setup_inputs(seed: int = 0) -> dict:
    key = jax.random.key(seed)
    ks = jax.random.split(key, 20)
    f32 = jnp.float32
    nrm = lambda k, shape: jax.random.normal(k, shape, f32)
    return {
        'x_prompt': nrm(ks[0], (BATCH, SEQ, D_MODEL)),
        'x_sample': nrm(ks[1], (DEC_BATCH, DEC_SEQ, D_MODEL)),
        'cache_sb_k': nrm(ks[2], (DEPTH, DEC_BATCH, PAST_LEN, H_SB, D_SB)),
        'cache_sb_v': nrm(ks[3], (DEPTH, DEC_BATCH, PAST_LEN, H_SB, D_SB)),
        'state_ret': 0.5 * nrm(ks[4], (DEPTH, DEC_BATCH, H_RET, DK_RET, DV_RET)),
        'state_pool': nrm(ks[5], (DEPTH, DEC_BATCH, POOL_BUF, D_POOL)),
        'meta_tokens': nrm(ks[6], (N_META, D_MODEL)),
        'w_in': nrm(ks[7], (DEPTH, D_MODEL, D_IN)) * D_MODEL ** -0.5,
        'ret_norm_g': 1.0 + 0.1 * nrm(ks[8], (DEPTH, H_RET, DV_RET)),
        'pool_mix_w': nrm(ks[9], (DEPTH, N_POOL_GROUPS, DG_POOL, DG_POOL)) * DG_POOL ** -0.5,
        'pool_scale': 1.0 + 0.1 * nrm(ks[10], (DEPTH, D_POOL)),
        'w_branch': nrm(ks[11], (DEPTH, N_BRANCH, D_BRANCH, D_MODEL)) * (D_BRANCH ** -0.5 * BETA),
        'w_out': nrm(ks[12], (DEPTH, D_MODEL, D_MODEL)) * (D_MODEL ** -0.5 * BETA),
        'ffn1_up': nrm(ks[13], (DEPTH, D_MODEL, 2 * D_FF)) * D_MODEL ** -0.5,
        'ffn1_down': nrm(ks[14], (DEPTH, D_FF, D_MODEL)) * (D_FF ** -0.5 * BETA),
        'ffn2_up': nrm(ks[15], (DEPTH, D_MODEL, 2 * D_FF)) * D_MODEL ** -0.5,
        'ffn2_down': nrm(ks[16], (DEPTH, D_FF, D_MODEL)) * (D_FF ** -0.5 * BETA),
        'ln_g': 1.0 + 0.1 * nrm(ks[17], (DEPTH, 3, D_MODEL)),
        'ln_b': 0.02 * nrm(ks[18], (DEPTH, 3, D_MODEL)),
    }


def reference(x_prompt, x_sample, cache_sb_k, cache_sb_v, state_ret, state_pool, meta_tokens,
              w_in, ret_norm_g, pool_mix_w, pool_scale, w_branch, w_out,
              ffn1_up, ffn1_down, ffn2_up, ffn2_down, ln_g, ln_b):
    dt = meta_tokens.dtype
    B = x_prompt.shape[0]
    bc = lambda a: jnp.broadcast_to(a, (B,) + a.shape[1:])
    h_meta = meta_tokens[None]
    h_p = x_prompt
    h_s = x_sample
    pk_l, pv_l, ps_l, pb_l = [], [], [], []
    sk_l, sv_l, ss_l, sb_l = [], [], [], []
    for l in range(DEPTH):
        lp = (w_in[l], ret_norm_g[l], pool_mix_w[l], pool_scale[l], w_branch[l], w_out[l],
              ffn1_up[l], ffn1_down[l], ffn2_up[l], ffn2_down[l], ln_g[l], ln_b[l])
        empty_kv = jnp.zeros((1, 0, H_SB, D_SB), dt)
        h_meta, (mk, mv, ms, mbuf) = trunk_layer(
            h_meta, empty_kv, empty_kv, jnp.zeros((1, H_RET, DK_RET, DV_RET), dt),
            jnp.zeros((1, 0, D_POOL), dt), 0, lp)
        h_p, (pk, pv, ps, pbuf) = trunk_layer(
            h_p, bc(mk), bc(mv), bc(ms), bc(mbuf), N_META, lp)
        h_s, (sk, sv, ss, sbuf) = trunk_layer(
            h_s, cache_sb_k[l], cache_sb_v[l], state_ret[l], state_pool[l], N_META + PAST_LEN, lp)
        pk_l.append(jnp.concatenate([bc(mk), pk], axis=1))
        pv_l.append(jnp.concatenate([bc(mv), pv], axis=1))
        ps_l.append(ps)
        pb_l.append(pbuf)
        sk_l.append(sk)
        sv_l.append(sv)
        ss_l.append(ss)
        sb_l.append(sbuf)
    new_sb_k_prompt = jnp.stack(pk_l)
    new_sb_v_prompt = jnp.stack(pv_l)
    new_ret_prompt = jnp.stack(ps_l)
    new_pool_prompt = jnp.stack(pb_l)
    new_sb_k_sample = jnp.stack(sk_l)
    new_sb_v_sample = jnp.stack(sv_l)
    new_ret_sample = jnp.stack(ss_l)
    new_pool_sample = jnp.stack(sb_l)
    return (h_p, h_s, new_sb_k_prompt, new_sb_v_prompt, new_ret_prompt, new_pool_prompt,
            new_sb_k_sample, new_sb_v_sample, new_ret_sample, new_pool_sample)
```

```python
import math
from contextlib import ExitStack
import numpy as np
import concourse.bass as bass
import concourse.mybir as mybir
from concourse.bass_utils import run_bass_kernel_spmd

F32 = mybir.dt.float32
BF16 = mybir.dt.bfloat16
AF = mybir.ActivationFunctionType
ALU = mybir.AluOpType

D = 1024
DFF = 2816
NMETA = 16
HR, DK, DV = 4, 128, 256
HS, DS = 8, 128
DEPTH = 2
PB = 15
ALPHA = (2 * DEPTH) ** 0.25
CRES = 0.5 / ALPHA
EPS_LN = 1e-5 / (ALPHA * ALPHA)
EPS_HN = 1e-5
NCORES = 8
NSLOT = 3
WMAX = 11 * 512
CASTW = 8192


def lg_decay():
    return np.log1p(-np.exp(np.linspace(math.log(1.0 / 32), math.log(1.0 / 512), HR, dtype=np.float32))).astype(np.float64)


def wtile_list():
    L = []
    for j in range(11):
        L.append(("up1", j, 8, 512))
    for nh in range(2):
        for kh in range(2):
            L.append(("dn1", (nh, kh), 11, 512))
    for nm in ("qr", "kr"):
        L.append((nm, 0, 8, 512))
    for nm in ("vr", "gr", "gt0", "wb0", "qs", "ks", "vs", "gt1", "wb1", "u"):
        L.append((nm, 0, 8, 512))
        L.append((nm, 1, 8, 512))
    L.append(("gt2", 0, 8, 512))
    L.append(("gt2", 1, 8, 512))
    L.append(("pm", 0, 8, 256))
    for nm in ("wb2", "wo"):
        L.append((nm, 0, 8, 512))
        L.append((nm, 1, 8, 512))
    for j in range(11):
        L.append(("up2", j, 8, 512))
    for nh in range(2):
        for kh in range(2):
            L.append(("dn2", (nh, kh), 11, 512))
    return L


WT = wtile_list()
WOFF = []
_o = 0
for _t in WT:
    WOFF.append(_o)
    _o += _t[2] * _t[3]
WTOT = _o
NCAST = (WTOT + CASTW - 1) // CASTW


def pack_tile(W, kc_n, cols):
    sub = W[:, cols]
    K = kc_n * 128
    assert sub.shape[0] == K
    return sub.reshape(kc_n, 128, len(cols)).transpose(1, 0, 2).reshape(128, kc_n * len(cols))


def pack_layer(l, inp):
    w_in = inp["w_in"][l]
    offs = np.cumsum([0, 512, 512, 1024, 1024, 1024, 1024, 1024, 1024, 3072])
    col0 = {"qr": offs[0], "kr": offs[1], "vr": offs[2], "gr": offs[3], "qs": offs[4], "ks": offs[5],
            "vs": offs[6], "u": offs[7], "gt0": offs[8], "gt1": offs[8] + 1024, "gt2": offs[8] + 2048}
    out = np.empty((128, WTOT), np.float32)
    for (nm, idx, kc_n, nc_), off in zip(WT, WOFF):
        if nm in ("up1", "up2"):
            W = inp["ffn1_up" if nm == "up1" else "ffn2_up"][l]
            cols = np.concatenate([np.arange(256 * idx, 256 * idx + 256), DFF + np.arange(256 * idx, 256 * idx + 256)])
            t = pack_tile(W, 8, cols)
        elif nm in ("dn1", "dn2"):
            W = inp["ffn1_down" if nm == "dn1" else "ffn2_down"][l]
            nh, kh = idx
            t = pack_tile(W[kh * 1408:(kh + 1) * 1408], 11, np.arange(nh * 512, nh * 512 + 512))
        elif nm in col0:
            t = pack_tile(w_in, 8, col0[nm] + np.arange(idx * 512, idx * 512 + 512))
        elif nm in ("wb0", "wb1", "wb2"):
            t = pack_tile(inp["w_branch"][l][int(nm[2])], 8, np.arange(idx * 512, idx * 512 + 512))
        elif nm == "wo":
            t = pack_tile(inp["w_out"][l], 8, np.arange(idx * 512, idx * 512 + 512))
        elif nm == "pm":
            pmw = inp["pool_mix_w"][l]
            t = pmw.reshape(4, 2, 128, 256).transpose(2, 0, 1, 3).reshape(128, 8 * 256)
        else:
            raise KeyError(nm)
        out[:, off:off + kc_n * nc_] = t
    return out


class Prog:
    def __init__(self):
        self.ins = []
        self.res = {}

    def add(self, eng, fn, reads=(), writes=(), dma=None):
        idx = len(self.ins)
        deps = set()
        for k in reads:
            st = self.res.setdefault(k, [None, {}])
            if st[0] is not None:
                deps.add(st[0])
        for k in writes:
            st = self.res.setdefault(k, [None, {}])
            if st[0] is not None:
                deps.add(st[0])
            deps.update(st[1].values())
        rkey = ("dma", idx) if dma else eng
        for k in reads:
            self.res[k][1][rkey] = idx
        for k in writes:
            self.res[k] = [idx, {}]
        deps.discard(idx)
        if eng == "pe":
            deps = {d for d in deps if self.ins[d]["eng"] != "pe" or self.ins[d]["dma"]}
        self.ins.append(dict(eng=eng, fn=fn, deps=sorted(deps), dma=dma))
        return idx

    def emit(self, nc, es):
        n = len(self.ins)
        flagged = [False] * n
        for it in self.ins:
            for d in it["deps"]:
                flagged[d] = True
        EPOCH = 12000
        NHW, NSW = 12, 6
        sems = {}

        def getsem(name):
            if name not in sems:
                sems[name] = es.enter_context(nc.semaphore(name))
            return sems[name]

        tok = [None] * n
        cnt = {}
        dcount = {}
        rr = {"hw": 0, "sw": 0}
        prevdma = [None] * n
        lastdma = {}
        for i, it in enumerate(self.ins):
            if it["dma"]:
                kind = it["dma"]
                nm = "d%s%d" % (kind, rr[kind] % (NHW if kind == "hw" else NSW))
                rr[kind] += 1
                dcount[nm] = dcount.get(nm, 0) + 1
                tok[i] = (nm, 16 * dcount[nm])
                prevdma[i] = lastdma.get(nm)
                lastdma[nm] = i
            elif flagged[i]:
                e = it["eng"]
                c = cnt.get(e, 0)
                cnt[e] = c + 1
                tok[i] = ("c%s%d" % (e, c // EPOCH), (c % EPOCH) + 1)
        byeng = {}
        for i, it in enumerate(self.ins):
            byeng.setdefault(it["eng"], []).append(i)
        final_waits = dict((nm, 16 * c) for nm, c in dcount.items())

        def run(eng_name, e):
            waited = {}
            for i in byeng.get(eng_name, []):
                it = self.ins[i]
                deps = list(it["deps"])
                if prevdma[i] is not None:
                    deps.append(prevdma[i])
                for d in deps:
                    nm, v = tok[d]
                    if waited.get(nm, 0) >= v:
                        continue
                    e.wait_ge(getsem(nm), v)
                    waited[nm] = v
                bi = it["fn"](e)
                if tok[i] is not None:
                    nm, v = tok[i]
                    bi.then_inc(getsem(nm), 16 if it["dma"] else 1)
            if eng_name == "sp":
                for nm, v in final_waits.items():
                    e.wait_ge(getsem(nm), v)

        for t in tok:
            if t is not None:
                getsem(t[0])
        with nc.Block() as block:
            @block.sync
            def _(e):
                run("sp", e)

            @block.scalar
            def _(e):
                run("act", e)

            @block.vector
            def _(e):
                run("dve", e)

            @block.gpsimd
            def _(e):
                run("pool", e)

            @block.tensor
            def _(e):
                run("pe", e)


class Cfg:
    def __init__(self, nseq, seq, past, dec_seq=32):
        self.nseq, self.seq, self.past, self.dec = nseq, seq, past, dec_seq


def build(cfg):
    nc = bass.Bass("TRN2", target_bir_lowering=False)
    P = Prog()
    es = ExitStack()
    NSEQ, SEQ, PAST, DEC = cfg.nseq, cfg.seq, cfg.past, cfg.dec
    NT = SEQ // 128

    def din(name, shape, dt=F32):
        return nc.dram_tensor(name, list(shape), dt, kind="ExternalInput").ap()

    def dout(name, shape, dt=F32):
        return nc.dram_tensor(name, list(shape), dt, kind="ExternalOutput").ap()

    x = din("x", [NSEQ, SEQ, D])
    xs = din("xs", [DEC, D])
    ck = din("ck", [DEPTH, PAST, D])
    cv = din("cv", [DEPTH, PAST, D])
    sret = din("sret", [DEPTH, HR, DK, DV])
    spool = din("spool", [DEPTH, PB, D])
    meta = din("meta", [NMETA, D])
    wfl = [din("wflat%d" % l, [128, WTOT]) for l in range(DEPTH)]
    lng = din("lng", [DEPTH * 3, D])
    lnb = din("lnb", [DEPTH * 3, D])
    retg = din("retg", [DEPTH, D])
    pscale = din("pscale", [DEPTH, D])
    ropec = din("ropec", [NMETA + SEQ + DEC, 64])
    ropes = din("ropes", [NMETA + SEQ + DEC, 128])
    cst = din("cst", [128, 640])
    cstb = din("cstb", [128, 512], BF16)
    rcnt = din("rcnt", [128, 8 * NMETA])

    y = dout("y", [NSEQ, SEQ, D])
    ys = dout("ys", [DEC, D])
    nk = dout("nk", [DEPTH, NSEQ, NMETA + SEQ, D])
    nv = dout("nv", [DEPTH, NSEQ, NMETA + SEQ, D])
    nret = dout("nret", [DEPTH, NSEQ, HR, DK, DV])
    npool = dout("npool", [DEPTH, NSEQ, PB, D])
    sk = dout("sk", [DEPTH, DEC, D])
    sv = dout("sv", [DEPTH, DEC, D])
    sreto = dout("sreto", [DEPTH, HR, DK, DV])
    spoolo = dout("spoolo", [DEPTH, PB, D])
    wbf = [nc.dram_tensor("wbf%d" % l, [128, WTOT], BF16, kind="Internal").ap() for l in range(DEPTH)]

    def sb(name, shape, dt=F32):
        return es.enter_context(nc.sbuf_tensor(name, list(shape), dt))

    def ps(name, shape, dt=F32):
        return es.enter_context(nc.psum_tensor(name, list(shape), dt))

    hres = sb("hres", [128, D])
    hT = sb("hT", [128, 8, 128], BF16)
    wsl = [sb("wsl%d" % i, [128, WMAX], BF16) for i in range(NSLOT)]
    S = [sb("S%d" % l, [128, HR, DV]) for l in range(DEPTH)]
    Sm = [sb("Sm%d" % l, [128, HR, DV]) for l in range(DEPTH)]
    Sbf = sb("Sbf", [128, HR, DV], BF16)
    zT = [sb("zT%d" % l, [128, 8, PB + 128]) for l in range(DEPTH)]
    zm = [sb("zm%d" % l, [128, 8, PB]) for l in range(DEPTH)]
    C = sb("C", [128, 640])
    CB = sb("CB", [128, 512], BF16)
    RC = sb("RC", [128, 8, NMETA])
    lnG = sb("lnG", [128, D])
    lnB = sb("lnB", [128, D])
    rc = sb("rc", [128, 64])
    rs = sb("rs", [128, 128])
    hb = sb("hb", [128, D], BF16)
    act_tok = sb("act_tok", [128, DFF], BF16)
    actT = sb("actT", [128, 22, 128], BF16)
    stmp = sb("stmp", [128, 256], BF16)
    f1 = sb("f1", [128, D])
    f2 = sb("f2", [128, D])
    mix = sb("mix", [128, D])
    st6 = sb("st6", [128, 8, 6])
    mv = sb("mv", [128, 4, 2])
    sc1 = sb("sc1", [128, 4])
    sc2 = sb("sc2", [128, 4])
    qt = sb("qt", [128, 512], BF16)
    kt = sb("kt", [128, 512], BF16)
    qkT = sb("qkT", [128, 8, 128], BF16)
    vr = sb("vr", [128, D], BF16)
    gr = sb("gr", [128, D], BF16)
    th = sb("th", [128, D], BF16)
    attm = sb("attm", [128, 4, 128], BF16)
    oT = sb("oT", [128, 8, 128], BF16)
    qsT = sb("qsT", [128, 8, 128], BF16)
    kb_own = sb("kb_own", [128, D], BF16)
    kT_own = sb("kT_own", [128, 8, 128], BF16)
    v_own = sb("v_own", [128, D], BF16)
    kpb = [sb("kpb%d" % i, [128, D], BF16) for i in range(2)]
    vpb = [sb("vpb%d" % i, [128, D], BF16) for i in range(2)]
    kTb = [sb("kTb%d" % i, [128, 8, 128], BF16) for i in range(2)]
    eb = [sb("eb%d" % i, [128, 4, 128]) for i in range(2)]
    spb = [sb("spb%d" % i, [128, 4, 128], BF16) for i in range(2)]
    wb_ = [sb("wb%d" % i, [128, 4, 128], BF16) for i in range(2)]
    ltc = [sb("ltc%d" % i, [128, 4, 128]) for i in range(2)]
    ltcb = [sb("ltcb%d" % i, [128, 4, 128], BF16) for i in range(2)]
    pooledT = sb("pooledT", [128, 8, 128], BF16)
    ptmp = sb("ptmp", [128, 2, PB + 128])
    ptmp2 = sb("ptmp2", [128, 2, PB + 128])

    pf = [ps("pf%d" % i, [128, 512]) for i in range(6)]
    pb = [ps("pb%d" % i, [128, 1024], BF16) for i in range(2)]
    pfi = [0]
    pbi = [0]

    def nf():
        i = pfi[0] % 4
        pfi[0] += 1
        return i

    def nb():
        i = pbi[0] % 2
        pbi[0] += 1
        return i

    identf = C[:, 0:128]
    cmask = C[:, 128:256]
    identb = CB[:, 0:128]
    dmask = CB[:, 128:256]
    ntri = CB[:, 256:384]
    nones = CB[:, 384:512]

    def DMA(out, in_, reads, writes, eng="sp"):
        kind = "sw" if eng == "pool" else "hw"
        P.add(eng, lambda e: e.dma_start(out=out, in_=in_), reads, writes, dma=kind)

    def MM(out, lhsT, rhs, start, stop, reads, writes):
        P.add("pe", lambda e: e.matmul(out, lhsT=lhsT, rhs=rhs, start=start, stop=stop), reads, writes)

    def TR(out, in_, np_, reads, writes):
        P.add("pe", lambda e: e.transpose(out, in_, identb[:np_, :np_]), list(reads) + ["CB"], writes)

    def ACTF(out, in_, func, reads, writes, scale=1.0, bias=0.0):
        P.add("act", lambda e: e.activation(out=out, in_=in_, func=func, bias=bias, scale=scale), reads, writes)

    def TT(eng, out, in0, in1, op, reads, writes):
        P.add(eng, lambda e: e.tensor_tensor(out=out, in0=in0, in1=in1, op=op), reads, writes)

    def STT(eng, out, in0, scalar, in1, op0, op1, reads, writes):
        P.add(eng, lambda e: e.scalar_tensor_tensor(out=out, in0=in0, scalar=scalar, in1=in1, op0=op0, op1=op1), reads, writes)

    def TS(eng, out, in0, s1, s2, op0, op1, reads, writes):
        if s2 is None:
            P.add(eng, lambda e: e.tensor_scalar(out=out, in0=in0, scalar1=s1, scalar2=None, op0=op0), reads, writes)
        else:
            P.add(eng, lambda e: e.tensor_scalar(out=out, in0=in0, scalar1=s1, scalar2=s2, op0=op0, op1=op1), reads, writes)

    def CP(eng, out, in_, reads, writes):
        if eng == "act":
            P.add("act", lambda e: e.activation(out=out, in_=in_, func=AF.Copy), reads, writes)
        else:
            P.add(eng, lambda e: e.tensor_copy(out=out, in_=in_), reads, writes)

    def MSET(eng, ap, val, writes):
        P.add(eng, lambda e: e.memset(ap, val), (), writes)

    MSET("pool", ptmp[:, :, :], 0.0, ["ptmp"])
    MSET("pool", ptmp2[:, :, :], 0.0, ["ptmp2"])
    DMA(C[:, :], cst[:, :], (), ["C"])
    DMA(CB[:, :], cstb[:, :], (), ["CB"])
    DMA(RC[:, :, :], rcnt.rearrange("p (c t) -> p c t", c=8), (), ["RC"])
    for l in range(DEPTH):
        for c in range(NCAST):
            c0, c1 = c * CASTW, min(WTOT, (c + 1) * CASTW)
            P.add("pool", (lambda e, l=l, c0=c0, c1=c1: e.dma_start(out=wbf[l][:, c0:c1], in_=wfl[l][:, c0:c1],
                                                                   max_dma_last_dim=4096)),
                  (), [("wbf", l, c)], dma="sw")

    ws = dict(n=0, seq=[])

    def w_issue(k):
        l, ti = ws["seq"][k]
        nm, idx, kc_n, ncol = WT[ti]
        off = WOFF[ti]
        width = kc_n * ncol
        slot = k % NSLOT
        cs = range(off // CASTW, (off + width - 1) // CASTW + 1)
        DMA(wsl[slot][:, 0:width], wbf[l][:, off:off + width], [("wbf", l, c) for c in cs], ["wsl%d" % slot])

    def w_start(seqlist):
        ws["seq"] = seqlist
        ws["n"] = 0
        for k in range(min(NSLOT, len(seqlist))):
            w_issue(k)

    def w_get(expect):
        k = ws["n"]
        l, ti = ws["seq"][k]
        nm, idx, kc_n, ncol = WT[ti]
        assert nm == expect, (nm, expect)
        slot = k % NSLOT
        view = wsl[slot][:, 0:kc_n * ncol].rearrange("p (k n) -> p k n", k=kc_n)
        return view, "wsl%d" % slot

    def w_done():
        k = ws["n"]
        ws["n"] = k + 1
        if k + NSLOT < len(ws["seq"]):
            w_issue(k + NSLOT)

    def transpose_to(dstT, src_tok, T, nchunk, srckey, dstkey, col0=0):
        c = 0
        while c < nchunk:
            n = min(8, nchunk - c)
            b = nb()
            for j in range(n):
                TR(pb[b][:, j * 128:j * 128 + T], src_tok[:T, col0 + (c + j) * 128: col0 + (c + j + 1) * 128], T,
                   [srckey], ["pb%d" % b])
            CP("act", dstT[:, c:c + n, :T], pb[b][:, 0:n * 128].rearrange("p (c t) -> p c t", c=n)[:, :, :T],
               ["pb%d" % b], [dstkey])
            c += n

    def proj_tok(T, inT, inkey, wname, nparts, evac):
        for part in range(nparts):
            w, wkey = w_get(wname)
            kc_n = w.shape[1]
            ncol = w.shape[2]
            b = nf()
            for kc in range(kc_n):
                MM(pf[b][:T, 0:ncol], inT[:, kc, :T], w[:, kc, :], kc == 0, kc == kc_n - 1, [inkey, wkey], ["pf%d" % b])
            w_done()
            evac(part, pf[b], "pf%d" % b)

    def layer_norm(T, l, which):
        row = l * 3 + which
        DMA(lnG[:T, :], lng[row:row + 1, :].to_broadcast([T, D]), (), ["lnG"])
        DMA(lnB[:T, :], lnb[row:row + 1, :].to_broadcast([T, D]), (), ["lnB"])
        for c in range(2):
            P.add("dve", lambda e, c=c: e.bn_stats(out=st6[:T, c, :], in_=hres[:T, c * 512:(c + 1) * 512]), ["hres"], ["st6"])
        P.add("dve", lambda e: e.bn_aggr(out=mv[:T, 0, :], in_=st6[:T, 0:2, :]), ["st6"], ["mv"])
        rstd_from_var(T, 1, EPS_LN)
        ACTF(f1[:T, :], hres[:T, :], AF.Identity, ["hres", "sc1", "sc2"], ["f1"], scale=sc1[:T, 0:1], bias=sc2[:T, 0:1])
        TT("dve", f1[:T, :], f1[:T, :], lnG[:T, :], ALU.mult, ["f1", "lnG"], ["f1"])
        TT("dve", hres[:T, :], f1[:T, :], lnB[:T, :], ALU.add, ["f1", "lnB"], ["hres"])
        CP("pool", hb[:T, :], hres[:T, :], ["hres"], ["hb"])
        transpose_to(hT, hb, T, 8, "hb", "hT")

    def rstd_from_var(T, n, eps):
        ACTF(sc1[:T, 0:n], mv[:T, 0:n, 1], AF.Ln, ["mv"], ["sc1"], bias=eps)
        ACTF(sc1[:T, 0:n], sc1[:T, 0:n], AF.Exp, ["sc1"], ["sc1"], scale=-0.5)
        STT("dve", sc2[:T, 0:n], mv[:T, 0:n, 0], -1.0, sc1[:T, 0:n], ALU.mult, ALU.mult, ["mv", "sc1"], ["sc2"])

    def ffn(T, l, which):
        upn, dnn = ("up1", "dn1") if which == 0 else ("up2", "dn2")
        for j in range(11):
            w, wkey = w_get(upn)
            b = nf()
            for kc in range(8):
                MM(pf[b][:T, :], hT[:, kc, :T], w[:, kc, :], kc == 0, kc == 7, ["hT", wkey], ["pf%d" % b])
            w_done()
            ACTF(stmp[:T, :], pf[b][:T, 0:256], AF.Silu, ["pf%d" % b], ["stmp"])
            TT("dve", act_tok[:T, j * 256:(j + 1) * 256], pf[b][:T, 256:512], stmp[:T, :], ALU.mult,
               ["pf%d" % b, "stmp"], ["act_tok"])
        transpose_to(actT, act_tok, T, 22, "act_tok", "actT")
        for nh in range(2):
            b = nf()
            for kh in range(2):
                w, wkey = w_get(dnn)
                for kc in range(11):
                    MM(pf[b][:T, :], actT[:, kh * 11 + kc, :T], w[:, kc, :], kh == 0 and kc == 0, kh == 1 and kc == 10,
                       ["actT", wkey], ["pf%d" % b])
                w_done()
            STT("dve", hres[:T, nh * 512:(nh + 1) * 512], pf[b][:T, :], CRES, hres[:T, nh * 512:(nh + 1) * 512],
                ALU.mult, ALU.add, ["pf%d" % b, "hres"], ["hres"])
        layer_norm(T, l, 0 if which == 0 else 2)

    def gate_mix(T, bi, first):
        def ev(part, pbank, pkey):
            sl = slice(part * 512, (part + 1) * 512)
            if first:
                STT("dve", mix[:T, sl], th[:T, sl], 1.0, pbank[:T, :], ALU.add, ALU.mult, ["th", pkey], ["mix"])
            else:
                STT("dve", f2[:T, sl], th[:T, sl], 1.0, pbank[:T, :], ALU.add, ALU.mult, ["th", pkey], ["f2"])
                TT("pool", mix[:T, sl], mix[:T, sl], f2[:T, sl], ALU.add, ["mix", "f2"], ["mix"])
        proj_tok(T, oT, "oT", "wb%d" % bi, 2, ev)

    def gates(T, bi):
        def ev(part, pbank, pkey):
            ACTF(th[:T, part * 512:(part + 1) * 512], pbank[:T, :], AF.Tanh, [pkey], ["th"], scale=0.5)
        proj_tok(T, hT, "hT", "gt%d" % bi, 2, ev)

    def rotary(T, dst, dkey, pbank, pkey, sccol):
        xv = pbank[:T, :].rearrange("p (h t d) -> p h t d", h=4, t=2)
        cosb = rc[:T, :].unsqueeze(1).unsqueeze(1).to_broadcast([T, 4, 2, 64])
        TT("dve", f1[:T, 0:512].rearrange("p (h t d) -> p h t d", h=4, t=2), xv, cosb, ALU.mult, [pkey, "rc"], ["f1"])
        f2v = f2[:T, 0:512].rearrange("p (h t d) -> p h t d", h=4, t=2)
        for t in range(2):
            sinb = rs[:T, t * 64:(t + 1) * 64].unsqueeze(1).to_broadcast([T, 4, 64])
            TT("dve", f2v[:, :, t, :], xv[:, :, 1 - t, :], sinb, ALU.mult, [pkey, "rs"], ["f2"])
        TT("pool", f1[:T, 0:512], f1[:T, 0:512], f2[:T, 0:512], ALU.add, ["f1", "f2"], ["f1"])
        scb = C[:T, sccol:sccol + 4].unsqueeze(2).to_broadcast([T, 4, 128])
        TT("pool", dst[:T, :].rearrange("p (h d) -> p h d", h=4), f1[:T, 0:512].rearrange("p (h d) -> p h d", h=4), scb,
           ALU.mult, ["f1", "C"], [dkey])

    def retention(T, l, Sl, Skey, dccol):
        qk_keys = {}

        def ev_q(part, pbank, pkey):
            rotary(T, qt, "qt", pbank, pkey, 256)

        def ev_k(part, pbank, pkey):
            rotary(T, kt, "kt", pbank, pkey, 260)

        proj_tok(T, hT, "hT", "qr", 1, ev_q)
        proj_tok(T, hT, "hT", "kr", 1, ev_k)

        def ev_v(part, pbank, pkey):
            CP("act", vr[:T, part * 512:(part + 1) * 512], pbank[:T, :], [pkey], ["vr"])
        proj_tok(T, hT, "hT", "vr", 2, ev_v)

        def ev_g(part, pbank, pkey):
            ACTF(gr[:T, part * 512:(part + 1) * 512], pbank[:T, :], AF.Silu, [pkey], ["gr"])
        proj_tok(T, hT, "hT", "gr", 2, ev_g)
        gates(T, 0)
        b = nb()
        for h in range(4):
            TR(pb[b][:, h * 128:h * 128 + T], qt[:T, h * 128:(h + 1) * 128], T, ["qt"], ["pb%d" % b])
        for h in range(4):
            TR(pb[b][:, (4 + h) * 128:(4 + h) * 128 + T], kt[:T, h * 128:(h + 1) * 128], T, ["kt"], ["pb%d" % b])
        CP("act", qkT[:, :, :T], pb[b][:, :].rearrange("p (c t) -> p c t", c=8)[:, :, :T], ["pb%d" % b], ["qkT"])
        CP("pool", Sbf[:, :, :], Sl[:, :, :], [Skey], ["Sbf"])
        b = nf()
        for h in range(4):
            MM(pf[b][:T, h * 128:h * 128 + T], qkT[:, 4 + h, :T], qkT[:, h, :T], h == 0, h == 3, ["qkT"], ["pf%d" % b])
        TT("dve", attm[:T, :, :T], pf[b][:T, :].rearrange("p (h t) -> p h t", h=4)[:, :, :T],
           cmask[:T, :T].unsqueeze(1).to_broadcast([T, 4, T]), ALU.mult, ["pf%d" % b, "C"], ["attm"])
        ob = [nf(), nf()]
        for h in range(4):
            bb = ob[h // 2]
            osl = pf[bb][:T, (h % 2) * 256:(h % 2) * 256 + 256]
            MM(osl, attm[:T, h, :T], vr[:T, h * 256:(h + 1) * 256], h % 2 == 0, False, ["attm", "vr"], ["pf%d" % bb])
            MM(osl, qkT[:, h, :T], Sbf[:, h, :], False, h % 2 == 1, ["qkT", "Sbf"], ["pf%d" % bb])
        for i in range(2):
            CP("act", f1[:T, i * 512:(i + 1) * 512], pf[ob[i]][:T, :], ["pf%d" % ob[i]], ["f1"])
        sbk = [nf(), nf()]
        for h in range(4):
            bb = sbk[h // 2]
            MM(pf[bb][:, (h % 2) * 256:(h % 2) * 256 + 256], kt[:T, h * 128:(h + 1) * 128], vr[:T, h * 256:(h + 1) * 256],
               h % 2 == 0, h % 2 == 1, ["kt", "vr"], ["pf%d" % bb])
        for i in range(2):
            TT("dve", Sl[:, 2 * i:2 * i + 2, :], pf[sbk[i]][:, :].rearrange("p (h v) -> p h v", h=2), Sl[:, 2 * i:2 * i + 2, :],
               ALU.add, ["pf%d" % sbk[i], Skey], [Skey])
        TT("dve", Sl[:, :, :], Sl[:, :, :], C[:, dccol:dccol + 4].unsqueeze(2).to_broadcast([128, 4, DV]), ALU.mult,
           [Skey, "C"], [Skey])
        for h in range(4):
            P.add("dve", lambda e, h=h: e.bn_stats(out=st6[:T, h, :], in_=f1[:T, h * 256:(h + 1) * 256]), ["f1"], ["st6"])
        for h in range(4):
            P.add("dve", lambda e, h=h: e.bn_aggr(out=mv[:T, h, :], in_=st6[:T, h:h + 1, :]), ["st6"], ["mv"])
        rstd_from_var(T, 4, EPS_HN)
        for h in range(4):
            ACTF(f2[:T, h * 256:(h + 1) * 256], f1[:T, h * 256:(h + 1) * 256], AF.Identity, ["f1", "sc1", "sc2"], ["f2"],
                 scale=sc1[:T, h:h + 1], bias=sc2[:T, h:h + 1])
        DMA(lnG[:T, :], retg[l:l + 1, :].to_broadcast([T, D]), (), ["lnG"])
        TT("dve", f2[:T, :], f2[:T, :], lnG[:T, :], ALU.mult, ["f2", "lnG"], ["f2"])
        TT("dve", hb[:T, :], f2[:T, :], gr[:T, :], ALU.mult, ["f2", "gr"], ["hb"])
        transpose_to(oT, hb, T, 8, "hb", "oT")
        gate_mix(T, 0, True)

    def stick(T, l, kdst, vdst, past):
        def ev_q(part, pbank, pkey):
            ACTF(hb[:T, part * 512:(part + 1) * 512], pbank[:T, :], AF.Copy, [pkey], ["hb"], scale=DS ** -0.5)
        proj_tok(T, hT, "hT", "qs", 2, ev_q)
        transpose_to(qsT, hb, T, 8, "hb", "qsT")

        def ev_k(part, pbank, pkey):
            CP("act", f1[:T, part * 512:(part + 1) * 512], pbank[:T, :], [pkey], ["f1"])
        proj_tok(T, hT, "hT", "ks", 2, ev_k)
        for (ap, wk) in kdst:
            DMA(ap, f1[:T, :], ["f1"], [wk])
        CP("pool", kb_own[:T, :], f1[:T, :], ["f1"], ["kb_own"])
        transpose_to(kT_own, kb_own, T, 8, "kb_own", "kT_own")

        def ev_v(part, pbank, pkey):
            CP("act", f2[:T, part * 512:(part + 1) * 512], pbank[:T, :], [pkey], ["f2"])
        proj_tok(T, hT, "hT", "vs", 2, ev_v)
        for (ap, wk) in vdst:
            DMA(ap, f2[:T, :], ["f2"], [wk])
        CP("pool", v_own[:T, :], f2[:T, :], ["f2"], ["v_own"])
        gates(T, 1)
        blocks = [("own", None, None, T, None)] + [("past",) + p for p in past]
        for hg in range(2):
            MSET("pool", ltc[hg][:, :, :], 0.0, ["ltc%d" % hg])
        ob = [4, 5]
        nblk = len(blocks)
        for bi_, blk in enumerate(blocks):
            kind = blk[0]
            nkk = blk[3]
            if kind == "own":
                kTs, kTkey, vs_, vkey = kT_own, "kT_own", v_own, "v_own"
            else:
                s = bi_ % 2
                DMA(kpb[s][:nkk, :], blk[1], blk[4], ["kpb%d" % s], eng="pool")
                DMA(vpb[s][:nkk, :], blk[2], blk[4], ["vpb%d" % s], eng="pool")
                transpose_to(kTb[s], kpb[s], nkk, 8, "kpb%d" % s, "kTb%d" % s)
                kTs, kTkey, vs_, vkey = kTb[s], "kTb%d" % s, vpb[s], "vpb%d" % s
            for hg in range(2):
                zb = nf()
                for hh in range(4):
                    h = hg * 4 + hh
                    MM(pf[zb][:nkk, hh * 128:hh * 128 + T], kTs[:, h, :nkk], qsT[:, h, :T], hh == 0, hh == 3,
                       [kTkey, "qsT"], ["pf%d" % zb])
                zv = pf[zb][:nkk, :].rearrange("p (h t) -> p h t", h=4)[:, :, :T]
                ACTF(eb[hg][:nkk, :, :T], zv, AF.Exp, ["pf%d" % zb], ["eb%d" % hg])
                ACTF(spb[hg][:nkk, :, :T], eb[hg][:nkk, :, :T], AF.Ln, ["eb%d" % hg], ["spb%d" % hg], bias=1.0)
                if kind == "own":
                    TT("pool", spb[hg][:nkk, :, :T], spb[hg][:nkk, :, :T],
                       dmask[:nkk, :T].unsqueeze(1).to_broadcast([nkk, 4, T]), ALU.mult, ["spb%d" % hg, "CB"], ["spb%d" % hg])
                bb = nf()
                for hh in range(4):
                    h = hg * 4 + hh
                    osl = pf[bb][:nkk, hh * 128:hh * 128 + T]
                    MM(osl, kTs[:, h, :nkk], qsT[:, h, :T], hh == 0, False, [kTkey, "qsT"], ["pf%d" % bb])
                    last = (bi_ == 0)
                    MM(osl, ntri[:nkk, :nkk], spb[hg][:nkk, hh, :T], False, last and hh == 3, ["CB", "spb%d" % hg], ["pf%d" % bb])
                    if bi_ > 0:
                        MM(osl, nones[:, :nkk], ltcb[hg][:, hh, :T], False, hh == 3, ["CB", "ltcb%d" % hg], ["pf%d" % bb])
                bv = pf[bb][:nkk, :].rearrange("p (h t) -> p h t", h=4)[:, :, :T]
                ACTF(wb_[hg][:nkk, :, :T], bv, AF.Exp, ["pf%d" % bb], ["wb%d" % hg])
                if kind == "own":
                    TT("pool", wb_[hg][:nkk, :, :T], wb_[hg][:nkk, :, :T],
                       dmask[:nkk, :T].unsqueeze(1).to_broadcast([nkk, 4, T]), ALU.mult, ["wb%d" % hg, "CB"], ["wb%d" % hg])
                for hh in range(4):
                    h = hg * 4 + hh
                    MM(pf[ob[hg]][:T, hh * 128:(hh + 1) * 128], wb_[hg][:nkk, hh, :T], vs_[:nkk, h * 128:(h + 1) * 128],
                       bi_ == 0 and hh == 0, bi_ == nblk - 1 and hh == 3, ["wb%d" % hg, vkey], ["pf%d" % ob[hg]])
                if bi_ < nblk - 1:
                    TT("pool", ltc[hg][:nkk, :, :T], ltc[hg][:nkk, :, :T], spb[hg][:nkk, :, :T], ALU.add,
                       ["ltc%d" % hg, "spb%d" % hg], ["ltc%d" % hg])
                    CP("pool", ltcb[hg][:, :, :T], ltc[hg][:, :, :T], ["ltc%d" % hg], ["ltcb%d" % hg])
        for hg in range(2):
            CP("act", hb[:T, hg * 512:(hg + 1) * 512], pf[ob[hg]][:T, :], ["pf%d" % ob[hg]], ["hb"])
        transpose_to(oT, hb, T, 8, "hb", "oT")
        gate_mix(T, 1, False)

    def pool_mixer(T, l, zl, zkey, is_meta, pooldst):
        for part in range(2):
            w, wkey = w_get("u")
            for oc in range(4):
                b = nf()
                for kc in range(8):
                    MM(pf[b][:, :T], w[:, kc, oc * 128:(oc + 1) * 128], hT[:, kc, :T], kc == 0, kc == 7, ["hT", wkey], ["pf%d" % b])
                CP("act", zl[:, part * 4 + oc, PB:PB + T], pf[b][:, :T], ["pf%d" % b], [zkey])
            w_done()
        gates(T, 2)
        W_ = PB + T
        for g, wnd in enumerate((2, 4, 8, 16)):
            cs = slice(2 * g, 2 * g + 2)
            cur = zl[:, cs, :]
            sh = 1
            bufs = [ptmp, ptmp2]
            bi_ = 0
            curkey = zkey
            while sh < wnd:
                dst = bufs[bi_ % 2]
                dkey = "ptmp" if bi_ % 2 == 0 else "ptmp2"
                TT("pool", dst[:, :, sh:W_], cur[:, :, sh:W_], cur[:, :, 0:W_ - sh], ALU.add, [curkey], [dkey])
                cur, curkey = dst, dkey
                sh *= 2
                bi_ += 1
            if is_meta:
                TT("dve", f1[:, 0:2 * T].rearrange("p (c t) -> p c t", c=2), cur[:, :, PB:PB + T], RC[:, cs, :T], ALU.mult,
                   [curkey, "RC"], ["f1"])
                TT("dve", pooledT[:, cs, :T], f1[:, 0:2 * T].rearrange("p (c t) -> p c t", c=2), zl[:, cs, PB:PB + T],
                   ALU.subtract, ["f1", zkey], ["pooledT"])
            else:
                STT("dve", pooledT[:, cs, :T], cur[:, :, PB:PB + T], 1.0 / wnd, zl[:, cs, PB:PB + T], ALU.mult, ALU.subtract,
                    [curkey, zkey], ["pooledT"])
        w, wkey = w_get("pm")
        ob = [nf(), nf()]
        for g in range(4):
            bb = ob[g // 2]
            for cc in range(2):
                MM(pf[bb][:T, (g % 2) * 256:(g % 2) * 256 + 256], pooledT[:, g * 2 + cc, :T], w[:, g * 2 + cc, :],
                   g % 2 == 0 and cc == 0, g % 2 == 1 and cc == 1, ["pooledT", wkey], ["pf%d" % bb])
        w_done()
        DMA(lnG[:T, :], pscale[l:l + 1, :].to_broadcast([T, D]), (), ["lnG"])
        for i in range(2):
            TT("dve", hb[:T, i * 512:(i + 1) * 512], pf[ob[i]][:T, :], lnG[:T, i * 512:(i + 1) * 512], ALU.mult,
               ["pf%d" % ob[i], "lnG"], ["hb"])
        transpose_to(oT, hb, T, 8, "hb", "oT")
        gate_mix(T, 2, False)
        if pooldst is not None:
            src0 = PB + T - 128 if T >= 128 else None
            for half in range(2):
                b = nf()
                for j in range(4):
                    c = half * 4 + j
                    if T >= 128:
                        P.add("pe", lambda e, b=b, j=j, c=c: e.transpose(pf[b][:, j * 128:(j + 1) * 128], zl[:, c, PB + T - 128:PB + T], identf),
                              [zkey, "C"], ["pf%d" % b])
                    else:
                        P.add("pe", lambda e, b=b, j=j, c=c: e.transpose(pf[b][:PB + T, j * 128:(j + 1) * 128], zl[:, c, 0:PB + T], identf),
                              [zkey, "C"], ["pf%d" % b])
                rows = 128 if T >= 128 else PB + T
                CP("act", f1[:rows, half * 512:(half + 1) * 512], pf[b][:rows, :], ["pf%d" % b], ["f1"])
            rows = 128 if T >= 128 else PB + T
            for (ap, wk) in pooldst:
                DMA(ap, f1[rows - PB:rows, :], ["f1"], [wk])
        CP("pool", zl[:, :, 0:PB], zl[:, :, T:T + PB], [zkey], [zkey])

    def out_proj(T, l):
        CP("act", hb[:T, :], mix[:T, :], ["mix"], ["hb"])
        transpose_to(oT, hb, T, 8, "hb", "oT")

        def ev(part, pbank, pkey):
            sl = slice(part * 512, (part + 1) * 512)
            STT("dve", hres[:T, sl], pbank[:T, :], CRES, hres[:T, sl], ALU.mult, ALU.add, [pkey, "hres"], ["hres"])
        proj_tok(T, oT, "oT", "wo", 2, ev)
        layer_norm(T, l, 1)

    def load_rope(T, row0):
        DMA(rc[:T, :], ropec[row0:row0 + T, :], (), ["rc"])
        DMA(rs[:T, :], ropes[row0:row0 + T, :], (), ["rs"])

    def load_tile(T, src):
        DMA(hres[:T, :], src, (), ["hres"])
        CP("pool", hb[:T, :], hres[:T, :], ["hres"], ["hb"])
        transpose_to(hT, hb, T, 8, "hb", "hT")

    def layer(T, l, Sl, Skey, dccol, zl, zkey, is_meta, kdst, vdst, past, pooldst):
        ffn(T, l, 0)
        retention(T, l, Sl, Skey, dccol)
        stick(T, l, kdst, vdst, past)
        pool_mixer(T, l, zl, zkey, is_meta, pooldst)
        out_proj(T, l)
        ffn(T, l, 1)

    ntiles_total = 2 + NSEQ * NT
    seqlist = []
    for _ in range(ntiles_total):
        for l in range(DEPTH):
            for ti in range(len(WT)):
                seqlist.append((l, ti))
    w_start(seqlist)

    T = NMETA
    load_rope(T, 0)
    load_tile(T, meta[:, :])
    for l in range(DEPTH):
        MSET("pool", Sm[l][:, :, :], 0.0, ["Sm%d" % l])
        MSET("pool", zT[l][:, :, :], 0.0, ["zT%d" % l])
        kd = [(nk[l, s, 0:NMETA, :], ("nk", l, s, "m")) for s in range(NSEQ)]
        vd = [(nv[l, s, 0:NMETA, :], ("nv", l, s, "m")) for s in range(NSEQ)]
        layer(T, l, Sm[l], "Sm%d" % l, 272, zT[l], "zT%d" % l, True, kd, vd, [], None)
        CP("pool", zm[l][:, :, :], zT[l][:, :, 0:PB], ["zT%d" % l], ["zm%d" % l])

    T = DEC
    load_rope(T, NMETA + SEQ)
    load_tile(T, xs[:, :])
    for l in range(DEPTH):
        DMA(S[l][:, :, :], sret[l].rearrange("h k v -> k h v"), (), ["S%d" % l])
        DMA(f1[:PB, :], spool[l, :, :], (), ["f1"])
        for half in range(2):
            b = nf()
            for j in range(4):
                c = half * 4 + j
                P.add("pe", lambda e, b=b, j=j, c=c: e.transpose(pf[b][:, j * PB:(j + 1) * PB], f1[:PB, c * 128:(c + 1) * 128], identf[:PB, :PB]),
                      ["f1", "C"], ["pf%d" % b])
            CP("act", zT[l][:, half * 4:half * 4 + 4, 0:PB], pf[b][:, 0:4 * PB].rearrange("p (c t) -> p c t", c=4),
               ["pf%d" % b], ["zT%d" % l])
        nblk = PAST // 128
        past = [(ck[l, bk * 128:(bk + 1) * 128, :], cv[l, bk * 128:(bk + 1) * 128, :], 128, []) for bk in reversed(range(nblk))]
        layer(T, l, S[l], "S%d" % l, 268, zT[l], "zT%d" % l, False, [(sk[l, :, :], ("sk", l))], [(sv[l, :, :], ("sv", l))],
              past, [(spoolo[l, :, :], ("spoolo", l))])
        DMA(sreto[l].rearrange("h k v -> k h v"), S[l][:, :, :], ["S%d" % l], [("sreto", l)])
    DMA(ys[:, :], hres[:T, :], ["hres"], ["ys"])

    T = 128
    for s in range(NSEQ):
        for l in range(DEPTH):
            CP("pool", S[l][:, :, :], Sm[l][:, :, :], ["Sm%d" % l], ["S%d" % l])
            CP("pool", zT[l][:, :, 0:PB], zm[l][:, :, :], ["zm%d" % l], ["zT%d" % l])
        for it in range(NT):
            load_rope(T, NMETA + it * 128)
            load_tile(T, x[s, it * 128:(it + 1) * 128, :])
            for l in range(DEPTH):
                r0 = NMETA + it * 128
                kd = [(nk[l, s, r0:r0 + 128, :], ("nk", l, s, it))]
                vd = [(nv[l, s, r0:r0 + 128, :], ("nv", l, s, it))]
                past = []
                for pt in reversed(range(it)):
                    p0 = NMETA + pt * 128
                    past.append((nk[l, s, p0:p0 + 128, :], nv[l, s, p0:p0 + 128, :], 128, [("nk", l, s, pt), ("nv", l, s, pt)]))
                past.append((nk[l, s, 0:NMETA, :], nv[l, s, 0:NMETA, :], NMETA, [("nk", l, s, "m"), ("nv", l, s, "m")]))
                pd = [(npool[l, s, :, :], ("npool", l, s))] if it == NT - 1 else None
                layer(T, l, S[l], "S%d" % l, 264, zT[l], "zT%d" % l, False, kd, vd, past, pd)
            DMA(y[s, it * 128:(it + 1) * 128, :], hres[:T, :], ["hres"], [("y", s, it)])
        for l in range(DEPTH):
            DMA(nret[l, s].rearrange("h k v -> k h v"), S[l][:, :, :], ["S%d" % l], [("nret", l, s)])

    P.emit(nc, es)
    return nc, es


def make_consts(seq, dec, past):
    import ml_dtypes
    lg = lg_decay()
    cst = np.zeros((128, 640), np.float32)
    cst[:, 0:128] = np.eye(128, dtype=np.float32)
    m = np.arange(128)
    cst[:, 128:256] = (m[None, :] >= m[:, None]).astype(np.float32)
    for h in range(HR):
        cst[:, 256 + h] = np.exp(lg[h] * (m + 1.0))
        cst[:, 260 + h] = (DK ** -0.5) * np.exp(-lg[h] * (m + 1.0))
        cst[:, 264 + h] = np.exp(lg[h] * 128.0)
        cst[:, 268 + h] = np.exp(lg[h] * float(dec))
        cst[:, 272 + h] = np.exp(lg[h] * float(NMETA))
    cb = np.zeros((128, 512), np.float32)
    cb[:, 0:128] = np.eye(128)
    cb[:, 128:256] = (m[:, None] < m[None, :])
    cb[:, 256:384] = -(m[:, None] >= m[None, :]).astype(np.float32)
    cb[:, 384:512] = -1.0
    cstb = cb.astype(ml_dtypes.bfloat16)
    half = 64
    inv = (10000.0 ** (-np.arange(half, dtype=np.float32) / np.float32(half))).astype(np.float32)
    pos = np.concatenate([np.arange(NMETA), NMETA + np.arange(seq), NMETA + past + np.arange(dec)]).astype(np.float32)
    ang = (pos[:, None] * inv[None, :]).astype(np.float32)
    cos = np.cos(ang).astype(np.float32)
    sin = np.sin(ang).astype(np.float32)
    ropec = cos
    ropes = np.concatenate([-sin, sin], axis=1).astype(np.float32)
    rc = np.zeros((128, 8, NMETA), np.float32)
    t = np.arange(NMETA)
    for g, w in enumerate((2, 4, 8, 16)):
        rc[:, 2 * g:2 * g + 2, :] = (1.0 / np.minimum(t + 1, w))[None, None, :]
    return cst, cstb, ropec, ropes, rc.reshape(128, 8 * NMETA)


_CACHE = {}


def run(inputs, cfg_seq=None):
    inp = {k: np.asarray(v) for k, v in inputs.items()}
    B, SEQ, _ = inp["x_prompt"].shape
    DB, DEC, _ = inp["x_sample"].shape
    PAST = inp["cache_sb_k"].shape[2]
    assert B % NCORES == 0 and DB == NCORES
    NSEQ = B // NCORES
    cfg = Cfg(NSEQ, SEQ, PAST, DEC)
    nc, es = build(cfg)
    cst, cstb, ropec, ropes, rcnt = make_consts(SEQ, DEC, PAST)
    wf = [pack_layer(l, inp) for l in range(DEPTH)]
    in_maps = []
    for c in range(NCORES):
        in_maps.append({
            "x": np.ascontiguousarray(inp["x_prompt"][c * NSEQ:(c + 1) * NSEQ]),
            "xs": np.ascontiguousarray(inp["x_sample"][c]),
            "ck": np.ascontiguousarray(inp["cache_sb_k"][:, c].reshape(DEPTH, PAST, D)),
            "cv": np.ascontiguousarray(inp["cache_sb_v"][:, c].reshape(DEPTH, PAST, D)),
            "sret": np.ascontiguousarray(inp["state_ret"][:, c]),
            "spool": np.ascontiguousarray(inp["state_pool"][:, c]),
            "meta": inp["meta_tokens"],
            "wflat0": wf[0], "wflat1": wf[1],
            "lng": np.ascontiguousarray(inp["ln_g"].reshape(DEPTH * 3, D)),
            "lnb": np.ascontiguousarray(inp["ln_b"].reshape(DEPTH * 3, D)),
            "retg": np.ascontiguousarray(inp["ret_norm_g"].reshape(DEPTH, D)),
            "pscale": np.ascontiguousarray(inp["pool_scale"].reshape(DEPTH, D)),
            "ropec": ropec, "ropes": ropes, "cst": cst, "cstb": cstb, "rcnt": rcnt,
        })
    return nc, es, in_maps, cfg


def assemble(results, cfg):
    NSEQ, SEQ, DEC = cfg.nseq, cfg.seq, cfg.dec
    cat = lambda k, ax: np.concatenate([r[k] for r in results], axis=ax)
    y = cat("y", 0)
    ys = np.stack([r["ys"] for r in results], 0)
    nk = cat("nk", 1).reshape(DEPTH, NSEQ * NCORES, NMETA + SEQ, HS, DS)
    nv = cat("nv", 1).reshape(DEPTH, NSEQ * NCORES, NMETA + SEQ, HS, DS)
    nret = cat("nret", 1)
    npool = cat("npool", 1)
    sk = np.stack([r["sk"] for r in results], 1).reshape(DEPTH, NCORES, DEC, HS, DS)
    sv = np.stack([r["sv"] for r in results], 1).reshape(DEPTH, NCORES, DEC, HS, DS)
    sreto = np.stack([r["sreto"] for r in results], 1)
    spoolo = np.stack([r["spoolo"] for r in results], 1)
    return tuple(np.ascontiguousarray(a, dtype=np.float32) for a in (y, ys, nk, nv, nret, npool, sk, sv, sreto, spoolo))


def kernel(**inputs):
    nc, es, in_maps, cfg = run(inputs)
    with es:
        res = run_bass_kernel_spmd(nc, in_maps, core_ids=list(range(NCORES)))
    return assemble(res.results, cfg)
```

```python
import math
from contextlib import ExitStack
import numpy as np
import concourse.bass as bass
import concourse.mybir as mybir
from concourse.bass_utils import run_bass_kernel_spmd

F32 = mybir.dt.float32
BF16 = mybir.dt.bfloat16
AF = mybir.ActivationFunctionType
ALU = mybir.AluOpType

D = 1024
DFF = 2816
NMETA = 16
HR, DK, DV = 4, 128, 256
HS, DS = 8, 128
DEPTH = 2
PB = 15
ALPHA = (2 * DEPTH) ** 0.25
CRES = 0.5 / ALPHA
EPS_LN = 1e-5 / (ALPHA * ALPHA)
EPS_HN = 1e-5
NCORES = 8
NSLOT = 3
WMAX = 11 * 512
CASTW = 8192


def lg_decay():
    return np.log1p(-np.exp(np.linspace(math.log(1.0 / 32), math.log(1.0 / 512), HR, dtype=np.float32))).astype(np.float64)


def wtile_list():
    L = []
    for j in range(11):
        L.append(("up1", j, 8, 512))
    for nh in range(2):
        for kh in range(2):
            L.append(("dn1", (nh, kh), 11, 512))
    for nm in ("qr", "kr"):
        L.append((nm, 0, 8, 512))
    for nm in ("vr", "gr", "gt0", "wb0", "qs", "ks", "vs", "gt1", "wb1", "u"):
        L.append((nm, 0, 8, 512))
        L.append((nm, 1, 8, 512))
    L.append(("gt2", 0, 8, 512))
    L.append(("gt2", 1, 8, 512))
    L.append(("pm", 0, 8, 256))
    for nm in ("wb2", "wo"):
        L.append((nm, 0, 8, 512))
        L.append((nm, 1, 8, 512))
    for j in range(11):
        L.append(("up2", j, 8, 512))
    for nh in range(2):
        for kh in range(2):
            L.append(("dn2", (nh, kh), 11, 512))
    return L


WT = wtile_list()
WOFF = []
_o = 0
for _t in WT:
    WOFF.append(_o)
    _o += _t[2] * _t[3]
WTOT = _o
NCAST = (WTOT + CASTW - 1) // CASTW


def pack_tile(W, kc_n, cols):
    sub = W[:, cols]
    K = kc_n * 128
    assert sub.shape[0] == K
    return sub.reshape(kc_n, 128, len(cols)).transpose(1, 0, 2).reshape(128, kc_n * len(cols))


def pack_layer(l, inp):
    w_in = inp["w_in"][l]
    offs = np.cumsum([0, 512, 512, 1024, 1024, 1024, 1024, 1024, 1024, 3072])
    col0 = {"qr": offs[0], "kr": offs[1], "vr": offs[2], "gr": offs[3], "qs": offs[4], "ks": offs[5],
            "vs": offs[6], "u": offs[7], "gt0": offs[8], "gt1": offs[8] + 1024, "gt2": offs[8] + 2048}
    out = np.empty((128, WTOT), np.float32)
    for (nm, idx, kc_n, nc_), off in zip(WT, WOFF):
        if nm in ("up1", "up2"):
            W = inp["ffn1_up" if nm == "up1" else "ffn2_up"][l]
            cols = np.concatenate([np.arange(256 * idx, 256 * idx + 256), DFF + np.arange(256 * idx, 256 * idx + 256)])
            t = pack_tile(W, 8, cols)
        elif nm in ("dn1", "dn2"):
            W = inp["ffn1_down" if nm == "dn1" else "ffn2_down"][l]
            nh, kh = idx
            t = pack_tile(W[kh * 1408:(kh + 1) * 1408], 11, np.arange(nh * 512, nh * 512 + 512))
        elif nm in col0:
            t = pack_tile(w_in, 8, col0[nm] + np.arange(idx * 512, idx * 512 + 512))
        elif nm in ("wb0", "wb1", "wb2"):
            t = pack_tile(inp["w_branch"][l][int(nm[2])], 8, np.arange(idx * 512, idx * 512 + 512))
        elif nm == "wo":
            t = pack_tile(inp["w_out"][l], 8, np.arange(idx * 512, idx * 512 + 512))
        elif nm == "pm":
            pmw = inp["pool_mix_w"][l]
            t = pmw.reshape(4, 2, 128, 256).transpose(2, 0, 1, 3).reshape(128, 8 * 256)
        else:
            raise KeyError(nm)
        out[:, off:off + kc_n * nc_] = t
    return out


class Prog:
    def __init__(self):
        self.ins = []
        self.res = {}

    def add(self, eng, fn, reads=(), writes=(), dma=None):
        idx = len(self.ins)
        deps = set()
        for k in reads:
            st = self.res.setdefault(k, [None, {}])
            if st[0] is not None:
                deps.add(st[0])
        for k in writes:
            st = self.res.setdefault(k, [None, {}])
            if st[0] is not None:
                deps.add(st[0])
            deps.update(st[1].values())
        rkey = ("dma", idx) if dma else eng
        for k in reads:
            self.res[k][1][rkey] = idx
        for k in writes:
            self.res[k] = [idx, {}]
        deps.discard(idx)
        if eng == "pe":
            deps = {d for d in deps if self.ins[d]["eng"] != "pe" or self.ins[d]["dma"]}
        self.ins.append(dict(eng=eng, fn=fn, deps=sorted(deps), dma=dma))
        return idx

    def emit(self, nc, es):
        n = len(self.ins)
        flagged = [False] * n
        for it in self.ins:
            for d in it["deps"]:
                flagged[d] = True
        EPOCH = 12000
        NHW, NSW = 12, 6
        sems = {}

        def getsem(name):
            if name not in sems:
                sems[name] = es.enter_context(nc.semaphore(name))
            return sems[name]

        tok = [None] * n
        cnt = {}
        dcount = {}
        rr = {"hw": 0, "sw": 0}
        prevdma = [None] * n
        lastdma = {}
        for i, it in enumerate(self.ins):
            if it["dma"]:
                kind = it["dma"]
                nm = "d%s%d" % (kind, rr[kind] % (NHW if kind == "hw" else NSW))
                rr[kind] += 1
                dcount[nm] = dcount.get(nm, 0) + 1
                tok[i] = (nm, 16 * dcount[nm])
                prevdma[i] = lastdma.get(nm)
                lastdma[nm] = i
            elif flagged[i]:
                e = it["eng"]
                c = cnt.get(e, 0)
                cnt[e] = c + 1
                tok[i] = ("c%s%d" % (e, c // EPOCH), (c % EPOCH) + 1)
        byeng = {}
        for i, it in enumerate(self.ins):
            byeng.setdefault(it["eng"], []).append(i)
        final_waits = dict((nm, 16 * c) for nm, c in dcount.items())

        def run(eng_name, e):
            waited = {}
            for i in byeng.get(eng_name, []):
                it = self.ins[i]
                deps = list(it["deps"])
                if prevdma[i] is not None:
                    deps.append(prevdma[i])
                need = {}
                for d in deps:
                    nm, v = tok[d]
                    if waited.get(nm, 0) >= v:
                        continue
                    need[nm] = max(need.get(nm, 0), v)
                need = list(need.items())
                for nm, v in need[:-1]:
                    e.wait_ge(getsem(nm), v)
                    waited[nm] = v
                bi = it["fn"](e)
                if need:
                    nm, v = need[-1]
                    bi._wait_ge(getsem(nm), v)
                    waited[nm] = v
                if tok[i] is not None:
                    nm, v = tok[i]
                    bi.then_inc(getsem(nm), 16 if it["dma"] else 1)
            if eng_name == "sp":
                for nm, v in final_waits.items():
                    e.wait_ge(getsem(nm), v)

        for t in tok:
            if t is not None:
                getsem(t[0])
        with nc.Block() as block:
            @block.sync
            def _(e):
                run("sp", e)

            @block.scalar
            def _(e):
                run("act", e)

            @block.vector
            def _(e):
                run("dve", e)

            @block.gpsimd
            def _(e):
                run("pool", e)

            @block.tensor
            def _(e):
                run("pe", e)


class Cfg:
    def __init__(self, nseq, seq, past, dec_seq=32):
        self.nseq, self.seq, self.past, self.dec = nseq, seq, past, dec_seq


def build(cfg):
    nc = bass.Bass("TRN2", target_bir_lowering=False)
    P = Prog()
    es = ExitStack()
    NSEQ, SEQ, PAST, DEC = cfg.nseq, cfg.seq, cfg.past, cfg.dec
    NT = SEQ // 128

    def din(name, shape, dt=F32):
        return nc.dram_tensor(name, list(shape), dt, kind="ExternalInput").ap()

    def dout(name, shape, dt=F32):
        return nc.dram_tensor(name, list(shape), dt, kind="ExternalOutput").ap()

    x = din("x", [NSEQ, SEQ, D])
    xs = din("xs", [DEC, D])
    ck = din("ck", [DEPTH, PAST, D])
    cv = din("cv", [DEPTH, PAST, D])
    sret = din("sret", [DEPTH, HR, DK, DV])
    spool = din("spool", [DEPTH, PB, D])
    meta = din("meta", [NMETA, D])
    wfl = [din("wflat%d" % l, [128, WTOT]) for l in range(DEPTH)]
    lng = din("lng", [DEPTH * 3, D])
    lnb = din("lnb", [DEPTH * 3, D])
    retg = din("retg", [DEPTH, D])
    pscale = din("pscale", [DEPTH, D])
    ropec = din("ropec", [NMETA + SEQ + DEC, 64])
    ropes = din("ropes", [NMETA + SEQ + DEC, 128])
    cst = din("cst", [128, 640])
    cstb = din("cstb", [128, 512], BF16)
    rcnt = din("rcnt", [128, 8 * NMETA])

    y = dout("y", [NSEQ, SEQ, D])
    ys = dout("ys", [DEC, D])
    nk = dout("nk", [DEPTH, NSEQ, NMETA + SEQ, D])
    nv = dout("nv", [DEPTH, NSEQ, NMETA + SEQ, D])
    nret = dout("nret", [DEPTH, NSEQ, HR, DK, DV])
    npool = dout("npool", [DEPTH, NSEQ, PB, D])
    sk = dout("sk", [DEPTH, DEC, D])
    sv = dout("sv", [DEPTH, DEC, D])
    sreto = dout("sreto", [DEPTH, HR, DK, DV])
    spoolo = dout("spoolo", [DEPTH, PB, D])
    wbf = [nc.dram_tensor("wbf%d" % l, [128, WTOT], BF16, kind="Internal").ap() for l in range(DEPTH)]

    def sb(name, shape, dt=F32):
        return es.enter_context(nc.sbuf_tensor(name, list(shape), dt))

    def ps(name, shape, dt=F32):
        return es.enter_context(nc.psum_tensor(name, list(shape), dt))

    hres = sb("hres", [128, D])
    hT = sb("hT", [128, 8, 128], BF16)
    wsl = [sb("wsl%d" % i, [128, WMAX], BF16) for i in range(NSLOT)]
    S = [sb("S%d" % l, [128, HR, DV]) for l in range(DEPTH)]
    Sm = [sb("Sm%d" % l, [128, HR, DV]) for l in range(DEPTH)]
    Sbf = sb("Sbf", [128, HR, DV], BF16)
    zT = [sb("zT%d" % l, [128, 8, PB + 128]) for l in range(DEPTH)]
    zm = [sb("zm%d" % l, [128, 8, PB]) for l in range(DEPTH)]
    C = sb("C", [128, 640])
    CB = sb("CB", [128, 512], BF16)
    RC = sb("RC", [128, 8, NMETA])
    lnG = sb("lnG", [128, D])
    lnB = sb("lnB", [128, D])
    rc = sb("rc", [128, 64])
    rs = sb("rs", [128, 128])
    hb = sb("hb", [128, D], BF16)
    act_tok = sb("act_tok", [128, DFF], BF16)
    actT = sb("actT", [128, 22, 128], BF16)
    stmp = sb("stmp", [128, 256], BF16)
    f1 = sb("f1", [128, D])
    f2 = sb("f2", [128, D])
    mix = sb("mix", [128, D])
    st6 = sb("st6", [128, 8, 6])
    mv = sb("mv", [128, 4, 2])
    sc1 = sb("sc1", [128, 4])
    sc2 = sb("sc2", [128, 4])
    qt = sb("qt", [128, 512], BF16)
    kt = sb("kt", [128, 512], BF16)
    qkT = sb("qkT", [128, 8, 128], BF16)
    vr = sb("vr", [128, D], BF16)
    gr = sb("gr", [128, D], BF16)
    th = sb("th", [128, D], BF16)
    attm = sb("attm", [128, 4, 128], BF16)
    oT = sb("oT", [128, 8, 128], BF16)
    qsT = sb("qsT", [128, 8, 128], BF16)
    kb_own = sb("kb_own", [128, D], BF16)
    kT_own = sb("kT_own", [128, 8, 128], BF16)
    v_own = sb("v_own", [128, D], BF16)
    kpb = [sb("kpb%d" % i, [128, D], BF16) for i in range(2)]
    vpb = [sb("vpb%d" % i, [128, D], BF16) for i in range(2)]
    kTb = [sb("kTb%d" % i, [128, 8, 128], BF16) for i in range(2)]
    eb = [sb("eb%d" % i, [128, 4, 128]) for i in range(2)]
    spb = [sb("spb%d" % i, [128, 4, 128], BF16) for i in range(2)]
    wb_ = [sb("wb%d" % i, [128, 4, 128], BF16) for i in range(2)]
    ltc = [sb("ltc%d" % i, [128, 4, 128]) for i in range(2)]
    ltcb = [sb("ltcb%d" % i, [128, 4, 128], BF16) for i in range(2)]
    pooledT = sb("pooledT", [128, 8, 128], BF16)
    ptmp = sb("ptmp", [128, 2, PB + 128])
    ptmp2 = sb("ptmp2", [128, 2, PB + 128])

    pf = [ps("pf%d" % i, [128, 512]) for i in range(6)]
    pb = [ps("pb%d" % i, [128, 1024], BF16) for i in range(2)]
    pfi = [0]
    pbi = [0]

    def nf():
        i = pfi[0] % 4
        pfi[0] += 1
        return i

    def nb():
        i = pbi[0] % 2
        pbi[0] += 1
        return i

    identf = C[:, 0:128]
    cmask = C[:, 128:256]
    identb = CB[:, 0:128]
    dmask = CB[:, 128:256]
    ntri = CB[:, 256:384]
    nones = CB[:, 384:512]

    def DMA(out, in_, reads, writes, eng="sp"):
        kind = "sw" if eng == "pool" else "hw"
        P.add(eng, lambda e: e.dma_start(out=out, in_=in_), reads, writes, dma=kind)

    def MM(out, lhsT, rhs, start, stop, reads, writes):
        P.add("pe", lambda e: e.matmul(out, lhsT=lhsT, rhs=rhs, start=start, stop=stop), reads, writes)

    def TR(out, in_, np_, reads, writes):
        P.add("pe", lambda e: e.transpose(out, in_, identb[:np_, :np_]), list(reads) + ["CB"], writes)

    def ACTF(out, in_, func, reads, writes, scale=1.0, bias=0.0):
        P.add("act", lambda e: e.activation(out=out, in_=in_, func=func, bias=bias, scale=scale), reads, writes)

    def TT(eng, out, in0, in1, op, reads, writes):
        P.add(eng, lambda e: e.tensor_tensor(out=out, in0=in0, in1=in1, op=op), reads, writes)

    def STT(eng, out, in0, scalar, in1, op0, op1, reads, writes):
        P.add(eng, lambda e: e.scalar_tensor_tensor(out=out, in0=in0, scalar=scalar, in1=in1, op0=op0, op1=op1), reads, writes)

    def TS(eng, out, in0, s1, s2, op0, op1, reads, writes):
        if s2 is None:
            P.add(eng, lambda e: e.tensor_scalar(out=out, in0=in0, scalar1=s1, scalar2=None, op0=op0), reads, writes)
        else:
            P.add(eng, lambda e: e.tensor_scalar(out=out, in0=in0, scalar1=s1, scalar2=s2, op0=op0, op1=op1), reads, writes)

    def CP(eng, out, in_, reads, writes):
        if eng == "act":
            P.add("act", lambda e: e.activation(out=out, in_=in_, func=AF.Copy), reads, writes)
        else:
            P.add(eng, lambda e: e.tensor_copy(out=out, in_=in_), reads, writes)

    def MSET(eng, ap, val, writes):
        P.add(eng, lambda e: e.memset(ap, val), (), writes)

    MSET("pool", ptmp[:, :, :], 0.0, ["ptmp"])
    MSET("pool", ptmp2[:, :, :], 0.0, ["ptmp2"])
    DMA(C[:, :], cst[:, :], (), ["C"])
    DMA(CB[:, :], cstb[:, :], (), ["CB"])
    DMA(RC[:, :, :], rcnt.rearrange("p (c t) -> p c t", c=8), (), ["RC"])
    for l in range(DEPTH):
        for c in range(NCAST):
            c0, c1 = c * CASTW, min(WTOT, (c + 1) * CASTW)
            P.add("pool", (lambda e, l=l, c0=c0, c1=c1: e.dma_start(out=wbf[l][:, c0:c1], in_=wfl[l][:, c0:c1],
                                                                   max_dma_last_dim=4096)),
                  (), [("wbf", l, c)], dma="sw")

    ws = dict(n=0, seq=[])

    def w_issue(k):
        l, ti = ws["seq"][k]
        nm, idx, kc_n, ncol = WT[ti]
        off = WOFF[ti]
        width = kc_n * ncol
        slot = k % NSLOT
        cs = range(off // CASTW, (off + width - 1) // CASTW + 1)
        DMA(wsl[slot][:, 0:width], wbf[l][:, off:off + width], [("wbf", l, c) for c in cs], ["wsl%d" % slot])

    def w_start(seqlist):
        ws["seq"] = seqlist
        ws["n"] = 0
        for k in range(min(NSLOT, len(seqlist))):
            w_issue(k)

    def w_get(expect):
        k = ws["n"]
        l, ti = ws["seq"][k]
        nm, idx, kc_n, ncol = WT[ti]
        assert nm == expect, (nm, expect)
        slot = k % NSLOT
        view = wsl[slot][:, 0:kc_n * ncol].rearrange("p (k n) -> p k n", k=kc_n)
        return view, "wsl%d" % slot

    def w_done():
        k = ws["n"]
        ws["n"] = k + 1
        if k + NSLOT < len(ws["seq"]):
            w_issue(k + NSLOT)

    def transpose_to(dstT, src_tok, T, nchunk, srckey, dstkey, col0=0):
        c = 0
        while c < nchunk:
            n = min(8, nchunk - c)
            b = nb()
            for j in range(n):
                TR(pb[b][:, j * 128:j * 128 + T], src_tok[:T, col0 + (c + j) * 128: col0 + (c + j + 1) * 128], T,
                   [srckey], ["pb%d" % b])
            CP("act", dstT[:, c:c + n, :T], pb[b][:, 0:n * 128].rearrange("p (c t) -> p c t", c=n)[:, :, :T],
               ["pb%d" % b], [dstkey])
            c += n

    def proj_tok(T, inT, inkey, wname, nparts, evac):
        for part in range(nparts):
            w, wkey = w_get(wname)
            kc_n = w.shape[1]
            ncol = w.shape[2]
            b = nf()
            for kc in range(kc_n):
                MM(pf[b][:T, 0:ncol], inT[:, kc, :T], w[:, kc, :], kc == 0, kc == kc_n - 1, [inkey, wkey], ["pf%d" % b])
            w_done()
            evac(part, pf[b], "pf%d" % b)

    def layer_norm(T, l, which):
        row = l * 3 + which
        DMA(lnG[:T, :], lng[row:row + 1, :].to_broadcast([T, D]), (), ["lnG"])
        DMA(lnB[:T, :], lnb[row:row + 1, :].to_broadcast([T, D]), (), ["lnB"])
        for c in range(2):
            P.add("dve", lambda e, c=c: e.bn_stats(out=st6[:T, c, :], in_=hres[:T, c * 512:(c + 1) * 512]), ["hres"], ["st6"])
        P.add("dve", lambda e: e.bn_aggr(out=mv[:T, 0, :], in_=st6[:T, 0:2, :]), ["st6"], ["mv"])
        rstd_from_var(T, 1, EPS_LN)
        ACTF(f1[:T, :], hres[:T, :], AF.Identity, ["hres", "sc1", "sc2"], ["f1"], scale=sc1[:T, 0:1], bias=sc2[:T, 0:1])
        TT("dve", f1[:T, :], f1[:T, :], lnG[:T, :], ALU.mult, ["f1", "lnG"], ["f1"])
        TT("dve", hres[:T, :], f1[:T, :], lnB[:T, :], ALU.add, ["f1", "lnB"], ["hres"])
        CP("pool", hb[:T, :], hres[:T, :], ["hres"], ["hb"])
        transpose_to(hT, hb, T, 8, "hb", "hT")

    def rstd_from_var(T, n, eps):
        ACTF(sc1[:T, 0:n], mv[:T, 0:n, 1], AF.Ln, ["mv"], ["sc1"], bias=eps)
        ACTF(sc1[:T, 0:n], sc1[:T, 0:n], AF.Exp, ["sc1"], ["sc1"], scale=-0.5)
        STT("dve", sc2[:T, 0:n], mv[:T, 0:n, 0], -1.0, sc1[:T, 0:n], ALU.mult, ALU.mult, ["mv", "sc1"], ["sc2"])

    def ffn(T, l, which):
        upn, dnn = ("up1", "dn1") if which == 0 else ("up2", "dn2")
        for j in range(11):
            w, wkey = w_get(upn)
            b = nf()
            for kc in range(8):
                MM(pf[b][:T, :], hT[:, kc, :T], w[:, kc, :], kc == 0, kc == 7, ["hT", wkey], ["pf%d" % b])
            w_done()
            ACTF(stmp[:T, :], pf[b][:T, 0:256], AF.Silu, ["pf%d" % b], ["stmp"])
            TT("dve", act_tok[:T, j * 256:(j + 1) * 256], pf[b][:T, 256:512], stmp[:T, :], ALU.mult,
               ["pf%d" % b, "stmp"], ["act_tok"])
        transpose_to(actT, act_tok, T, 22, "act_tok", "actT")
        for nh in range(2):
            b = nf()
            for kh in range(2):
                w, wkey = w_get(dnn)
                for kc in range(11):
                    MM(pf[b][:T, :], actT[:, kh * 11 + kc, :T], w[:, kc, :], kh == 0 and kc == 0, kh == 1 and kc == 10,
                       ["actT", wkey], ["pf%d" % b])
                w_done()
            STT("dve", hres[:T, nh * 512:(nh + 1) * 512], pf[b][:T, :], CRES, hres[:T, nh * 512:(nh + 1) * 512],
                ALU.mult, ALU.add, ["pf%d" % b, "hres"], ["hres"])
        layer_norm(T, l, 0 if which == 0 else 2)

    def gate_mix(T, bi, first):
        def ev(part, pbank, pkey):
            sl = slice(part * 512, (part + 1) * 512)
            if first:
                STT("dve", mix[:T, sl], th[:T, sl], 1.0, pbank[:T, :], ALU.add, ALU.mult, ["th", pkey], ["mix"])
            else:
                STT("dve", f2[:T, sl], th[:T, sl], 1.0, pbank[:T, :], ALU.add, ALU.mult, ["th", pkey], ["f2"])
                TT("pool", mix[:T, sl], mix[:T, sl], f2[:T, sl], ALU.add, ["mix", "f2"], ["mix"])
        proj_tok(T, oT, "oT", "wb%d" % bi, 2, ev)

    def gates(T, bi):
        def ev(part, pbank, pkey):
            ACTF(th[:T, part * 512:(part + 1) * 512], pbank[:T, :], AF.Tanh, [pkey], ["th"], scale=0.5)
        proj_tok(T, hT, "hT", "gt%d" % bi, 2, ev)

    def rotary(T, dst, dkey, pbank, pkey, sccol):
        xv = pbank[:T, :].rearrange("p (h t d) -> p h t d", h=4, t=2)
        cosb = rc[:T, :].unsqueeze(1).unsqueeze(1).to_broadcast([T, 4, 2, 64])
        TT("dve", f1[:T, 0:512].rearrange("p (h t d) -> p h t d", h=4, t=2), xv, cosb, ALU.mult, [pkey, "rc"], ["f1"])
        f2v = f2[:T, 0:512].rearrange("p (h t d) -> p h t d", h=4, t=2)
        for t in range(2):
            sinb = rs[:T, t * 64:(t + 1) * 64].unsqueeze(1).to_broadcast([T, 4, 64])
            TT("dve", f2v[:, :, t, :], xv[:, :, 1 - t, :], sinb, ALU.mult, [pkey, "rs"], ["f2"])
        TT("pool", f1[:T, 0:512], f1[:T, 0:512], f2[:T, 0:512], ALU.add, ["f1", "f2"], ["f1"])
        scb = C[:T, sccol:sccol + 4].unsqueeze(2).to_broadcast([T, 4, 128])
        TT("pool", dst[:T, :].rearrange("p (h d) -> p h d", h=4), f1[:T, 0:512].rearrange("p (h d) -> p h d", h=4), scb,
           ALU.mult, ["f1", "C"], [dkey])

    def retention(T, l, Sl, Skey, dccol):
        qk_keys = {}

        def ev_q(part, pbank, pkey):
            rotary(T, qt, "qt", pbank, pkey, 256)

        def ev_k(part, pbank, pkey):
            rotary(T, kt, "kt", pbank, pkey, 260)

        proj_tok(T, hT, "hT", "qr", 1, ev_q)
        proj_tok(T, hT, "hT", "kr", 1, ev_k)

        def ev_v(part, pbank, pkey):
            CP("act", vr[:T, part * 512:(part + 1) * 512], pbank[:T, :], [pkey], ["vr"])
        proj_tok(T, hT, "hT", "vr", 2, ev_v)

        def ev_g(part, pbank, pkey):
            ACTF(gr[:T, part * 512:(part + 1) * 512], pbank[:T, :], AF.Silu, [pkey], ["gr"])
        proj_tok(T, hT, "hT", "gr", 2, ev_g)
        gates(T, 0)
        b = nb()
        for h in range(4):
            TR(pb[b][:, h * 128:h * 128 + T], qt[:T, h * 128:(h + 1) * 128], T, ["qt"], ["pb%d" % b])
        for h in range(4):
            TR(pb[b][:, (4 + h) * 128:(4 + h) * 128 + T], kt[:T, h * 128:(h + 1) * 128], T, ["kt"], ["pb%d" % b])
        CP("act", qkT[:, :, :T], pb[b][:, :].rearrange("p (c t) -> p c t", c=8)[:, :, :T], ["pb%d" % b], ["qkT"])
        CP("pool", Sbf[:, :, :], Sl[:, :, :], [Skey], ["Sbf"])
        b = nf()
        for h in range(4):
            MM(pf[b][:T, h * 128:h * 128 + T], qkT[:, 4 + h, :T], qkT[:, h, :T], h == 0, h == 3, ["qkT"], ["pf%d" % b])
        TT("dve", attm[:T, :, :T], pf[b][:T, :].rearrange("p (h t) -> p h t", h=4)[:, :, :T],
           cmask[:T, :T].unsqueeze(1).to_broadcast([T, 4, T]), ALU.mult, ["pf%d" % b, "C"], ["attm"])
        ob = [nf(), nf()]
        for h in range(4):
            bb = ob[h // 2]
            osl = pf[bb][:T, (h % 2) * 256:(h % 2) * 256 + 256]
            MM(osl, attm[:T, h, :T], vr[:T, h * 256:(h + 1) * 256], h % 2 == 0, False, ["attm", "vr"], ["pf%d" % bb])
            MM(osl, qkT[:, h, :T], Sbf[:, h, :], False, h % 2 == 1, ["qkT", "Sbf"], ["pf%d" % bb])
        for i in range(2):
            CP("act", f1[:T, i * 512:(i + 1) * 512], pf[ob[i]][:T, :], ["pf%d" % ob[i]], ["f1"])
        sbk = [nf(), nf()]
        for h in range(4):
            bb = sbk[h // 2]
            MM(pf[bb][:, (h % 2) * 256:(h % 2) * 256 + 256], kt[:T, h * 128:(h + 1) * 128], vr[:T, h * 256:(h + 1) * 256],
               h % 2 == 0, h % 2 == 1, ["kt", "vr"], ["pf%d" % bb])
        for i in range(2):
            TT("dve", Sl[:, 2 * i:2 * i + 2, :], pf[sbk[i]][:, :].rearrange("p (h v) -> p h v", h=2), Sl[:, 2 * i:2 * i + 2, :],
               ALU.add, ["pf%d" % sbk[i], Skey], [Skey])
        TT("dve", Sl[:, :, :], Sl[:, :, :], C[:, dccol:dccol + 4].unsqueeze(2).to_broadcast([128, 4, DV]), ALU.mult,
           [Skey, "C"], [Skey])
        for h in range(4):
            P.add("dve", lambda e, h=h: e.bn_stats(out=st6[:T, h, :], in_=f1[:T, h * 256:(h + 1) * 256]), ["f1"], ["st6"])
        for h in range(4):
            P.add("dve", lambda e, h=h: e.bn_aggr(out=mv[:T, h, :], in_=st6[:T, h:h + 1, :]), ["st6"], ["mv"])
        rstd_from_var(T, 4, EPS_HN)
        for h in range(4):
            ACTF(f2[:T, h * 256:(h + 1) * 256], f1[:T, h * 256:(h + 1) * 256], AF.Identity, ["f1", "sc1", "sc2"], ["f2"],
                 scale=sc1[:T, h:h + 1], bias=sc2[:T, h:h + 1])
        DMA(lnG[:T, :], retg[l:l + 1, :].to_broadcast([T, D]), (), ["lnG"])
        TT("dve", f2[:T, :], f2[:T, :], lnG[:T, :], ALU.mult, ["f2", "lnG"], ["f2"])
        TT("dve", hb[:T, :], f2[:T, :], gr[:T, :], ALU.mult, ["f2", "gr"], ["hb"])
        transpose_to(oT, hb, T, 8, "hb", "oT")
        gate_mix(T, 0, True)

    def stick(T, l, kdst, vdst, past):
        def ev_q(part, pbank, pkey):
            ACTF(hb[:T, part * 512:(part + 1) * 512], pbank[:T, :], AF.Copy, [pkey], ["hb"], scale=DS ** -0.5)
        proj_tok(T, hT, "hT", "qs", 2, ev_q)
        transpose_to(qsT, hb, T, 8, "hb", "qsT")

        def ev_k(part, pbank, pkey):
            CP("act", f1[:T, part * 512:(part + 1) * 512], pbank[:T, :], [pkey], ["f1"])
        proj_tok(T, hT, "hT", "ks", 2, ev_k)
        for (ap, wk) in kdst:
            DMA(ap, f1[:T, :], ["f1"], [wk])
        CP("pool", kb_own[:T, :], f1[:T, :], ["f1"], ["kb_own"])
        transpose_to(kT_own, kb_own, T, 8, "kb_own", "kT_own")

        def ev_v(part, pbank, pkey):
            CP("act", f2[:T, part * 512:(part + 1) * 512], pbank[:T, :], [pkey], ["f2"])
        proj_tok(T, hT, "hT", "vs", 2, ev_v)
        for (ap, wk) in vdst:
            DMA(ap, f2[:T, :], ["f2"], [wk])
        CP("pool", v_own[:T, :], f2[:T, :], ["f2"], ["v_own"])
        gates(T, 1)
        blocks = [("own", None, None, T, None)] + [("past",) + p for p in past]
        for hg in range(2):
            MSET("pool", ltc[hg][:, :, :], 0.0, ["ltc%d" % hg])
        ob = [4, 5]
        nblk = len(blocks)

        def load_dma(bi_):
            blk = blocks[bi_]
            nkk = blk[3]
            s = bi_ % 2
            DMA(kpb[s][:nkk, :], blk[1], blk[4], ["kpb%d" % s], eng="pool")
            DMA(vpb[s][:nkk, :], blk[2], blk[4], ["vpb%d" % s], eng="pool")

        def load_tr(bi_):
            blk = blocks[bi_]
            s = bi_ % 2
            transpose_to(kTb[s], kpb[s], blk[3], 8, "kpb%d" % s, "kTb%d" % s)

        if nblk > 1:
            load_dma(1)
            load_tr(1)
        for bi_, blk in enumerate(blocks):
            kind = blk[0]
            nkk = blk[3]
            if kind == "own":
                kTs, kTkey, vs_, vkey = kT_own, "kT_own", v_own, "v_own"
            else:
                s = bi_ % 2
                kTs, kTkey, vs_, vkey = kTb[s], "kTb%d" % s, vpb[s], "vpb%d" % s
            if bi_ >= 1 and bi_ + 1 < nblk:
                load_dma(bi_ + 1)
            HG = range(2)
            mk = dmask[:nkk, :T].unsqueeze(1).to_broadcast([nkk, 4, T])
            zb = [nf(), nf()]
            for hg in HG:
                for hh in range(4):
                    h = hg * 4 + hh
                    MM(pf[zb[hg]][:nkk, hh * 128:hh * 128 + T], kTs[:, h, :nkk], qsT[:, h, :T], hh == 0, hh == 3,
                       [kTkey, "qsT"], ["pf%d" % zb[hg]])
            for hg in HG:
                zv = pf[zb[hg]][:nkk, :].rearrange("p (h t) -> p h t", h=4)[:, :, :T]
                ACTF(eb[hg][:nkk, :, :T], zv, AF.Exp, ["pf%d" % zb[hg]], ["eb%d" % hg])
            for hg in HG:
                ACTF(spb[hg][:nkk, :, :T], eb[hg][:nkk, :, :T], AF.Ln, ["eb%d" % hg], ["spb%d" % hg], bias=1.0)
            if kind == "own":
                for hg in HG:
                    TT("pool", spb[hg][:nkk, :, :T], spb[hg][:nkk, :, :T], mk, ALU.mult, ["spb%d" % hg, "CB"], ["spb%d" % hg])
            bb = [nf(), nf()]
            for hg in HG:
                for hh in range(4):
                    h = hg * 4 + hh
                    osl = pf[bb[hg]][:nkk, hh * 128:hh * 128 + T]
                    MM(osl, kTs[:, h, :nkk], qsT[:, h, :T], hh == 0, False, [kTkey, "qsT"], ["pf%d" % bb[hg]])
                    MM(osl, ntri[:nkk, :nkk], spb[hg][:nkk, hh, :T], False, bi_ == 0 and hh == 3, ["CB", "spb%d" % hg],
                       ["pf%d" % bb[hg]])
                    if bi_ > 0:
                        MM(osl, nones[:, :nkk], ltcb[hg][:, hh, :T], False, hh == 3, ["CB", "ltcb%d" % hg], ["pf%d" % bb[hg]])
            for hg in HG:
                bv = pf[bb[hg]][:nkk, :].rearrange("p (h t) -> p h t", h=4)[:, :, :T]
                ACTF(wb_[hg][:nkk, :, :T], bv, AF.Exp, ["pf%d" % bb[hg]], ["wb%d" % hg])
            if kind == "own":
                for hg in HG:
                    TT("pool", wb_[hg][:nkk, :, :T], wb_[hg][:nkk, :, :T], mk, ALU.mult, ["wb%d" % hg, "CB"], ["wb%d" % hg])
            for hg in HG:
                for hh in range(4):
                    h = hg * 4 + hh
                    MM(pf[ob[hg]][:T, hh * 128:(hh + 1) * 128], wb_[hg][:nkk, hh, :T], vs_[:nkk, h * 128:(h + 1) * 128],
                       bi_ == 0 and hh == 0, bi_ == nblk - 1 and hh == 3, ["wb%d" % hg, vkey], ["pf%d" % ob[hg]])
            if bi_ < nblk - 1:
                for hg in HG:
                    TT("pool", ltc[hg][:nkk, :, :T], ltc[hg][:nkk, :, :T], spb[hg][:nkk, :, :T], ALU.add,
                       ["ltc%d" % hg, "spb%d" % hg], ["ltc%d" % hg])
                    CP("pool", ltcb[hg][:, :, :T], ltc[hg][:, :, :T], ["ltc%d" % hg], ["ltcb%d" % hg])
            if bi_ >= 1 and bi_ + 1 < nblk:
                load_tr(bi_ + 1)
        for hg in range(2):
            CP("act", hb[:T, hg * 512:(hg + 1) * 512], pf[ob[hg]][:T, :], ["pf%d" % ob[hg]], ["hb"])
        transpose_to(oT, hb, T, 8, "hb", "oT")
        gate_mix(T, 1, False)

    def pool_mixer(T, l, zl, zkey, is_meta, pooldst):
        for part in range(2):
            w, wkey = w_get("u")
            for oc in range(4):
                b = nf()
                for kc in range(8):
                    MM(pf[b][:, :T], w[:, kc, oc * 128:(oc + 1) * 128], hT[:, kc, :T], kc == 0, kc == 7, ["hT", wkey], ["pf%d" % b])
                CP("act", zl[:, part * 4 + oc, PB:PB + T], pf[b][:, :T], ["pf%d" % b], [zkey])
            w_done()
        gates(T, 2)
        W_ = PB + T
        for g, wnd in enumerate((2, 4, 8, 16)):
            cs = slice(2 * g, 2 * g + 2)
            cur = zl[:, cs, :]
            sh = 1
            bufs = [ptmp, ptmp2]
            bi_ = 0
            curkey = zkey
            while sh < wnd:
                dst = bufs[bi_ % 2]
                dkey = "ptmp" if bi_ % 2 == 0 else "ptmp2"
                TT("pool", dst[:, :, sh:W_], cur[:, :, sh:W_], cur[:, :, 0:W_ - sh], ALU.add, [curkey], [dkey])
                cur, curkey = dst, dkey
                sh *= 2
                bi_ += 1
            if is_meta:
                TT("dve", f1[:, 0:2 * T].rearrange("p (c t) -> p c t", c=2), cur[:, :, PB:PB + T], RC[:, cs, :T], ALU.mult,
                   [curkey, "RC"], ["f1"])
                TT("dve", pooledT[:, cs, :T], f1[:, 0:2 * T].rearrange("p (c t) -> p c t", c=2), zl[:, cs, PB:PB + T],
                   ALU.subtract, ["f1", zkey], ["pooledT"])
            else:
                STT("dve", pooledT[:, cs, :T], cur[:, :, PB:PB + T], 1.0 / wnd, zl[:, cs, PB:PB + T], ALU.mult, ALU.subtract,
                    [curkey, zkey], ["pooledT"])
        w, wkey = w_get("pm")
        ob = [nf(), nf()]
        for g in range(4):
            bb = ob[g // 2]
            for cc in range(2):
                MM(pf[bb][:T, (g % 2) * 256:(g % 2) * 256 + 256], pooledT[:, g * 2 + cc, :T], w[:, g * 2 + cc, :],
                   g % 2 == 0 and cc == 0, g % 2 == 1 and cc == 1, ["pooledT", wkey], ["pf%d" % bb])
        w_done()
        DMA(lnG[:T, :], pscale[l:l + 1, :].to_broadcast([T, D]), (), ["lnG"])
        for i in range(2):
            TT("dve", hb[:T, i * 512:(i + 1) * 512], pf[ob[i]][:T, :], lnG[:T, i * 512:(i + 1) * 512], ALU.mult,
               ["pf%d" % ob[i], "lnG"], ["hb"])
        transpose_to(oT, hb, T, 8, "hb", "oT")
        gate_mix(T, 2, False)
        if pooldst is not None:
            src0 = PB + T - 128 if T >= 128 else None
            for half in range(2):
                b = nf()
                for j in range(4):
                    c = half * 4 + j
                    if T >= 128:
                        P.add("pe", lambda e, b=b, j=j, c=c: e.transpose(pf[b][:, j * 128:(j + 1) * 128], zl[:, c, PB + T - 128:PB + T], identf),
                              [zkey, "C"], ["pf%d" % b])
                    else:
                        P.add("pe", lambda e, b=b, j=j, c=c: e.transpose(pf[b][:PB + T, j * 128:(j + 1) * 128], zl[:, c, 0:PB + T], identf),
                              [zkey, "C"], ["pf%d" % b])
                rows = 128 if T >= 128 else PB + T
                CP("act", f1[:rows, half * 512:(half + 1) * 512], pf[b][:rows, :], ["pf%d" % b], ["f1"])
            rows = 128 if T >= 128 else PB + T
            for (ap, wk) in pooldst:
                DMA(ap, f1[rows - PB:rows, :], ["f1"], [wk])
        CP("pool", zl[:, :, 0:PB], zl[:, :, T:T + PB], [zkey], [zkey])

    def out_proj(T, l):
        CP("act", hb[:T, :], mix[:T, :], ["mix"], ["hb"])
        transpose_to(oT, hb, T, 8, "hb", "oT")

        def ev(part, pbank, pkey):
            sl = slice(part * 512, (part + 1) * 512)
            STT("dve", hres[:T, sl], pbank[:T, :], CRES, hres[:T, sl], ALU.mult, ALU.add, [pkey, "hres"], ["hres"])
        proj_tok(T, oT, "oT", "wo", 2, ev)
        layer_norm(T, l, 1)

    def load_rope(T, row0):
        DMA(rc[:T, :], ropec[row0:row0 + T, :], (), ["rc"])
        DMA(rs[:T, :], ropes[row0:row0 + T, :], (), ["rs"])

    def load_tile(T, src):
        DMA(hres[:T, :], src, (), ["hres"])
        CP("pool", hb[:T, :], hres[:T, :], ["hres"], ["hb"])
        transpose_to(hT, hb, T, 8, "hb", "hT")

    def layer(T, l, Sl, Skey, dccol, zl, zkey, is_meta, kdst, vdst, past, pooldst):
        ffn(T, l, 0)
        retention(T, l, Sl, Skey, dccol)
        stick(T, l, kdst, vdst, past)
        pool_mixer(T, l, zl, zkey, is_meta, pooldst)
        out_proj(T, l)
        ffn(T, l, 1)

    ntiles_total = 2 + NSEQ * NT
    seqlist = []
    for _ in range(ntiles_total):
        for l in range(DEPTH):
            for ti in range(len(WT)):
                seqlist.append((l, ti))
    w_start(seqlist)

    T = NMETA
    load_rope(T, 0)
    load_tile(T, meta[:, :])
    for l in range(DEPTH):
        MSET("pool", Sm[l][:, :, :], 0.0, ["Sm%d" % l])
        MSET("pool", zT[l][:, :, :], 0.0, ["zT%d" % l])
        kd = [(nk[l, s, 0:NMETA, :], ("nk", l, s, "m")) for s in range(NSEQ)]
        vd = [(nv[l, s, 0:NMETA, :], ("nv", l, s, "m")) for s in range(NSEQ)]
        layer(T, l, Sm[l], "Sm%d" % l, 272, zT[l], "zT%d" % l, True, kd, vd, [], None)
        CP("pool", zm[l][:, :, :], zT[l][:, :, 0:PB], ["zT%d" % l], ["zm%d" % l])

    T = DEC
    load_rope(T, NMETA + SEQ)
    load_tile(T, xs[:, :])
    for l in range(DEPTH):
        DMA(S[l][:, :, :], sret[l].rearrange("h k v -> k h v"), (), ["S%d" % l])
        DMA(f1[:PB, :], spool[l, :, :], (), ["f1"])
        for half in range(2):
            b = nf()
            for j in range(4):
                c = half * 4 + j
                P.add("pe", lambda e, b=b, j=j, c=c: e.transpose(pf[b][:, j * PB:(j + 1) * PB], f1[:PB, c * 128:(c + 1) * 128], identf[:PB, :PB]),
                      ["f1", "C"], ["pf%d" % b])
            CP("act", zT[l][:, half * 4:half * 4 + 4, 0:PB], pf[b][:, 0:4 * PB].rearrange("p (c t) -> p c t", c=4),
               ["pf%d" % b], ["zT%d" % l])
        nblk = PAST // 128
        past = [(ck[l, bk * 128:(bk + 1) * 128, :], cv[l, bk * 128:(bk + 1) * 128, :], 128, []) for bk in reversed(range(nblk))]
        layer(T, l, S[l], "S%d" % l, 268, zT[l], "zT%d" % l, False, [(sk[l, :, :], ("sk", l))], [(sv[l, :, :], ("sv", l))],
              past, [(spoolo[l, :, :], ("spoolo", l))])
        DMA(sreto[l].rearrange("h k v -> k h v"), S[l][:, :, :], ["S%d" % l], [("sreto", l)])
    DMA(ys[:, :], hres[:T, :], ["hres"], ["ys"])

    T = 128
    for s in range(NSEQ):
        for l in range(DEPTH):
            CP("pool", S[l][:, :, :], Sm[l][:, :, :], ["Sm%d" % l], ["S%d" % l])
            CP("pool", zT[l][:, :, 0:PB], zm[l][:, :, :], ["zm%d" % l], ["zT%d" % l])
        for it in range(NT):
            load_rope(T, NMETA + it * 128)
            load_tile(T, x[s, it * 128:(it + 1) * 128, :])
            for l in range(DEPTH):
                r0 = NMETA + it * 128
                kd = [(nk[l, s, r0:r0 + 128, :], ("nk", l, s, it))]
                vd = [(nv[l, s, r0:r0 + 128, :], ("nv", l, s, it))]
                past = []
                for pt in reversed(range(it)):
                    p0 = NMETA + pt * 128
                    past.append((nk[l, s, p0:p0 + 128, :], nv[l, s, p0:p0 + 128, :], 128, [("nk", l, s, pt), ("nv", l, s, pt)]))
                past.append((nk[l, s, 0:NMETA, :], nv[l, s, 0:NMETA, :], NMETA, [("nk", l, s, "m"), ("nv", l, s, "m")]))
                pd = [(npool[l, s, :, :], ("npool", l, s))] if it == NT - 1 else None
                layer(T, l, S[l], "S%d" % l, 264, zT[l], "zT%d" % l, False, kd, vd, past, pd)
            DMA(y[s, it * 128:(it + 1) * 128, :], hres[:T, :], ["hres"], [("y", s, it)])
        for l in range(DEPTH):
            DMA(nret[l, s].rearrange("h k v -> k h v"), S[l][:, :, :], ["S%d" % l], [("nret", l, s)])

    P.emit(nc, es)
    return nc, es


def make_consts(seq, dec, past):
    import ml_dtypes
    lg = lg_decay()
    cst = np.zeros((128, 640), np.float32)
    cst[:, 0:128] = np.eye(128, dtype=np.float32)
    m = np.arange(128)
    cst[:, 128:256] = (m[None, :] >= m[:, None]).astype(np.float32)
    for h in range(HR):
        cst[:, 256 + h] = np.exp(lg[h] * (m + 1.0))
        cst[:, 260 + h] = (DK ** -0.5) * np.exp(-lg[h] * (m + 1.0))
        cst[:, 264 + h] = np.exp(lg[h] * 128.0)
        cst[:, 268 + h] = np.exp(lg[h] * float(dec))
        cst[:, 272 + h] = np.exp(lg[h] * float(NMETA))
    cb = np.zeros((128, 512), np.float32)
    cb[:, 0:128] = np.eye(128)
    cb[:, 128:256] = (m[:, None] < m[None, :])
    cb[:, 256:384] = -(m[:, None] >= m[None, :]).astype(np.float32)
    cb[:, 384:512] = -1.0
    cstb = cb.astype(ml_dtypes.bfloat16)
    half = 64
    inv = (10000.0 ** (-np.arange(half, dtype=np.float32) / np.float32(half))).astype(np.float32)
    pos = np.concatenate([np.arange(NMETA), NMETA + np.arange(seq), NMETA + past + np.arange(dec)]).astype(np.float32)
    ang = (pos[:, None] * inv[None, :]).astype(np.float32)
    cos = np.cos(ang).astype(np.float32)
    sin = np.sin(ang).astype(np.float32)
    ropec = cos
    ropes = np.concatenate([-sin, sin], axis=1).astype(np.float32)
    rc = np.zeros((128, 8, NMETA), np.float32)
    t = np.arange(NMETA)
    for g, w in enumerate((2, 4, 8, 16)):
        rc[:, 2 * g:2 * g + 2, :] = (1.0 / np.minimum(t + 1, w))[None, None, :]
    return cst, cstb, ropec, ropes, rc.reshape(128, 8 * NMETA)


_CACHE = {}


def run(inputs, cfg_seq=None):
    inp = {k: np.asarray(v) for k, v in inputs.items()}
    B, SEQ, _ = inp["x_prompt"].shape
    DB, DEC, _ = inp["x_sample"].shape
    PAST = inp["cache_sb_k"].shape[2]
    assert B % NCORES == 0 and DB == NCORES
    NSEQ = B // NCORES
    cfg = Cfg(NSEQ, SEQ, PAST, DEC)
    nc, es = build(cfg)
    cst, cstb, ropec, ropes, rcnt = make_consts(SEQ, DEC, PAST)
    wf = [pack_layer(l, inp) for l in range(DEPTH)]
    in_maps = []
    for c in range(NCORES):
        in_maps.append({
            "x": np.ascontiguousarray(inp["x_prompt"][c * NSEQ:(c + 1) * NSEQ]),
            "xs": np.ascontiguousarray(inp["x_sample"][c]),
            "ck": np.ascontiguousarray(inp["cache_sb_k"][:, c].reshape(DEPTH, PAST, D)),
            "cv": np.ascontiguousarray(inp["cache_sb_v"][:, c].reshape(DEPTH, PAST, D)),
            "sret": np.ascontiguousarray(inp["state_ret"][:, c]),
            "spool": np.ascontiguousarray(inp["state_pool"][:, c]),
            "meta": inp["meta_tokens"],
            "wflat0": wf[0], "wflat1": wf[1],
            "lng": np.ascontiguousarray(inp["ln_g"].reshape(DEPTH * 3, D)),
            "lnb": np.ascontiguousarray(inp["ln_b"].reshape(DEPTH * 3, D)),
            "retg": np.ascontiguousarray(inp["ret_norm_g"].reshape(DEPTH, D)),
            "pscale": np.ascontiguousarray(inp["pool_scale"].reshape(DEPTH, D)),
            "ropec": ropec, "ropes": ropes, "cst": cst, "cstb": cstb, "rcnt": rcnt,
        })
    return nc, es, in_maps, cfg


def assemble(results, cfg):
    NSEQ, SEQ, DEC = cfg.nseq, cfg.seq, cfg.dec
    cat = lambda k, ax: np.concatenate([r[k] for r in results], axis=ax)
    y = cat("y", 0)
    ys = np.stack([r["ys"] for r in results], 0)
    nk = cat("nk", 1).reshape(DEPTH, NSEQ * NCORES, NMETA + SEQ, HS, DS)
    nv = cat("nv", 1).reshape(DEPTH, NSEQ * NCORES, NMETA + SEQ, HS, DS)
    nret = cat("nret", 1)
    npool = cat("npool", 1)
    sk = np.stack([r["sk"] for r in results], 1).reshape(DEPTH, NCORES, DEC, HS, DS)
    sv = np.stack([r["sv"] for r in results], 1).reshape(DEPTH, NCORES, DEC, HS, DS)
    sreto = np.stack([r["sreto"] for r in results], 1)
    spoolo = np.stack([r["spoolo"] for r in results], 1)
    return tuple(np.ascontiguousarray(a, dtype=np.float32) for a in (y, ys, nk, nv, nret, npool, sk, sv, sreto, spoolo))


def kernel(**inputs):
    nc, es, in_maps, cfg = run(inputs)
    with es:
        res = run_bass_kernel_spmd(nc, in_maps, core_ids=list(range(NCORES)))
    return assemble(res.results, cfg)
```

```python
import math
from contextlib import ExitStack
import numpy as np
import concourse.bass as bass
import concourse.mybir as mybir
from concourse.bass_utils import run_bass_kernel_spmd

F32 = mybir.dt.float32
BF16 = mybir.dt.bfloat16
AF = mybir.ActivationFunctionType
ALU = mybir.AluOpType

D = 1024
DFF = 2816
NMETA = 16
HR, DK, DV = 4, 128, 256
HS, DS = 8, 128
DEPTH = 2
PB = 15
ALPHA = (2 * DEPTH) ** 0.25
CRES = 0.5 / ALPHA
EPS_LN = 1e-5 / (ALPHA * ALPHA)
EPS_HN = 1e-5
NCORES = 8
NSLOT = 6
WMAX = 11 * 512
CASTW = 8192


def lg_decay():
    return np.log1p(-np.exp(np.linspace(math.log(1.0 / 32), math.log(1.0 / 512), HR, dtype=np.float32))).astype(np.float64)


def wtile_list():
    L = []
    for j in range(11):
        L.append(("up1", j, 8, 512))
    for nh in range(2):
        for kh in range(2):
            L.append(("dn1", (nh, kh), 11, 512))
    for nm in ("qr", "kr"):
        L.append((nm, 0, 8, 512))
    for nm in ("vr", "gr", "gt0", "wb0", "qs", "ks", "vs", "gt1", "wb1", "u"):
        L.append((nm, 0, 8, 512))
        L.append((nm, 1, 8, 512))
    L.append(("gt2", 0, 8, 512))
    L.append(("gt2", 1, 8, 512))
    L.append(("pm", 0, 8, 256))
    for nm in ("wb2", "wo"):
        L.append((nm, 0, 8, 512))
        L.append((nm, 1, 8, 512))
    for j in range(11):
        L.append(("up2", j, 8, 512))
    for nh in range(2):
        for kh in range(2):
            L.append(("dn2", (nh, kh), 11, 512))
    return L


WT = wtile_list()
WOFF = []
_o = 0
for _t in WT:
    WOFF.append(_o)
    _o += _t[2] * _t[3]
WTOT = _o
NCAST = (WTOT + CASTW - 1) // CASTW


def pack_tile(W, kc_n, cols):
    sub = W[:, cols]
    K = kc_n * 128
    assert sub.shape[0] == K
    return sub.reshape(kc_n, 128, len(cols)).transpose(1, 0, 2).reshape(128, kc_n * len(cols))


def pack_layer(l, inp):
    w_in = inp["w_in"][l]
    offs = np.cumsum([0, 512, 512, 1024, 1024, 1024, 1024, 1024, 1024, 3072])
    col0 = {"qr": offs[0], "kr": offs[1], "vr": offs[2], "gr": offs[3], "qs": offs[4], "ks": offs[5],
            "vs": offs[6], "u": offs[7], "gt0": offs[8], "gt1": offs[8] + 1024, "gt2": offs[8] + 2048}
    out = np.empty((128, WTOT), np.float32)
    for (nm, idx, kc_n, nc_), off in zip(WT, WOFF):
        if nm in ("up1", "up2"):
            W = inp["ffn1_up" if nm == "up1" else "ffn2_up"][l]
            cols = np.concatenate([np.arange(256 * idx, 256 * idx + 256), DFF + np.arange(256 * idx, 256 * idx + 256)])
            t = pack_tile(W, 8, cols)
        elif nm in ("dn1", "dn2"):
            W = inp["ffn1_down" if nm == "dn1" else "ffn2_down"][l]
            nh, kh = idx
            t = pack_tile(W[kh * 1408:(kh + 1) * 1408], 11, np.arange(nh * 512, nh * 512 + 512))
        elif nm in col0:
            t = pack_tile(w_in, 8, col0[nm] + np.arange(idx * 512, idx * 512 + 512))
        elif nm in ("wb0", "wb1", "wb2"):
            t = pack_tile(inp["w_branch"][l][int(nm[2])], 8, np.arange(idx * 512, idx * 512 + 512))
        elif nm == "wo":
            t = pack_tile(inp["w_out"][l], 8, np.arange(idx * 512, idx * 512 + 512))
        elif nm == "pm":
            pmw = inp["pool_mix_w"][l]
            t = pmw.reshape(4, 2, 128, 256).transpose(2, 0, 1, 3).reshape(128, 8 * 256)
        else:
            raise KeyError(nm)
        out[:, off:off + kc_n * nc_] = t
    return out


class Prog:
    def __init__(self):
        self.ins = []
        self.res = {}

    def add(self, eng, fn, reads=(), writes=(), dma=None):
        idx = len(self.ins)
        deps = set()
        for k in reads:
            st = self.res.setdefault(k, [None, {}])
            if st[0] is not None:
                deps.add(st[0])
        for k in writes:
            st = self.res.setdefault(k, [None, {}])
            if st[0] is not None:
                deps.add(st[0])
            deps.update(st[1].values())
        rkey = ("dma", idx) if dma else eng
        for k in reads:
            self.res[k][1][rkey] = idx
        for k in writes:
            self.res[k] = [idx, {}]
        deps.discard(idx)
        if eng == "pe":
            deps = {d for d in deps if self.ins[d]["eng"] != "pe" or self.ins[d]["dma"]}
        self.ins.append(dict(eng=eng, fn=fn, deps=sorted(deps), dma=dma))
        return idx

    def emit(self, nc, es):
        n = len(self.ins)
        flagged = [False] * n
        for it in self.ins:
            for d in it["deps"]:
                flagged[d] = True
        EPOCH = 12000
        NHW, NSW = 12, 6
        sems = {}

        def getsem(name):
            if name not in sems:
                sems[name] = es.enter_context(nc.semaphore(name))
            return sems[name]

        tok = [None] * n
        cnt = {}
        dcount = {}
        rr = {"hw": 0, "sw": 0}
        prevdma = [None] * n
        lastdma = {}
        for i, it in enumerate(self.ins):
            if it["dma"]:
                kind = it["dma"]
                nm = "d%s%d" % (kind, rr[kind] % (NHW if kind == "hw" else NSW))
                rr[kind] += 1
                dcount[nm] = dcount.get(nm, 0) + 1
                tok[i] = (nm, 16 * dcount[nm])
                prevdma[i] = lastdma.get(nm)
                lastdma[nm] = i
            elif flagged[i]:
                e = it["eng"]
                c = cnt.get(e, 0)
                cnt[e] = c + 1
                tok[i] = ("c%s%d" % (e, c // EPOCH), (c % EPOCH) + 1)
        byeng = {}
        for i, it in enumerate(self.ins):
            byeng.setdefault(it["eng"], []).append(i)
        final_waits = dict((nm, 16 * c) for nm, c in dcount.items())

        def run(eng_name, e):
            waited = {}
            for i in byeng.get(eng_name, []):
                it = self.ins[i]
                deps = list(it["deps"])
                if prevdma[i] is not None:
                    deps.append(prevdma[i])
                need = {}
                for d in deps:
                    nm, v = tok[d]
                    if waited.get(nm, 0) >= v:
                        continue
                    need[nm] = max(need.get(nm, 0), v)
                need = list(need.items())
                for nm, v in need[:-1]:
                    e.wait_ge(getsem(nm), v)
                    waited[nm] = v
                bi = it["fn"](e)
                if need:
                    nm, v = need[-1]
                    bi._wait_ge(getsem(nm), v)
                    waited[nm] = v
                if tok[i] is not None:
                    nm, v = tok[i]
                    bi.then_inc(getsem(nm), 16 if it["dma"] else 1)
            if eng_name == "sp":
                for nm, v in final_waits.items():
                    e.wait_ge(getsem(nm), v)

        for t in tok:
            if t is not None:
                getsem(t[0])
        with nc.Block() as block:
            @block.sync
            def _(e):
                run("sp", e)

            @block.scalar
            def _(e):
                run("act", e)

            @block.vector
            def _(e):
                run("dve", e)

            @block.gpsimd
            def _(e):
                run("pool", e)

            @block.tensor
            def _(e):
                run("pe", e)


class Cfg:
    def __init__(self, nseq, seq, past, dec_seq=32):
        self.nseq, self.seq, self.past, self.dec = nseq, seq, past, dec_seq


def build(cfg):
    nc = bass.Bass("TRN2", target_bir_lowering=False)
    P = Prog()
    es = ExitStack()
    NSEQ, SEQ, PAST, DEC = cfg.nseq, cfg.seq, cfg.past, cfg.dec
    NT = SEQ // 128

    def din(name, shape, dt=F32):
        return nc.dram_tensor(name, list(shape), dt, kind="ExternalInput").ap()

    def dout(name, shape, dt=F32):
        return nc.dram_tensor(name, list(shape), dt, kind="ExternalOutput").ap()

    x = din("x", [NSEQ, SEQ, D])
    xs = din("xs", [DEC, D])
    ck = din("ck", [DEPTH, PAST, D])
    cv = din("cv", [DEPTH, PAST, D])
    sret = din("sret", [DEPTH, HR, DK, DV])
    spool = din("spool", [DEPTH, PB, D])
    meta = din("meta", [NMETA, D])
    wfl = [din("wflat%d" % l, [128, WTOT]) for l in range(DEPTH)]
    lng = din("lng", [DEPTH * 3, D])
    lnb = din("lnb", [DEPTH * 3, D])
    retg = din("retg", [DEPTH, D])
    pscale = din("pscale", [DEPTH, D])
    ropec = din("ropec", [NMETA + SEQ + DEC, 64])
    ropes = din("ropes", [NMETA + SEQ + DEC, 128])
    cst = din("cst", [128, 640])
    cstb = din("cstb", [128, 512], BF16)
    rcnt = din("rcnt", [128, 8 * NMETA])

    y = dout("y", [NSEQ, SEQ, D])
    ys = dout("ys", [DEC, D])
    nk = dout("nk", [DEPTH, NSEQ, NMETA + SEQ, D])
    nv = dout("nv", [DEPTH, NSEQ, NMETA + SEQ, D])
    nret = dout("nret", [DEPTH, NSEQ, HR, DK, DV])
    npool = dout("npool", [DEPTH, NSEQ, PB, D])
    sk = dout("sk", [DEPTH, DEC, D])
    sv = dout("sv", [DEPTH, DEC, D])
    sreto = dout("sreto", [DEPTH, HR, DK, DV])
    spoolo = dout("spoolo", [DEPTH, PB, D])
    wbf = [nc.dram_tensor("wbf%d" % l, [128, WTOT], BF16, kind="Internal").ap() for l in range(DEPTH)]

    def sb(name, shape, dt=F32):
        return es.enter_context(nc.sbuf_tensor(name, list(shape), dt))

    def ps(name, shape, dt=F32):
        return es.enter_context(nc.psum_tensor(name, list(shape), dt))

    hres = sb("hres", [128, D])
    hT = sb("hT", [128, 8, 128], BF16)
    wsl = [sb("wsl%d" % i, [128, WMAX], BF16) for i in range(NSLOT)]
    S = [sb("S%d" % l, [128, HR, DV]) for l in range(DEPTH)]
    Sm = [sb("Sm%d" % l, [128, HR, DV]) for l in range(DEPTH)]
    Sbf = sb("Sbf", [128, HR, DV], BF16)
    zT = [sb("zT%d" % l, [128, 8, PB + 128]) for l in range(DEPTH)]
    zm = [sb("zm%d" % l, [128, 8, PB]) for l in range(DEPTH)]
    C = sb("C", [128, 640])
    CB = sb("CB", [128, 512], BF16)
    RC = sb("RC", [128, 8, NMETA])
    lnG = sb("lnG", [128, D])
    lnB = sb("lnB", [128, D])
    rc = sb("rc", [128, 64])
    rs = sb("rs", [128, 128])
    hb = sb("hb", [128, D], BF16)
    act_tok = sb("act_tok", [128, DFF], BF16)
    actT = sb("actT", [128, 22, 128], BF16)
    stmp = sb("stmp", [128, 256], BF16)
    f1 = sb("f1", [128, D])
    f2 = sb("f2", [128, D])
    mix = sb("mix", [128, D])
    st6 = sb("st6", [128, 8, 6])
    mv = sb("mv", [128, 4, 2])
    sc1 = sb("sc1", [128, 4])
    sc2 = sb("sc2", [128, 4])
    qt = sb("qt", [128, 512], BF16)
    kt = sb("kt", [128, 512], BF16)
    qkT = sb("qkT", [128, 8, 128], BF16)
    vr = sb("vr", [128, D], BF16)
    gr = sb("gr", [128, D], BF16)
    th = sb("th", [128, D], BF16)
    attm = sb("attm", [128, 4, 128], BF16)
    oT = sb("oT", [128, 8, 128], BF16)
    qsT = sb("qsT", [128, 8, 128], BF16)
    kb_own = sb("kb_own", [128, D], BF16)
    kT_own = sb("kT_own", [128, 8, 128], BF16)
    v_own = sb("v_own", [128, D], BF16)
    kpb = [sb("kpb%d" % i, [128, D], BF16) for i in range(2)]
    vpb = [sb("vpb%d" % i, [128, D], BF16) for i in range(2)]
    kTb = [sb("kTb%d" % i, [128, 8, 128], BF16) for i in range(2)]
    eb = [sb("eb%d" % i, [128, 4, 128]) for i in range(2)]
    spb = [sb("spb%d" % i, [128, 4, 128], BF16) for i in range(2)]
    wb_ = [sb("wb%d" % i, [128, 4, 128], BF16) for i in range(2)]
    ltc = [sb("ltc%d" % i, [128, 4, 128]) for i in range(2)]
    ltcb = [sb("ltcb%d" % i, [128, 4, 128], BF16) for i in range(2)]
    pooledT = sb("pooledT", [128, 8, 128], BF16)
    ptmp = sb("ptmp", [128, 2, PB + 128])
    ptmp2 = sb("ptmp2", [128, 2, PB + 128])

    pf = [ps("pf%d" % i, [128, 512]) for i in range(6)]
    pb = [ps("pb%d" % i, [128, 1024], BF16) for i in range(2)]
    pfi = [0]
    pbi = [0]

    def nf():
        i = pfi[0] % 4
        pfi[0] += 1
        return i

    def nb():
        i = pbi[0] % 2
        pbi[0] += 1
        return i

    identf = C[:, 0:128]
    cmask = C[:, 128:256]
    identb = CB[:, 0:128]
    dmask = CB[:, 128:256]
    ntri = CB[:, 256:384]
    nones = CB[:, 384:512]

    def DMA(out, in_, reads, writes, eng="sp"):
        kind = "sw" if eng == "pool" else "hw"
        P.add(eng, lambda e: e.dma_start(out=out, in_=in_), reads, writes, dma=kind)

    def MM(out, lhsT, rhs, start, stop, reads, writes):
        P.add("pe", lambda e: e.matmul(out, lhsT=lhsT, rhs=rhs, start=start, stop=stop), reads, writes)

    def TR(out, in_, np_, reads, writes):
        P.add("pe", lambda e: e.transpose(out, in_, identb[:np_, :np_]), list(reads) + ["CB"], writes)

    def ACTF(out, in_, func, reads, writes, scale=1.0, bias=0.0):
        P.add("act", lambda e: e.activation(out=out, in_=in_, func=func, bias=bias, scale=scale), reads, writes)

    def TT(eng, out, in0, in1, op, reads, writes):
        P.add(eng, lambda e: e.tensor_tensor(out=out, in0=in0, in1=in1, op=op), reads, writes)

    def STT(eng, out, in0, scalar, in1, op0, op1, reads, writes):
        P.add(eng, lambda e: e.scalar_tensor_tensor(out=out, in0=in0, scalar=scalar, in1=in1, op0=op0, op1=op1), reads, writes)

    def TS(eng, out, in0, s1, s2, op0, op1, reads, writes):
        if s2 is None:
            P.add(eng, lambda e: e.tensor_scalar(out=out, in0=in0, scalar1=s1, scalar2=None, op0=op0), reads, writes)
        else:
            P.add(eng, lambda e: e.tensor_scalar(out=out, in0=in0, scalar1=s1, scalar2=s2, op0=op0, op1=op1), reads, writes)

    def CP(eng, out, in_, reads, writes):
        if eng == "act":
            P.add("act", lambda e: e.activation(out=out, in_=in_, func=AF.Copy), reads, writes)
        else:
            P.add(eng, lambda e: e.tensor_copy(out=out, in_=in_), reads, writes)

    def MSET(eng, ap, val, writes):
        P.add(eng, lambda e: e.memset(ap, val), (), writes)

    MSET("pool", ptmp[:, :, :], 0.0, ["ptmp"])
    MSET("pool", ptmp2[:, :, :], 0.0, ["ptmp2"])
    DMA(C[:, :], cst[:, :], (), ["C"])
    DMA(CB[:, :], cstb[:, :], (), ["CB"])
    DMA(RC[:, :, :], rcnt.rearrange("p (c t) -> p c t", c=8), (), ["RC"])
    for l in range(DEPTH):
        for c in range(NCAST):
            c0, c1 = c * CASTW, min(WTOT, (c + 1) * CASTW)
            P.add("pool", (lambda e, l=l, c0=c0, c1=c1: e.dma_start(out=wbf[l][:, c0:c1], in_=wfl[l][:, c0:c1],
                                                                   max_dma_last_dim=4096)),
                  (), [("wbf", l, c)], dma="sw")

    ws = dict(n=0, seq=[])

    def w_issue(k):
        l, ti = ws["seq"][k]
        nm, idx, kc_n, ncol = WT[ti]
        off = WOFF[ti]
        width = kc_n * ncol
        slot = k % NSLOT
        cs = range(off // CASTW, (off + width - 1) // CASTW + 1)
        DMA(wsl[slot][:, 0:width], wbf[l][:, off:off + width], [("wbf", l, c) for c in cs], ["wsl%d" % slot])

    def w_start(seqlist):
        ws["seq"] = seqlist
        ws["n"] = 0
        for k in range(min(NSLOT, len(seqlist))):
            w_issue(k)

    def w_get(expect):
        k = ws["n"]
        l, ti = ws["seq"][k]
        nm, idx, kc_n, ncol = WT[ti]
        assert nm == expect, (nm, expect)
        slot = k % NSLOT
        view = wsl[slot][:, 0:kc_n * ncol].rearrange("p (k n) -> p k n", k=kc_n)
        return view, "wsl%d" % slot

    def w_done():
        k = ws["n"]
        ws["n"] = k + 1
        if k + NSLOT < len(ws["seq"]):
            w_issue(k + NSLOT)

    def transpose_to(dstT, src_tok, T, nchunk, srckey, dstkey, col0=0):
        c = 0
        while c < nchunk:
            n = min(8, nchunk - c)
            b = nb()
            for j in range(n):
                TR(pb[b][:, j * 128:j * 128 + T], src_tok[:T, col0 + (c + j) * 128: col0 + (c + j + 1) * 128], T,
                   [srckey], ["pb%d" % b])
            CP("act", dstT[:, c:c + n, :T], pb[b][:, 0:n * 128].rearrange("p (c t) -> p c t", c=n)[:, :, :T],
               ["pb%d" % b], [dstkey])
            c += n

    def proj_tok(T, inT, inkey, wname, nparts, evac):
        for part in range(nparts):
            w, wkey = w_get(wname)
            kc_n = w.shape[1]
            ncol = w.shape[2]
            b = nf()
            for kc in range(kc_n):
                MM(pf[b][:T, 0:ncol], inT[:, kc, :T], w[:, kc, :], kc == 0, kc == kc_n - 1, [inkey, wkey], ["pf%d" % b])
            w_done()
            evac(part, pf[b], "pf%d" % b)

    def layer_norm(T, l, which):
        row = l * 3 + which
        DMA(lnG[:T, :], lng[row:row + 1, :].to_broadcast([T, D]), (), ["lnG"])
        DMA(lnB[:T, :], lnb[row:row + 1, :].to_broadcast([T, D]), (), ["lnB"])
        for c in range(2):
            P.add("dve", lambda e, c=c: e.bn_stats(out=st6[:T, c, :], in_=hres[:T, c * 512:(c + 1) * 512]), ["hres"], ["st6"])
        P.add("dve", lambda e: e.bn_aggr(out=mv[:T, 0, :], in_=st6[:T, 0:2, :]), ["st6"], ["mv"])
        rstd_from_var(T, 1, EPS_LN)
        ACTF(f1[:T, :], hres[:T, :], AF.Identity, ["hres", "sc1", "sc2"], ["f1"], scale=sc1[:T, 0:1], bias=sc2[:T, 0:1])
        TT("dve", f1[:T, :], f1[:T, :], lnG[:T, :], ALU.mult, ["f1", "lnG"], ["f1"])
        TT("dve", hres[:T, :], f1[:T, :], lnB[:T, :], ALU.add, ["f1", "lnB"], ["hres"])
        CP("pool", hb[:T, :], hres[:T, :], ["hres"], ["hb"])
        transpose_to(hT, hb, T, 8, "hb", "hT")

    def rstd_from_var(T, n, eps):
        ACTF(sc1[:T, 0:n], mv[:T, 0:n, 1], AF.Ln, ["mv"], ["sc1"], bias=eps)
        ACTF(sc1[:T, 0:n], sc1[:T, 0:n], AF.Exp, ["sc1"], ["sc1"], scale=-0.5)
        STT("dve", sc2[:T, 0:n], mv[:T, 0:n, 0], -1.0, sc1[:T, 0:n], ALU.mult, ALU.mult, ["mv", "sc1"], ["sc2"])

    def ffn(T, l, which):
        upn, dnn = ("up1", "dn1") if which == 0 else ("up2", "dn2")
        for j in range(11):
            w, wkey = w_get(upn)
            b = nf()
            for kc in range(8):
                MM(pf[b][:T, :], hT[:, kc, :T], w[:, kc, :], kc == 0, kc == 7, ["hT", wkey], ["pf%d" % b])
            w_done()
            ACTF(stmp[:T, :], pf[b][:T, 0:256], AF.Silu, ["pf%d" % b], ["stmp"])
            TT("dve", act_tok[:T, j * 256:(j + 1) * 256], pf[b][:T, 256:512], stmp[:T, :], ALU.mult,
               ["pf%d" % b, "stmp"], ["act_tok"])
        transpose_to(actT, act_tok, T, 22, "act_tok", "actT")
        for nh in range(2):
            b = nf()
            for kh in range(2):
                w, wkey = w_get(dnn)
                for kc in range(11):
                    MM(pf[b][:T, :], actT[:, kh * 11 + kc, :T], w[:, kc, :], kh == 0 and kc == 0, kh == 1 and kc == 10,
                       ["actT", wkey], ["pf%d" % b])
                w_done()
            STT("dve", hres[:T, nh * 512:(nh + 1) * 512], pf[b][:T, :], CRES, hres[:T, nh * 512:(nh + 1) * 512],
                ALU.mult, ALU.add, ["pf%d" % b, "hres"], ["hres"])
        layer_norm(T, l, 0 if which == 0 else 2)

    def gate_mix(T, bi, first):
        def ev(part, pbank, pkey):
            sl = slice(part * 512, (part + 1) * 512)
            if first:
                STT("dve", mix[:T, sl], th[:T, sl], 1.0, pbank[:T, :], ALU.add, ALU.mult, ["th", pkey], ["mix"])
            else:
                STT("dve", f2[:T, sl], th[:T, sl], 1.0, pbank[:T, :], ALU.add, ALU.mult, ["th", pkey], ["f2"])
                TT("pool", mix[:T, sl], mix[:T, sl], f2[:T, sl], ALU.add, ["mix", "f2"], ["mix"])
        proj_tok(T, oT, "oT", "wb%d" % bi, 2, ev)

    def gates(T, bi):
        def ev(part, pbank, pkey):
            ACTF(th[:T, part * 512:(part + 1) * 512], pbank[:T, :], AF.Tanh, [pkey], ["th"], scale=0.5)
        proj_tok(T, hT, "hT", "gt%d" % bi, 2, ev)

    def rotary(T, dst, dkey, pbank, pkey, sccol):
        xv = pbank[:T, :].rearrange("p (h t d) -> p h t d", h=4, t=2)
        cosb = rc[:T, :].unsqueeze(1).unsqueeze(1).to_broadcast([T, 4, 2, 64])
        TT("dve", f1[:T, 0:512].rearrange("p (h t d) -> p h t d", h=4, t=2), xv, cosb, ALU.mult, [pkey, "rc"], ["f1"])
        f2v = f2[:T, 0:512].rearrange("p (h t d) -> p h t d", h=4, t=2)
        for t in range(2):
            sinb = rs[:T, t * 64:(t + 1) * 64].unsqueeze(1).to_broadcast([T, 4, 64])
            TT("dve", f2v[:, :, t, :], xv[:, :, 1 - t, :], sinb, ALU.mult, [pkey, "rs"], ["f2"])
        TT("pool", f1[:T, 0:512], f1[:T, 0:512], f2[:T, 0:512], ALU.add, ["f1", "f2"], ["f1"])
        scb = C[:T, sccol:sccol + 4].unsqueeze(2).to_broadcast([T, 4, 128])
        TT("pool", dst[:T, :].rearrange("p (h d) -> p h d", h=4), f1[:T, 0:512].rearrange("p (h d) -> p h d", h=4), scb,
           ALU.mult, ["f1", "C"], [dkey])

    def retention(T, l, Sl, Skey, dccol):
        qk_keys = {}

        def ev_q(part, pbank, pkey):
            rotary(T, qt, "qt", pbank, pkey, 256)

        def ev_k(part, pbank, pkey):
            rotary(T, kt, "kt", pbank, pkey, 260)

        proj_tok(T, hT, "hT", "qr", 1, ev_q)
        proj_tok(T, hT, "hT", "kr", 1, ev_k)

        def ev_v(part, pbank, pkey):
            CP("act", vr[:T, part * 512:(part + 1) * 512], pbank[:T, :], [pkey], ["vr"])
        proj_tok(T, hT, "hT", "vr", 2, ev_v)

        def ev_g(part, pbank, pkey):
            ACTF(gr[:T, part * 512:(part + 1) * 512], pbank[:T, :], AF.Silu, [pkey], ["gr"])
        proj_tok(T, hT, "hT", "gr", 2, ev_g)
        gates(T, 0)
        b = nb()
        for h in range(4):
            TR(pb[b][:, h * 128:h * 128 + T], qt[:T, h * 128:(h + 1) * 128], T, ["qt"], ["pb%d" % b])
        for h in range(4):
            TR(pb[b][:, (4 + h) * 128:(4 + h) * 128 + T], kt[:T, h * 128:(h + 1) * 128], T, ["kt"], ["pb%d" % b])
        CP("act", qkT[:, :, :T], pb[b][:, :].rearrange("p (c t) -> p c t", c=8)[:, :, :T], ["pb%d" % b], ["qkT"])
        CP("pool", Sbf[:, :, :], Sl[:, :, :], [Skey], ["Sbf"])
        b = nf()
        for h in range(4):
            MM(pf[b][:T, h * 128:h * 128 + T], qkT[:, 4 + h, :T], qkT[:, h, :T], h == 0, h == 3, ["qkT"], ["pf%d" % b])
        TT("dve", attm[:T, :, :T], pf[b][:T, :].rearrange("p (h t) -> p h t", h=4)[:, :, :T],
           cmask[:T, :T].unsqueeze(1).to_broadcast([T, 4, T]), ALU.mult, ["pf%d" % b, "C"], ["attm"])
        ob = [nf(), nf()]
        for h in range(4):
            bb = ob[h // 2]
            osl = pf[bb][:T, (h % 2) * 256:(h % 2) * 256 + 256]
            MM(osl, attm[:T, h, :T], vr[:T, h * 256:(h + 1) * 256], h % 2 == 0, False, ["attm", "vr"], ["pf%d" % bb])
            MM(osl, qkT[:, h, :T], Sbf[:, h, :], False, h % 2 == 1, ["qkT", "Sbf"], ["pf%d" % bb])
        for i in range(2):
            CP("act", f1[:T, i * 512:(i + 1) * 512], pf[ob[i]][:T, :], ["pf%d" % ob[i]], ["f1"])
        sbk = [nf(), nf()]
        for h in range(4):
            bb = sbk[h // 2]
            MM(pf[bb][:, (h % 2) * 256:(h % 2) * 256 + 256], kt[:T, h * 128:(h + 1) * 128], vr[:T, h * 256:(h + 1) * 256],
               h % 2 == 0, h % 2 == 1, ["kt", "vr"], ["pf%d" % bb])
        for i in range(2):
            TT("dve", Sl[:, 2 * i:2 * i + 2, :], pf[sbk[i]][:, :].rearrange("p (h v) -> p h v", h=2), Sl[:, 2 * i:2 * i + 2, :],
               ALU.add, ["pf%d" % sbk[i], Skey], [Skey])
        TT("dve", Sl[:, :, :], Sl[:, :, :], C[:, dccol:dccol + 4].unsqueeze(2).to_broadcast([128, 4, DV]), ALU.mult,
           [Skey, "C"], [Skey])
        for h in range(4):
            P.add("dve", lambda e, h=h: e.bn_stats(out=st6[:T, h, :], in_=f1[:T, h * 256:(h + 1) * 256]), ["f1"], ["st6"])
        for h in range(4):
            P.add("dve", lambda e, h=h: e.bn_aggr(out=mv[:T, h, :], in_=st6[:T, h:h + 1, :]), ["st6"], ["mv"])
        rstd_from_var(T, 4, EPS_HN)
        for h in range(4):
            ACTF(f2[:T, h * 256:(h + 1) * 256], f1[:T, h * 256:(h + 1) * 256], AF.Identity, ["f1", "sc1", "sc2"], ["f2"],
                 scale=sc1[:T, h:h + 1], bias=sc2[:T, h:h + 1])
        DMA(lnG[:T, :], retg[l:l + 1, :].to_broadcast([T, D]), (), ["lnG"])
        TT("dve", f2[:T, :], f2[:T, :], lnG[:T, :], ALU.mult, ["f2", "lnG"], ["f2"])
        TT("dve", hb[:T, :], f2[:T, :], gr[:T, :], ALU.mult, ["f2", "gr"], ["hb"])
        transpose_to(oT, hb, T, 8, "hb", "oT")
        gate_mix(T, 0, True)

    def stick(T, l, kdst, vdst, past):
        def ev_q(part, pbank, pkey):
            ACTF(hb[:T, part * 512:(part + 1) * 512], pbank[:T, :], AF.Copy, [pkey], ["hb"], scale=DS ** -0.5)
        proj_tok(T, hT, "hT", "qs", 2, ev_q)
        transpose_to(qsT, hb, T, 8, "hb", "qsT")

        def ev_k(part, pbank, pkey):
            CP("act", f1[:T, part * 512:(part + 1) * 512], pbank[:T, :], [pkey], ["f1"])
        proj_tok(T, hT, "hT", "ks", 2, ev_k)
        for (ap, wk) in kdst:
            DMA(ap, f1[:T, :], ["f1"], [wk])
        CP("pool", kb_own[:T, :], f1[:T, :], ["f1"], ["kb_own"])
        transpose_to(kT_own, kb_own, T, 8, "kb_own", "kT_own")

        def ev_v(part, pbank, pkey):
            CP("act", f2[:T, part * 512:(part + 1) * 512], pbank[:T, :], [pkey], ["f2"])
        proj_tok(T, hT, "hT", "vs", 2, ev_v)
        for (ap, wk) in vdst:
            DMA(ap, f2[:T, :], ["f2"], [wk])
        CP("pool", v_own[:T, :], f2[:T, :], ["f2"], ["v_own"])
        gates(T, 1)
        blocks = [("own", None, None, T, None)] + [("past",) + p for p in past]
        for hg in range(2):
            MSET("pool", ltc[hg][:, :, :], 0.0, ["ltc%d" % hg])
        ob = [4, 5]
        nblk = len(blocks)

        def load_dma(bi_):
            blk = blocks[bi_]
            nkk = blk[3]
            s = bi_ % 2
            DMA(kpb[s][:nkk, :], blk[1], blk[4], ["kpb%d" % s], eng="pool")
            DMA(vpb[s][:nkk, :], blk[2], blk[4], ["vpb%d" % s], eng="pool")

        def load_tr(bi_):
            blk = blocks[bi_]
            s = bi_ % 2
            transpose_to(kTb[s], kpb[s], blk[3], 8, "kpb%d" % s, "kTb%d" % s)

        if nblk > 1:
            load_dma(1)
            load_tr(1)
        for bi_, blk in enumerate(blocks):
            kind = blk[0]
            nkk = blk[3]
            if kind == "own":
                kTs, kTkey, vs_, vkey = kT_own, "kT_own", v_own, "v_own"
            else:
                s = bi_ % 2
                kTs, kTkey, vs_, vkey = kTb[s], "kTb%d" % s, vpb[s], "vpb%d" % s
            if bi_ >= 1 and bi_ + 1 < nblk:
                load_dma(bi_ + 1)
            HG = range(2)
            mk = dmask[:nkk, :T].unsqueeze(1).to_broadcast([nkk, 4, T])
            zb = [nf(), nf()]
            for hg in HG:
                for hh in range(4):
                    h = hg * 4 + hh
                    MM(pf[zb[hg]][:nkk, hh * 128:hh * 128 + T], kTs[:, h, :nkk], qsT[:, h, :T], hh == 0, hh == 3,
                       [kTkey, "qsT"], ["pf%d" % zb[hg]])
            for hg in HG:
                zv = pf[zb[hg]][:nkk, :].rearrange("p (h t) -> p h t", h=4)[:, :, :T]
                ACTF(eb[hg][:nkk, :, :T], zv, AF.Exp, ["pf%d" % zb[hg]], ["eb%d" % hg])
            for hg in HG:
                ACTF(spb[hg][:nkk, :, :T], eb[hg][:nkk, :, :T], AF.Ln, ["eb%d" % hg], ["spb%d" % hg], bias=1.0)
            if kind == "own":
                for hg in HG:
                    TT("pool", spb[hg][:nkk, :, :T], spb[hg][:nkk, :, :T], mk, ALU.mult, ["spb%d" % hg, "CB"], ["spb%d" % hg])
            bb = [nf(), nf()]
            for hg in HG:
                for hh in range(4):
                    h = hg * 4 + hh
                    osl = pf[bb[hg]][:nkk, hh * 128:hh * 128 + T]
                    MM(osl, kTs[:, h, :nkk], qsT[:, h, :T], hh == 0, False, [kTkey, "qsT"], ["pf%d" % bb[hg]])
                    MM(osl, ntri[:nkk, :nkk], spb[hg][:nkk, hh, :T], False, bi_ == 0 and hh == 3, ["CB", "spb%d" % hg],
                       ["pf%d" % bb[hg]])
                    if bi_ > 0:
                        MM(osl, nones[:, :nkk], ltcb[hg][:, hh, :T], False, hh == 3, ["CB", "ltcb%d" % hg], ["pf%d" % bb[hg]])
            for hg in HG:
                bv = pf[bb[hg]][:nkk, :].rearrange("p (h t) -> p h t", h=4)[:, :, :T]
                ACTF(wb_[hg][:nkk, :, :T], bv, AF.Exp, ["pf%d" % bb[hg]], ["wb%d" % hg])
            if kind == "own":
                for hg in HG:
                    TT("pool", wb_[hg][:nkk, :, :T], wb_[hg][:nkk, :, :T], mk, ALU.mult, ["wb%d" % hg, "CB"], ["wb%d" % hg])
            for hg in HG:
                for hh in range(4):
                    h = hg * 4 + hh
                    MM(pf[ob[hg]][:T, hh * 128:(hh + 1) * 128], wb_[hg][:nkk, hh, :T], vs_[:nkk, h * 128:(h + 1) * 128],
                       bi_ == 0 and hh == 0, bi_ == nblk - 1 and hh == 3, ["wb%d" % hg, vkey], ["pf%d" % ob[hg]])
            if bi_ < nblk - 1:
                for hg in HG:
                    TT("pool", ltc[hg][:nkk, :, :T], ltc[hg][:nkk, :, :T], spb[hg][:nkk, :, :T], ALU.add,
                       ["ltc%d" % hg, "spb%d" % hg], ["ltc%d" % hg])
                    CP("pool", ltcb[hg][:, :, :T], ltc[hg][:, :, :T], ["ltc%d" % hg], ["ltcb%d" % hg])
            if bi_ >= 1 and bi_ + 1 < nblk:
                load_tr(bi_ + 1)
        for hg in range(2):
            CP("act", hb[:T, hg * 512:(hg + 1) * 512], pf[ob[hg]][:T, :], ["pf%d" % ob[hg]], ["hb"])
        transpose_to(oT, hb, T, 8, "hb", "oT")
        gate_mix(T, 1, False)

    def pool_mixer(T, l, zl, zkey, is_meta, pooldst):
        for part in range(2):
            w, wkey = w_get("u")
            for oc in range(4):
                b = nf()
                for kc in range(8):
                    MM(pf[b][:, :T], w[:, kc, oc * 128:(oc + 1) * 128], hT[:, kc, :T], kc == 0, kc == 7, ["hT", wkey], ["pf%d" % b])
                CP("act", zl[:, part * 4 + oc, PB:PB + T], pf[b][:, :T], ["pf%d" % b], [zkey])
            w_done()
        gates(T, 2)
        W_ = PB + T
        for g, wnd in enumerate((2, 4, 8, 16)):
            cs = slice(2 * g, 2 * g + 2)
            cur = zl[:, cs, :]
            sh = 1
            bufs = [ptmp, ptmp2]
            bi_ = 0
            curkey = zkey
            while sh < wnd:
                dst = bufs[bi_ % 2]
                dkey = "ptmp" if bi_ % 2 == 0 else "ptmp2"
                TT("pool", dst[:, :, sh:W_], cur[:, :, sh:W_], cur[:, :, 0:W_ - sh], ALU.add, [curkey], [dkey])
                cur, curkey = dst, dkey
                sh *= 2
                bi_ += 1
            if is_meta:
                TT("dve", f1[:, 0:2 * T].rearrange("p (c t) -> p c t", c=2), cur[:, :, PB:PB + T], RC[:, cs, :T], ALU.mult,
                   [curkey, "RC"], ["f1"])
                TT("dve", pooledT[:, cs, :T], f1[:, 0:2 * T].rearrange("p (c t) -> p c t", c=2), zl[:, cs, PB:PB + T],
                   ALU.subtract, ["f1", zkey], ["pooledT"])
            else:
                STT("dve", pooledT[:, cs, :T], cur[:, :, PB:PB + T], 1.0 / wnd, zl[:, cs, PB:PB + T], ALU.mult, ALU.subtract,
                    [curkey, zkey], ["pooledT"])
        w, wkey = w_get("pm")
        ob = [nf(), nf()]
        for g in range(4):
            bb = ob[g // 2]
            for cc in range(2):
                MM(pf[bb][:T, (g % 2) * 256:(g % 2) * 256 + 256], pooledT[:, g * 2 + cc, :T], w[:, g * 2 + cc, :],
                   g % 2 == 0 and cc == 0, g % 2 == 1 and cc == 1, ["pooledT", wkey], ["pf%d" % bb])
        w_done()
        DMA(lnG[:T, :], pscale[l:l + 1, :].to_broadcast([T, D]), (), ["lnG"])
        for i in range(2):
            TT("dve", hb[:T, i * 512:(i + 1) * 512], pf[ob[i]][:T, :], lnG[:T, i * 512:(i + 1) * 512], ALU.mult,
               ["pf%d" % ob[i], "lnG"], ["hb"])
        transpose_to(oT, hb, T, 8, "hb", "oT")
        gate_mix(T, 2, False)
        if pooldst is not None:
            src0 = PB + T - 128 if T >= 128 else None
            for half in range(2):
                b = nf()
                for j in range(4):
                    c = half * 4 + j
                    if T >= 128:
                        P.add("pe", lambda e, b=b, j=j, c=c: e.transpose(pf[b][:, j * 128:(j + 1) * 128], zl[:, c, PB + T - 128:PB + T], identf),
                              [zkey, "C"], ["pf%d" % b])
                    else:
                        P.add("pe", lambda e, b=b, j=j, c=c: e.transpose(pf[b][:PB + T, j * 128:(j + 1) * 128], zl[:, c, 0:PB + T], identf),
                              [zkey, "C"], ["pf%d" % b])
                rows = 128 if T >= 128 else PB + T
                CP("act", f1[:rows, half * 512:(half + 1) * 512], pf[b][:rows, :], ["pf%d" % b], ["f1"])
            rows = 128 if T >= 128 else PB + T
            for (ap, wk) in pooldst:
                DMA(ap, f1[rows - PB:rows, :], ["f1"], [wk])
        CP("pool", zl[:, :, 0:PB], zl[:, :, T:T + PB], [zkey], [zkey])

    def out_proj(T, l):
        CP("act", hb[:T, :], mix[:T, :], ["mix"], ["hb"])
        transpose_to(oT, hb, T, 8, "hb", "oT")

        def ev(part, pbank, pkey):
            sl = slice(part * 512, (part + 1) * 512)
            STT("dve", hres[:T, sl], pbank[:T, :], CRES, hres[:T, sl], ALU.mult, ALU.add, [pkey, "hres"], ["hres"])
        proj_tok(T, oT, "oT", "wo", 2, ev)
        layer_norm(T, l, 1)

    def load_rope(T, row0):
        DMA(rc[:T, :], ropec[row0:row0 + T, :], (), ["rc"])
        DMA(rs[:T, :], ropes[row0:row0 + T, :], (), ["rs"])

    def load_tile(T, src):
        DMA(hres[:T, :], src, (), ["hres"])
        CP("pool", hb[:T, :], hres[:T, :], ["hres"], ["hb"])
        transpose_to(hT, hb, T, 8, "hb", "hT")

    def layer(T, l, Sl, Skey, dccol, zl, zkey, is_meta, kdst, vdst, past, pooldst):
        ffn(T, l, 0)
        retention(T, l, Sl, Skey, dccol)
        stick(T, l, kdst, vdst, past)
        pool_mixer(T, l, zl, zkey, is_meta, pooldst)
        out_proj(T, l)
        ffn(T, l, 1)

    ntiles_total = 2 + NSEQ * NT
    seqlist = []
    for _ in range(ntiles_total):
        for l in range(DEPTH):
            for ti in range(len(WT)):
                seqlist.append((l, ti))
    w_start(seqlist)

    T = NMETA
    load_rope(T, 0)
    load_tile(T, meta[:, :])
    for l in range(DEPTH):
        MSET("pool", Sm[l][:, :, :], 0.0, ["Sm%d" % l])
        MSET("pool", zT[l][:, :, :], 0.0, ["zT%d" % l])
        kd = [(nk[l, s, 0:NMETA, :], ("nk", l, s, "m")) for s in range(NSEQ)]
        vd = [(nv[l, s, 0:NMETA, :], ("nv", l, s, "m")) for s in range(NSEQ)]
        layer(T, l, Sm[l], "Sm%d" % l, 272, zT[l], "zT%d" % l, True, kd, vd, [], None)
        CP("pool", zm[l][:, :, :], zT[l][:, :, 0:PB], ["zT%d" % l], ["zm%d" % l])

    T = DEC
    load_rope(T, NMETA + SEQ)
    load_tile(T, xs[:, :])
    for l in range(DEPTH):
        DMA(S[l][:, :, :], sret[l].rearrange("h k v -> k h v"), (), ["S%d" % l])
        DMA(f1[:PB, :], spool[l, :, :], (), ["f1"])
        for half in range(2):
            b = nf()
            for j in range(4):
                c = half * 4 + j
                P.add("pe", lambda e, b=b, j=j, c=c: e.transpose(pf[b][:, j * PB:(j + 1) * PB], f1[:PB, c * 128:(c + 1) * 128], identf[:PB, :PB]),
                      ["f1", "C"], ["pf%d" % b])
            CP("act", zT[l][:, half * 4:half * 4 + 4, 0:PB], pf[b][:, 0:4 * PB].rearrange("p (c t) -> p c t", c=4),
               ["pf%d" % b], ["zT%d" % l])
        nblk = PAST // 128
        past = [(ck[l, bk * 128:(bk + 1) * 128, :], cv[l, bk * 128:(bk + 1) * 128, :], 128, []) for bk in reversed(range(nblk))]
        layer(T, l, S[l], "S%d" % l, 268, zT[l], "zT%d" % l, False, [(sk[l, :, :], ("sk", l))], [(sv[l, :, :], ("sv", l))],
              past, [(spoolo[l, :, :], ("spoolo", l))])
        DMA(sreto[l].rearrange("h k v -> k h v"), S[l][:, :, :], ["S%d" % l], [("sreto", l)])
    DMA(ys[:, :], hres[:T, :], ["hres"], ["ys"])

    T = 128
    for s in range(NSEQ):
        for l in range(DEPTH):
            CP("pool", S[l][:, :, :], Sm[l][:, :, :], ["Sm%d" % l], ["S%d" % l])
            CP("pool", zT[l][:, :, 0:PB], zm[l][:, :, :], ["zm%d" % l], ["zT%d" % l])
        for it in range(NT):
            load_rope(T, NMETA + it * 128)
            load_tile(T, x[s, it * 128:(it + 1) * 128, :])
            for l in range(DEPTH):
                r0 = NMETA + it * 128
                kd = [(nk[l, s, r0:r0 + 128, :], ("nk", l, s, it))]
                vd = [(nv[l, s, r0:r0 + 128, :], ("nv", l, s, it))]
                past = []
                for pt in reversed(range(it)):
                    p0 = NMETA + pt * 128
                    past.append((nk[l, s, p0:p0 + 128, :], nv[l, s, p0:p0 + 128, :], 128, [("nk", l, s, pt), ("nv", l, s, pt)]))
                past.append((nk[l, s, 0:NMETA, :], nv[l, s, 0:NMETA, :], NMETA, [("nk", l, s, "m"), ("nv", l, s, "m")]))
                pd = [(npool[l, s, :, :], ("npool", l, s))] if it == NT - 1 else None
                layer(T, l, S[l], "S%d" % l, 264, zT[l], "zT%d" % l, False, kd, vd, past, pd)
            DMA(y[s, it * 128:(it + 1) * 128, :], hres[:T, :], ["hres"], [("y", s, it)])
        for l in range(DEPTH):
            DMA(nret[l, s].rearrange("h k v -> k h v"), S[l][:, :, :], ["S%d" % l], [("nret", l, s)])

    P.emit(nc, es)
    return nc, es


def make_consts(seq, dec, past):
    import ml_dtypes
    lg = lg_decay()
    cst = np.zeros((128, 640), np.float32)
    cst[:, 0:128] = np.eye(128, dtype=np.float32)
    m = np.arange(128)
    cst[:, 128:256] = (m[None, :] >= m[:, None]).astype(np.float32)
    for h in range(HR):
        cst[:, 256 + h] = np.exp(lg[h] * (m + 1.0))
        cst[:, 260 + h] = (DK ** -0.5) * np.exp(-lg[h] * (m + 1.0))
        cst[:, 264 + h] = np.exp(lg[h] * 128.0)
        cst[:, 268 + h] = np.exp(lg[h] * float(dec))
        cst[:, 272 + h] = np.exp(lg[h] * float(NMETA))
    cb = np.zeros((128, 512), np.float32)
    cb[:, 0:128] = np.eye(128)
    cb[:, 128:256] = (m[:, None] < m[None, :])
    cb[:, 256:384] = -(m[:, None] >= m[None, :]).astype(np.float32)
    cb[:, 384:512] = -1.0
    cstb = cb.astype(ml_dtypes.bfloat16)
    half = 64
    inv = (10000.0 ** (-np.arange(half, dtype=np.float32) / np.float32(half))).astype(np.float32)
    pos = np.concatenate([np.arange(NMETA), NMETA + np.arange(seq), NMETA + past + np.arange(dec)]).astype(np.float32)
    ang = (pos[:, None] * inv[None, :]).astype(np.float32)
    cos = np.cos(ang).astype(np.float32)
    sin = np.sin(ang).astype(np.float32)
    ropec = cos
    ropes = np.concatenate([-sin, sin], axis=1).astype(np.float32)
    rc = np.zeros((128, 8, NMETA), np.float32)
    t = np.arange(NMETA)
    for g, w in enumerate((2, 4, 8, 16)):
        rc[:, 2 * g:2 * g + 2, :] = (1.0 / np.minimum(t + 1, w))[None, None, :]
    return cst, cstb, ropec, ropes, rc.reshape(128, 8 * NMETA)


_CACHE = {}


def run(inputs, cfg_seq=None):
    inp = {k: np.asarray(v) for k, v in inputs.items()}
    B, SEQ, _ = inp["x_prompt"].shape
    DB, DEC, _ = inp["x_sample"].shape
    PAST = inp["cache_sb_k"].shape[2]
    assert B % NCORES == 0 and DB == NCORES
    NSEQ = B // NCORES
    cfg = Cfg(NSEQ, SEQ, PAST, DEC)
    nc, es = build(cfg)
    cst, cstb, ropec, ropes, rcnt = make_consts(SEQ, DEC, PAST)
    wf = [pack_layer(l, inp) for l in range(DEPTH)]
    in_maps = []
    for c in range(NCORES):
        in_maps.append({
            "x": np.ascontiguousarray(inp["x_prompt"][c * NSEQ:(c + 1) * NSEQ]),
            "xs": np.ascontiguousarray(inp["x_sample"][c]),
            "ck": np.ascontiguousarray(inp["cache_sb_k"][:, c].reshape(DEPTH, PAST, D)),
            "cv": np.ascontiguousarray(inp["cache_sb_v"][:, c].reshape(DEPTH, PAST, D)),
            "sret": np.ascontiguousarray(inp["state_ret"][:, c]),
            "spool": np.ascontiguousarray(inp["state_pool"][:, c]),
            "meta": inp["meta_tokens"],
            "wflat0": wf[0], "wflat1": wf[1],
            "lng": np.ascontiguousarray(inp["ln_g"].reshape(DEPTH * 3, D)),
            "lnb": np.ascontiguousarray(inp["ln_b"].reshape(DEPTH * 3, D)),
            "retg": np.ascontiguousarray(inp["ret_norm_g"].reshape(DEPTH, D)),
            "pscale": np.ascontiguousarray(inp["pool_scale"].reshape(DEPTH, D)),
            "ropec": ropec, "ropes": ropes, "cst": cst, "cstb": cstb, "rcnt": rcnt,
        })
    return nc, es, in_maps, cfg


def assemble(results, cfg):
    NSEQ, SEQ, DEC = cfg.nseq, cfg.seq, cfg.dec
    cat = lambda k, ax: np.concatenate([r[k] for r in results], axis=ax)
    y = cat("y", 0)
    ys = np.stack([r["ys"] for r in results], 0)
    nk = cat("nk", 1).reshape(DEPTH, NSEQ * NCORES, NMETA + SEQ, HS, DS)
    nv = cat("nv", 1).reshape(DEPTH, NSEQ * NCORES, NMETA + SEQ, HS, DS)
    nret = cat("nret", 1)
    npool = cat("npool", 1)
    sk = np.stack([r["sk"] for r in results], 1).reshape(DEPTH, NCORES, DEC, HS, DS)
    sv = np.stack([r["sv"] for r in results], 1).reshape(DEPTH, NCORES, DEC, HS, DS)
    sreto = np.stack([r["sreto"] for r in results], 1)
    spoolo = np.stack([r["spoolo"] for r in results], 1)
    return tuple(np.ascontiguousarray(a, dtype=np.float32) for a in (y, ys, nk, nv, nret, npool, sk, sv, sreto, spoolo))


def kernel(**inputs):
    nc, es, in_maps, cfg = run(inputs)
    with es:
        res = run_bass_kernel_spmd(nc, in_maps, core_ids=list(range(NCORES)))
    return assemble(res.results, cfg)
```

```python
import math
from contextlib import ExitStack
import numpy as np
import concourse.bass as bass
import concourse.mybir as mybir
from concourse.bass_utils import run_bass_kernel_spmd

F32 = mybir.dt.float32
BF16 = mybir.dt.bfloat16
AF = mybir.ActivationFunctionType
ALU = mybir.AluOpType

D = 1024
DFF = 2816
NMETA = 16
HR, DK, DV = 4, 128, 256
HS, DS = 8, 128
DEPTH = 2
PB = 15
ALPHA = (2 * DEPTH) ** 0.25
CRES = 0.5 / ALPHA
EPS_LN = 1e-5 / (ALPHA * ALPHA)
EPS_HN = 1e-5
NCORES = 8
NSLOT = 6
WMAX = 11 * 512
CASTW = 8192


def lg_decay():
    return np.log1p(-np.exp(np.linspace(math.log(1.0 / 32), math.log(1.0 / 512), HR, dtype=np.float32))).astype(np.float64)


def wtile_list():
    L = []
    for j in range(11):
        L.append(("up1", j, 8, 512))
    for nh in range(2):
        for kh in range(2):
            L.append(("dn1", (nh, kh), 11, 512))
    for nm in ("qr", "kr"):
        L.append((nm, 0, 8, 512))
    for nm in ("vr", "gr", "gt0", "wb0", "qs", "ks", "vs", "gt1", "wb1", "u"):
        L.append((nm, 0, 8, 512))
        L.append((nm, 1, 8, 512))
    L.append(("gt2", 0, 8, 512))
    L.append(("gt2", 1, 8, 512))
    L.append(("pm", 0, 8, 256))
    for nm in ("wb2", "wo"):
        L.append((nm, 0, 8, 512))
        L.append((nm, 1, 8, 512))
    for j in range(11):
        L.append(("up2", j, 8, 512))
    for nh in range(2):
        for kh in range(2):
            L.append(("dn2", (nh, kh), 11, 512))
    return L


WT = wtile_list()
WOFF = []
_o = 0
for _t in WT:
    WOFF.append(_o)
    _o += _t[2] * _t[3]
WTOT = _o
NCAST = (WTOT + CASTW - 1) // CASTW


def pack_tile(W, kc_n, cols):
    sub = W[:, cols]
    K = kc_n * 128
    assert sub.shape[0] == K
    return sub.reshape(kc_n, 128, len(cols)).transpose(1, 0, 2).reshape(128, kc_n * len(cols))


def pack_layer(l, inp):
    w_in = inp["w_in"][l]
    offs = np.cumsum([0, 512, 512, 1024, 1024, 1024, 1024, 1024, 1024, 3072])
    col0 = {"qr": offs[0], "kr": offs[1], "vr": offs[2], "gr": offs[3], "qs": offs[4], "ks": offs[5],
            "vs": offs[6], "u": offs[7], "gt0": offs[8], "gt1": offs[8] + 1024, "gt2": offs[8] + 2048}
    out = np.empty((128, WTOT), np.float32)
    for (nm, idx, kc_n, nc_), off in zip(WT, WOFF):
        if nm in ("up1", "up2"):
            W = inp["ffn1_up" if nm == "up1" else "ffn2_up"][l]
            cols = np.concatenate([np.arange(256 * idx, 256 * idx + 256), DFF + np.arange(256 * idx, 256 * idx + 256)])
            t = pack_tile(W, 8, cols)
        elif nm in ("dn1", "dn2"):
            W = inp["ffn1_down" if nm == "dn1" else "ffn2_down"][l]
            nh, kh = idx
            t = pack_tile(W[kh * 1408:(kh + 1) * 1408], 11, np.arange(nh * 512, nh * 512 + 512))
        elif nm in col0:
            t = pack_tile(w_in, 8, col0[nm] + np.arange(idx * 512, idx * 512 + 512))
        elif nm in ("wb0", "wb1", "wb2"):
            t = pack_tile(inp["w_branch"][l][int(nm[2])], 8, np.arange(idx * 512, idx * 512 + 512))
        elif nm == "wo":
            t = pack_tile(inp["w_out"][l], 8, np.arange(idx * 512, idx * 512 + 512))
        elif nm == "pm":
            pmw = inp["pool_mix_w"][l]
            t = pmw.reshape(4, 2, 128, 256).transpose(2, 0, 1, 3).reshape(128, 8 * 256)
        else:
            raise KeyError(nm)
        out[:, off:off + kc_n * nc_] = t
    return out


class Prog:
    def __init__(self):
        self.ins = []
        self.res = {}

    def add(self, eng, fn, reads=(), writes=(), dma=None):
        idx = len(self.ins)
        deps = set()
        for k in reads:
            st = self.res.setdefault(k, [None, {}])
            if st[0] is not None:
                deps.add(st[0])
        for k in writes:
            st = self.res.setdefault(k, [None, {}])
            if st[0] is not None:
                deps.add(st[0])
            deps.update(st[1].values())
        rkey = ("dma", idx) if dma else eng
        for k in reads:
            self.res[k][1][rkey] = idx
        for k in writes:
            self.res[k] = [idx, {}]
        deps.discard(idx)
        if eng == "pe":
            deps = {d for d in deps if self.ins[d]["eng"] != "pe" or self.ins[d]["dma"]}
        self.ins.append(dict(eng=eng, fn=fn, deps=sorted(deps), dma=dma))
        return idx

    def emit(self, nc, es):
        n = len(self.ins)
        flagged = [False] * n
        for it in self.ins:
            for d in it["deps"]:
                flagged[d] = True
        EPOCH = 12000
        NHW, NSW = 12, 6
        sems = {}

        def getsem(name):
            if name not in sems:
                sems[name] = es.enter_context(nc.semaphore(name))
            return sems[name]

        tok = [None] * n
        cnt = {}
        dcount = {}
        rr = {"hw": 0, "sw": 0}
        prevdma = [None] * n
        lastdma = {}
        for i, it in enumerate(self.ins):
            if it["dma"]:
                kind = it["dma"]
                nm = "d%s%d" % (kind, rr[kind] % (NHW if kind == "hw" else NSW))
                rr[kind] += 1
                dcount[nm] = dcount.get(nm, 0) + 1
                tok[i] = (nm, 16 * dcount[nm])
                prevdma[i] = lastdma.get(nm)
                lastdma[nm] = i
            elif flagged[i]:
                e = it["eng"]
                c = cnt.get(e, 0)
                cnt[e] = c + 1
                tok[i] = ("c%s%d" % (e, c // EPOCH), (c % EPOCH) + 1)
        byeng = {}
        for i, it in enumerate(self.ins):
            byeng.setdefault(it["eng"], []).append(i)
        final_waits = dict((nm, 16 * c) for nm, c in dcount.items())

        def run(eng_name, e):
            waited = {}
            for i in byeng.get(eng_name, []):
                it = self.ins[i]
                deps = list(it["deps"])
                if prevdma[i] is not None:
                    deps.append(prevdma[i])
                need = {}
                for d in deps:
                    nm, v = tok[d]
                    if waited.get(nm, 0) >= v:
                        continue
                    need[nm] = max(need.get(nm, 0), v)
                need = list(need.items())
                for nm, v in need[:-1]:
                    e.wait_ge(getsem(nm), v)
                    waited[nm] = v
                bi = it["fn"](e)
                if need:
                    nm, v = need[-1]
                    bi._wait_ge(getsem(nm), v)
                    waited[nm] = v
                if tok[i] is not None:
                    nm, v = tok[i]
                    bi.then_inc(getsem(nm), 16 if it["dma"] else 1)
            if eng_name == "sp":
                for nm, v in final_waits.items():
                    e.wait_ge(getsem(nm), v)

        for t in tok:
            if t is not None:
                getsem(t[0])
        with nc.Block() as block:
            @block.sync
            def _(e):
                run("sp", e)

            @block.scalar
            def _(e):
                run("act", e)

            @block.vector
            def _(e):
                run("dve", e)

            @block.gpsimd
            def _(e):
                run("pool", e)

            @block.tensor
            def _(e):
                run("pe", e)


class Cfg:
    def __init__(self, nseq, seq, past, dec_seq=32):
        self.nseq, self.seq, self.past, self.dec = nseq, seq, past, dec_seq


def build(cfg):
    nc = bass.Bass("TRN2", target_bir_lowering=False)
    P = Prog()
    es = ExitStack()
    NSEQ, SEQ, PAST, DEC = cfg.nseq, cfg.seq, cfg.past, cfg.dec
    NT = SEQ // 128

    def din(name, shape, dt=F32):
        return nc.dram_tensor(name, list(shape), dt, kind="ExternalInput").ap()

    def dout(name, shape, dt=F32):
        return nc.dram_tensor(name, list(shape), dt, kind="ExternalOutput").ap()

    x = din("x", [NSEQ, SEQ, D])
    xs = din("xs", [DEC, D])
    ck = din("ck", [DEPTH, PAST, D])
    cv = din("cv", [DEPTH, PAST, D])
    sret = din("sret", [DEPTH, HR, DK, DV])
    spool = din("spool", [DEPTH, PB, D])
    meta = din("meta", [NMETA, D])
    wfl = [din("wflat%d" % l, [128, WTOT]) for l in range(DEPTH)]
    lng = din("lng", [DEPTH * 3, D])
    lnb = din("lnb", [DEPTH * 3, D])
    retg = din("retg", [DEPTH, D])
    pscale = din("pscale", [DEPTH, D])
    ropec = din("ropec", [NMETA + SEQ + DEC, 64])
    ropes = din("ropes", [NMETA + SEQ + DEC, 128])
    cst = din("cst", [128, 640])
    cstb = din("cstb", [128, 512], BF16)
    rcnt = din("rcnt", [128, 8 * NMETA])

    y = dout("y", [NSEQ, SEQ, D])
    ys = dout("ys", [DEC, D])
    nk = dout("nk", [DEPTH, NSEQ, NMETA + SEQ, D])
    nv = dout("nv", [DEPTH, NSEQ, NMETA + SEQ, D])
    nret = dout("nret", [DEPTH, NSEQ, HR, DK, DV])
    npool = dout("npool", [DEPTH, NSEQ, PB, D])
    sk = dout("sk", [DEPTH, DEC, D])
    sv = dout("sv", [DEPTH, DEC, D])
    sreto = dout("sreto", [DEPTH, HR, DK, DV])
    spoolo = dout("spoolo", [DEPTH, PB, D])
    wbf = [nc.dram_tensor("wbf%d" % l, [128, WTOT], BF16, kind="Internal").ap() for l in range(DEPTH)]

    def sb(name, shape, dt=F32):
        return es.enter_context(nc.sbuf_tensor(name, list(shape), dt))

    def ps(name, shape, dt=F32):
        return es.enter_context(nc.psum_tensor(name, list(shape), dt))

    hres = sb("hres", [128, D])
    hT = sb("hT", [128, 8, 128], BF16)
    wsl = [sb("wsl%d" % i, [128, WMAX], BF16) for i in range(NSLOT)]
    S = [sb("S%d" % l, [128, HR, DV]) for l in range(DEPTH)]
    Sm = [sb("Sm%d" % l, [128, HR, DV]) for l in range(DEPTH)]
    Sbf = sb("Sbf", [128, HR, DV], BF16)
    zT = [sb("zT%d" % l, [128, 8, PB + 128]) for l in range(DEPTH)]
    zm = [sb("zm%d" % l, [128, 8, PB]) for l in range(DEPTH)]
    C = sb("C", [128, 640])
    CB = sb("CB", [128, 512], BF16)
    RC = sb("RC", [128, 8, NMETA])
    lnG = sb("lnG", [128, D])
    lnB = sb("lnB", [128, D])
    rc = sb("rc", [128, 64])
    rs = sb("rs", [128, 128])
    hb = sb("hb", [128, D], BF16)
    act_tok = sb("act_tok", [128, DFF], BF16)
    actT = sb("actT", [128, 22, 128], BF16)
    stmp = sb("stmp", [128, 256], BF16)
    f1 = sb("f1", [128, D])
    f2 = sb("f2", [128, D])
    mix = sb("mix", [128, D])
    st6 = sb("st6", [128, 8, 6])
    mv = sb("mv", [128, 4, 2])
    sc1 = sb("sc1", [128, 4])
    sc2 = sb("sc2", [128, 4])
    qt = sb("qt", [128, 512], BF16)
    kt = sb("kt", [128, 512], BF16)
    qkT = sb("qkT", [128, 8, 128], BF16)
    vr = sb("vr", [128, D], BF16)
    gr = sb("gr", [128, D], BF16)
    th = sb("th", [128, D], BF16)
    attm = sb("attm", [128, 4, 128], BF16)
    oT = sb("oT", [128, 8, 128], BF16)
    qsT = sb("qsT", [128, 8, 128], BF16)
    kb_own = sb("kb_own", [128, D], BF16)
    kT_own = sb("kT_own", [128, 8, 128], BF16)
    v_own = sb("v_own", [128, D], BF16)
    kpb = [sb("kpb%d" % i, [128, D], BF16) for i in range(2)]
    vpb = [sb("vpb%d" % i, [128, D], BF16) for i in range(2)]
    kTb = [sb("kTb%d" % i, [128, 8, 128], BF16) for i in range(2)]
    eb = [sb("eb%d" % i, [128, 4, 128]) for i in range(2)]
    spb = [sb("spb%d" % i, [128, 4, 128], BF16) for i in range(2)]
    wb_ = [sb("wb%d" % i, [128, 4, 128], BF16) for i in range(2)]
    ltc = [sb("ltc%d" % i, [128, 4, 128]) for i in range(2)]
    ltcb = [sb("ltcb%d" % i, [128, 4, 128], BF16) for i in range(2)]
    pooledT = sb("pooledT", [128, 8, 128], BF16)
    ptmp = sb("ptmp", [128, 2, PB + 128])
    ptmp2 = sb("ptmp2", [128, 2, PB + 128])

    pf = [ps("pf%d" % i, [128, 512]) for i in range(6)]
    pb = [ps("pb%d" % i, [128, 1024], BF16) for i in range(2)]
    pfi = [0]
    pbi = [0]

    def nf():
        i = pfi[0] % 4
        pfi[0] += 1
        return i

    def nb():
        i = pbi[0] % 2
        pbi[0] += 1
        return i

    identf = C[:, 0:128]
    cmask = C[:, 128:256]
    identb = CB[:, 0:128]
    dmask = CB[:, 128:256]
    ntri = CB[:, 256:384]
    nones = CB[:, 384:512]

    def DMA(out, in_, reads, writes, eng="sp"):
        kind = "sw" if eng == "pool" else "hw"
        P.add(eng, lambda e: e.dma_start(out=out, in_=in_), reads, writes, dma=kind)

    def MM(out, lhsT, rhs, start, stop, reads, writes):
        P.add("pe", lambda e: e.matmul(out, lhsT=lhsT, rhs=rhs, start=start, stop=stop), reads, writes)

    def TR(out, in_, np_, reads, writes):
        P.add("pe", lambda e: e.transpose(out, in_, identb[:np_, :np_]), list(reads) + ["CB"], writes)

    def ACTF(out, in_, func, reads, writes, scale=1.0, bias=0.0):
        P.add("act", lambda e: e.activation(out=out, in_=in_, func=func, bias=bias, scale=scale), reads, writes)

    def TT(eng, out, in0, in1, op, reads, writes):
        P.add(eng, lambda e: e.tensor_tensor(out=out, in0=in0, in1=in1, op=op), reads, writes)

    def STT(eng, out, in0, scalar, in1, op0, op1, reads, writes):
        P.add(eng, lambda e: e.scalar_tensor_tensor(out=out, in0=in0, scalar=scalar, in1=in1, op0=op0, op1=op1), reads, writes)

    def TS(eng, out, in0, s1, s2, op0, op1, reads, writes):
        if s2 is None:
            P.add(eng, lambda e: e.tensor_scalar(out=out, in0=in0, scalar1=s1, scalar2=None, op0=op0), reads, writes)
        else:
            P.add(eng, lambda e: e.tensor_scalar(out=out, in0=in0, scalar1=s1, scalar2=s2, op0=op0, op1=op1), reads, writes)

    def CP(eng, out, in_, reads, writes):
        if eng == "act":
            P.add("act", lambda e: e.activation(out=out, in_=in_, func=AF.Copy), reads, writes)
        else:
            P.add(eng, lambda e: e.tensor_copy(out=out, in_=in_), reads, writes)

    def MSET(eng, ap, val, writes):
        P.add(eng, lambda e: e.memset(ap, val), (), writes)

    MSET("pool", ptmp[:, :, :], 0.0, ["ptmp"])
    MSET("pool", ptmp2[:, :, :], 0.0, ["ptmp2"])
    DMA(C[:, :], cst[:, :], (), ["C"])
    DMA(CB[:, :], cstb[:, :], (), ["CB"])
    DMA(RC[:, :, :], rcnt.rearrange("p (c t) -> p c t", c=8), (), ["RC"])
    for l in range(DEPTH):
        for c in range(NCAST):
            c0, c1 = c * CASTW, min(WTOT, (c + 1) * CASTW)
            P.add("pool", (lambda e, l=l, c0=c0, c1=c1: e.dma_start(out=wbf[l][:, c0:c1], in_=wfl[l][:, c0:c1],
                                                                   max_dma_last_dim=4096)),
                  (), [("wbf", l, c)], dma="sw")

    ws = dict(n=0, seq=[])

    def w_issue(k):
        l, ti = ws["seq"][k]
        nm, idx, kc_n, ncol = WT[ti]
        off = WOFF[ti]
        width = kc_n * ncol
        slot = k % NSLOT
        cs = range(off // CASTW, (off + width - 1) // CASTW + 1)
        DMA(wsl[slot][:, 0:width], wbf[l][:, off:off + width], [("wbf", l, c) for c in cs], ["wsl%d" % slot])

    def w_start(seqlist):
        ws["seq"] = seqlist
        ws["n"] = 0
        for k in range(min(NSLOT, len(seqlist))):
            w_issue(k)

    def w_get(expect):
        k = ws["n"]
        l, ti = ws["seq"][k]
        nm, idx, kc_n, ncol = WT[ti]
        assert nm == expect, (nm, expect)
        slot = k % NSLOT
        view = wsl[slot][:, 0:kc_n * ncol].rearrange("p (k n) -> p k n", k=kc_n)
        return view, "wsl%d" % slot

    def w_done():
        k = ws["n"]
        ws["n"] = k + 1
        if k + NSLOT < len(ws["seq"]):
            w_issue(k + NSLOT)

    def transpose_to(dstT, src_tok, T, nchunk, srckey, dstkey, col0=0):
        c = 0
        while c < nchunk:
            n = min(8, nchunk - c)
            b = nb()
            for j in range(n):
                TR(pb[b][:, j * 128:j * 128 + T], src_tok[:T, col0 + (c + j) * 128: col0 + (c + j + 1) * 128], T,
                   [srckey], ["pb%d" % b])
            CP("act", dstT[:, c:c + n, :T], pb[b][:, 0:n * 128].rearrange("p (c t) -> p c t", c=n)[:, :, :T],
               ["pb%d" % b], [dstkey])
            c += n

    def proj_tok(T, inT, inkey, wname, nparts, evac):
        for part in range(nparts):
            w, wkey = w_get(wname)
            kc_n = w.shape[1]
            ncol = w.shape[2]
            b = nf()
            for kc in range(kc_n):
                MM(pf[b][:T, 0:ncol], inT[:, kc, :T], w[:, kc, :], kc == 0, kc == kc_n - 1, [inkey, wkey], ["pf%d" % b])
            w_done()
            evac(part, pf[b], "pf%d" % b)

    def layer_norm(T, l, which):
        row = l * 3 + which
        DMA(lnG[:T, :], lng[row:row + 1, :].to_broadcast([T, D]), (), ["lnG"])
        DMA(lnB[:T, :], lnb[row:row + 1, :].to_broadcast([T, D]), (), ["lnB"])
        for c in range(2):
            P.add("dve", lambda e, c=c: e.bn_stats(out=st6[:T, c, :], in_=hres[:T, c * 512:(c + 1) * 512]), ["hres"], ["st6"])
        P.add("dve", lambda e: e.bn_aggr(out=mv[:T, 0, :], in_=st6[:T, 0:2, :]), ["st6"], ["mv"])
        rstd_from_var(T, 1, EPS_LN)
        ACTF(f1[:T, :], hres[:T, :], AF.Identity, ["hres", "sc1", "sc2"], ["f1"], scale=sc1[:T, 0:1], bias=sc2[:T, 0:1])
        TT("dve", f1[:T, :], f1[:T, :], lnG[:T, :], ALU.mult, ["f1", "lnG"], ["f1"])
        TT("dve", hres[:T, :], f1[:T, :], lnB[:T, :], ALU.add, ["f1", "lnB"], ["hres"])
        CP("dve", hb[:T, :], hres[:T, :], ["hres"], ["hb"])
        transpose_to(hT, hb, T, 8, "hb", "hT")

    def rstd_from_var(T, n, eps):
        ACTF(sc1[:T, 0:n], mv[:T, 0:n, 1], AF.Ln, ["mv"], ["sc1"], bias=eps)
        ACTF(sc1[:T, 0:n], sc1[:T, 0:n], AF.Exp, ["sc1"], ["sc1"], scale=-0.5)
        STT("dve", sc2[:T, 0:n], mv[:T, 0:n, 0], -1.0, sc1[:T, 0:n], ALU.mult, ALU.mult, ["mv", "sc1"], ["sc2"])

    def ffn(T, l, which):
        upn, dnn = ("up1", "dn1") if which == 0 else ("up2", "dn2")
        for j in range(11):
            w, wkey = w_get(upn)
            b = nf()
            for kc in range(8):
                MM(pf[b][:T, :], hT[:, kc, :T], w[:, kc, :], kc == 0, kc == 7, ["hT", wkey], ["pf%d" % b])
            w_done()
            ACTF(stmp[:T, :], pf[b][:T, 0:256], AF.Silu, ["pf%d" % b], ["stmp"])
            TT("dve", act_tok[:T, j * 256:(j + 1) * 256], pf[b][:T, 256:512], stmp[:T, :], ALU.mult,
               ["pf%d" % b, "stmp"], ["act_tok"])
        transpose_to(actT, act_tok, T, 22, "act_tok", "actT")
        for nh in range(2):
            b = nf()
            for kh in range(2):
                w, wkey = w_get(dnn)
                for kc in range(11):
                    MM(pf[b][:T, :], actT[:, kh * 11 + kc, :T], w[:, kc, :], kh == 0 and kc == 0, kh == 1 and kc == 10,
                       ["actT", wkey], ["pf%d" % b])
                w_done()
            STT("dve", hres[:T, nh * 512:(nh + 1) * 512], pf[b][:T, :], CRES, hres[:T, nh * 512:(nh + 1) * 512],
                ALU.mult, ALU.add, ["pf%d" % b, "hres"], ["hres"])
        layer_norm(T, l, 0 if which == 0 else 2)

    def gate_mix(T, bi, first):
        def ev(part, pbank, pkey):
            sl = slice(part * 512, (part + 1) * 512)
            if first:
                STT("dve", mix[:T, sl], th[:T, sl], 1.0, pbank[:T, :], ALU.add, ALU.mult, ["th", pkey], ["mix"])
            else:
                STT("dve", f2[:T, sl], th[:T, sl], 1.0, pbank[:T, :], ALU.add, ALU.mult, ["th", pkey], ["f2"])
                TT("pool", mix[:T, sl], mix[:T, sl], f2[:T, sl], ALU.add, ["mix", "f2"], ["mix"])
        proj_tok(T, oT, "oT", "wb%d" % bi, 2, ev)

    def gates(T, bi):
        def ev(part, pbank, pkey):
            ACTF(th[:T, part * 512:(part + 1) * 512], pbank[:T, :], AF.Tanh, [pkey], ["th"], scale=0.5)
        proj_tok(T, hT, "hT", "gt%d" % bi, 2, ev)

    def rotary(T, dst, dkey, pbank, pkey, sccol):
        xv = pbank[:T, :].rearrange("p (h t d) -> p h t d", h=4, t=2)
        cosb = rc[:T, :].unsqueeze(1).unsqueeze(1).to_broadcast([T, 4, 2, 64])
        TT("dve", f1[:T, 0:512].rearrange("p (h t d) -> p h t d", h=4, t=2), xv, cosb, ALU.mult, [pkey, "rc"], ["f1"])
        f2v = f2[:T, 0:512].rearrange("p (h t d) -> p h t d", h=4, t=2)
        for t in range(2):
            sinb = rs[:T, t * 64:(t + 1) * 64].unsqueeze(1).to_broadcast([T, 4, 64])
            TT("dve", f2v[:, :, t, :], xv[:, :, 1 - t, :], sinb, ALU.mult, [pkey, "rs"], ["f2"])
        TT("pool", f1[:T, 0:512], f1[:T, 0:512], f2[:T, 0:512], ALU.add, ["f1", "f2"], ["f1"])
        scb = C[:T, sccol:sccol + 4].unsqueeze(2).to_broadcast([T, 4, 128])
        TT("pool", dst[:T, :].rearrange("p (h d) -> p h d", h=4), f1[:T, 0:512].rearrange("p (h d) -> p h d", h=4), scb,
           ALU.mult, ["f1", "C"], [dkey])

    def retention(T, l, Sl, Skey, dccol):
        qk_keys = {}

        def ev_q(part, pbank, pkey):
            rotary(T, qt, "qt", pbank, pkey, 256)

        def ev_k(part, pbank, pkey):
            rotary(T, kt, "kt", pbank, pkey, 260)

        proj_tok(T, hT, "hT", "qr", 1, ev_q)
        proj_tok(T, hT, "hT", "kr", 1, ev_k)

        def ev_v(part, pbank, pkey):
            CP("act", vr[:T, part * 512:(part + 1) * 512], pbank[:T, :], [pkey], ["vr"])
        proj_tok(T, hT, "hT", "vr", 2, ev_v)

        def ev_g(part, pbank, pkey):
            ACTF(gr[:T, part * 512:(part + 1) * 512], pbank[:T, :], AF.Silu, [pkey], ["gr"])
        proj_tok(T, hT, "hT", "gr", 2, ev_g)
        gates(T, 0)
        b = nb()
        for h in range(4):
            TR(pb[b][:, h * 128:h * 128 + T], qt[:T, h * 128:(h + 1) * 128], T, ["qt"], ["pb%d" % b])
        for h in range(4):
            TR(pb[b][:, (4 + h) * 128:(4 + h) * 128 + T], kt[:T, h * 128:(h + 1) * 128], T, ["kt"], ["pb%d" % b])
        CP("act", qkT[:, :, :T], pb[b][:, :].rearrange("p (c t) -> p c t", c=8)[:, :, :T], ["pb%d" % b], ["qkT"])
        CP("dve", Sbf[:, :, :], Sl[:, :, :], [Skey], ["Sbf"])
        b = nf()
        for h in range(4):
            MM(pf[b][:T, h * 128:h * 128 + T], qkT[:, 4 + h, :T], qkT[:, h, :T], h == 0, h == 3, ["qkT"], ["pf%d" % b])
        TT("dve", attm[:T, :, :T], pf[b][:T, :].rearrange("p (h t) -> p h t", h=4)[:, :, :T],
           cmask[:T, :T].unsqueeze(1).to_broadcast([T, 4, T]), ALU.mult, ["pf%d" % b, "C"], ["attm"])
        ob = [nf(), nf()]
        for h in range(4):
            bb = ob[h // 2]
            osl = pf[bb][:T, (h % 2) * 256:(h % 2) * 256 + 256]
            MM(osl, attm[:T, h, :T], vr[:T, h * 256:(h + 1) * 256], h % 2 == 0, False, ["attm", "vr"], ["pf%d" % bb])
            MM(osl, qkT[:, h, :T], Sbf[:, h, :], False, h % 2 == 1, ["qkT", "Sbf"], ["pf%d" % bb])
        for i in range(2):
            CP("act", f1[:T, i * 512:(i + 1) * 512], pf[ob[i]][:T, :], ["pf%d" % ob[i]], ["f1"])
        sbk = [nf(), nf()]
        for h in range(4):
            bb = sbk[h // 2]
            MM(pf[bb][:, (h % 2) * 256:(h % 2) * 256 + 256], kt[:T, h * 128:(h + 1) * 128], vr[:T, h * 256:(h + 1) * 256],
               h % 2 == 0, h % 2 == 1, ["kt", "vr"], ["pf%d" % bb])
        for i in range(2):
            TT("dve", Sl[:, 2 * i:2 * i + 2, :], pf[sbk[i]][:, :].rearrange("p (h v) -> p h v", h=2), Sl[:, 2 * i:2 * i + 2, :],
               ALU.add, ["pf%d" % sbk[i], Skey], [Skey])
        TT("dve", Sl[:, :, :], Sl[:, :, :], C[:, dccol:dccol + 4].unsqueeze(2).to_broadcast([128, 4, DV]), ALU.mult,
           [Skey, "C"], [Skey])
        for h in range(4):
            P.add("dve", lambda e, h=h: e.bn_stats(out=st6[:T, h, :], in_=f1[:T, h * 256:(h + 1) * 256]), ["f1"], ["st6"])
        for h in range(4):
            P.add("dve", lambda e, h=h: e.bn_aggr(out=mv[:T, h, :], in_=st6[:T, h:h + 1, :]), ["st6"], ["mv"])
        rstd_from_var(T, 4, EPS_HN)
        for h in range(4):
            ACTF(f2[:T, h * 256:(h + 1) * 256], f1[:T, h * 256:(h + 1) * 256], AF.Identity, ["f1", "sc1", "sc2"], ["f2"],
                 scale=sc1[:T, h:h + 1], bias=sc2[:T, h:h + 1])
        DMA(lnG[:T, :], retg[l:l + 1, :].to_broadcast([T, D]), (), ["lnG"])
        TT("dve", f2[:T, :], f2[:T, :], lnG[:T, :], ALU.mult, ["f2", "lnG"], ["f2"])
        TT("dve", hb[:T, :], f2[:T, :], gr[:T, :], ALU.mult, ["f2", "gr"], ["hb"])
        transpose_to(oT, hb, T, 8, "hb", "oT")
        gate_mix(T, 0, True)

    def stick(T, l, kdst, vdst, past):
        def ev_q(part, pbank, pkey):
            ACTF(hb[:T, part * 512:(part + 1) * 512], pbank[:T, :], AF.Copy, [pkey], ["hb"], scale=DS ** -0.5)
        proj_tok(T, hT, "hT", "qs", 2, ev_q)
        transpose_to(qsT, hb, T, 8, "hb", "qsT")

        def ev_k(part, pbank, pkey):
            CP("act", f1[:T, part * 512:(part + 1) * 512], pbank[:T, :], [pkey], ["f1"])
        proj_tok(T, hT, "hT", "ks", 2, ev_k)
        for (ap, wk) in kdst:
            DMA(ap, f1[:T, :], ["f1"], [wk])
        CP("dve", kb_own[:T, :], f1[:T, :], ["f1"], ["kb_own"])
        transpose_to(kT_own, kb_own, T, 8, "kb_own", "kT_own")

        def ev_v(part, pbank, pkey):
            CP("act", f2[:T, part * 512:(part + 1) * 512], pbank[:T, :], [pkey], ["f2"])
        proj_tok(T, hT, "hT", "vs", 2, ev_v)
        for (ap, wk) in vdst:
            DMA(ap, f2[:T, :], ["f2"], [wk])
        CP("dve", v_own[:T, :], f2[:T, :], ["f2"], ["v_own"])
        gates(T, 1)
        blocks = [("own", None, None, T, None)] + [("past",) + p for p in past]
        for hg in range(2):
            MSET("pool", ltc[hg][:, :, :], 0.0, ["ltc%d" % hg])
        ob = [4, 5]
        nblk = len(blocks)

        def load_dma(bi_):
            blk = blocks[bi_]
            nkk = blk[3]
            s = bi_ % 2
            DMA(kpb[s][:nkk, :], blk[1], blk[4], ["kpb%d" % s], eng="pool")
            DMA(vpb[s][:nkk, :], blk[2], blk[4], ["vpb%d" % s], eng="pool")

        def load_tr(bi_):
            blk = blocks[bi_]
            s = bi_ % 2
            transpose_to(kTb[s], kpb[s], blk[3], 8, "kpb%d" % s, "kTb%d" % s)

        if nblk > 1:
            load_dma(1)
            load_tr(1)
        for bi_, blk in enumerate(blocks):
            kind = blk[0]
            nkk = blk[3]
            if kind == "own":
                kTs, kTkey, vs_, vkey = kT_own, "kT_own", v_own, "v_own"
            else:
                s = bi_ % 2
                kTs, kTkey, vs_, vkey = kTb[s], "kTb%d" % s, vpb[s], "vpb%d" % s
            if bi_ >= 1 and bi_ + 1 < nblk:
                load_dma(bi_ + 1)
            HG = range(2)
            mk = dmask[:nkk, :T].unsqueeze(1).to_broadcast([nkk, 4, T])
            zb = [nf(), nf()]
            for hg in HG:
                for hh in range(4):
                    h = hg * 4 + hh
                    MM(pf[zb[hg]][:nkk, hh * 128:hh * 128 + T], kTs[:, h, :nkk], qsT[:, h, :T], hh == 0, hh == 3,
                       [kTkey, "qsT"], ["pf%d" % zb[hg]])
            for hg in HG:
                zv = pf[zb[hg]][:nkk, :].rearrange("p (h t) -> p h t", h=4)[:, :, :T]
                ACTF(eb[hg][:nkk, :, :T], zv, AF.Exp, ["pf%d" % zb[hg]], ["eb%d" % hg])
            for hg in HG:
                ACTF(spb[hg][:nkk, :, :T], eb[hg][:nkk, :, :T], AF.Ln, ["eb%d" % hg], ["spb%d" % hg], bias=1.0)
            if kind == "own":
                for hg in HG:
                    TT("pool", spb[hg][:nkk, :, :T], spb[hg][:nkk, :, :T], mk, ALU.mult, ["spb%d" % hg, "CB"], ["spb%d" % hg])
            bb = [nf(), nf()]
            for hg in HG:
                for hh in range(4):
                    h = hg * 4 + hh
                    osl = pf[bb[hg]][:nkk, hh * 128:hh * 128 + T]
                    MM(osl, kTs[:, h, :nkk], qsT[:, h, :T], hh == 0, False, [kTkey, "qsT"], ["pf%d" % bb[hg]])
                    MM(osl, ntri[:nkk, :nkk], spb[hg][:nkk, hh, :T], False, bi_ == 0 and hh == 3, ["CB", "spb%d" % hg],
                       ["pf%d" % bb[hg]])
                    if bi_ > 0:
                        MM(osl, nones[:, :nkk], ltcb[hg][:, hh, :T], False, hh == 3, ["CB", "ltcb%d" % hg], ["pf%d" % bb[hg]])
            for hg in HG:
                bv = pf[bb[hg]][:nkk, :].rearrange("p (h t) -> p h t", h=4)[:, :, :T]
                ACTF(wb_[hg][:nkk, :, :T], bv, AF.Exp, ["pf%d" % bb[hg]], ["wb%d" % hg])
            if kind == "own":
                for hg in HG:
                    TT("pool", wb_[hg][:nkk, :, :T], wb_[hg][:nkk, :, :T], mk, ALU.mult, ["wb%d" % hg, "CB"], ["wb%d" % hg])
            for hg in HG:
                for hh in range(4):
                    h = hg * 4 + hh
                    MM(pf[ob[hg]][:T, hh * 128:(hh + 1) * 128], wb_[hg][:nkk, hh, :T], vs_[:nkk, h * 128:(h + 1) * 128],
                       bi_ == 0 and hh == 0, bi_ == nblk - 1 and hh == 3, ["wb%d" % hg, vkey], ["pf%d" % ob[hg]])
            if bi_ < nblk - 1:
                for hg in HG:
                    TT("dve", ltc[hg][:nkk, :, :T], ltc[hg][:nkk, :, :T], spb[hg][:nkk, :, :T], ALU.add,
                       ["ltc%d" % hg, "spb%d" % hg], ["ltc%d" % hg])
                    CP("dve", ltcb[hg][:, :, :T], ltc[hg][:, :, :T], ["ltc%d" % hg], ["ltcb%d" % hg])
            if bi_ >= 1 and bi_ + 1 < nblk:
                load_tr(bi_ + 1)
        for hg in range(2):
            CP("act", hb[:T, hg * 512:(hg + 1) * 512], pf[ob[hg]][:T, :], ["pf%d" % ob[hg]], ["hb"])
        transpose_to(oT, hb, T, 8, "hb", "oT")
        gate_mix(T, 1, False)

    def pool_mixer(T, l, zl, zkey, is_meta, pooldst):
        for part in range(2):
            w, wkey = w_get("u")
            for oc in range(4):
                b = nf()
                for kc in range(8):
                    MM(pf[b][:, :T], w[:, kc, oc * 128:(oc + 1) * 128], hT[:, kc, :T], kc == 0, kc == 7, ["hT", wkey], ["pf%d" % b])
                CP("act", zl[:, part * 4 + oc, PB:PB + T], pf[b][:, :T], ["pf%d" % b], [zkey])
            w_done()
        gates(T, 2)
        W_ = PB + T
        for g, wnd in enumerate((2, 4, 8, 16)):
            cs = slice(2 * g, 2 * g + 2)
            cur = zl[:, cs, :]
            sh = 1
            bufs = [ptmp, ptmp2]
            bi_ = 0
            curkey = zkey
            while sh < wnd:
                dst = bufs[bi_ % 2]
                dkey = "ptmp" if bi_ % 2 == 0 else "ptmp2"
                TT("pool", dst[:, :, sh:W_], cur[:, :, sh:W_], cur[:, :, 0:W_ - sh], ALU.add, [curkey], [dkey])
                cur, curkey = dst, dkey
                sh *= 2
                bi_ += 1
            if is_meta:
                TT("dve", f1[:, 0:2 * T].rearrange("p (c t) -> p c t", c=2), cur[:, :, PB:PB + T], RC[:, cs, :T], ALU.mult,
                   [curkey, "RC"], ["f1"])
                TT("dve", pooledT[:, cs, :T], f1[:, 0:2 * T].rearrange("p (c t) -> p c t", c=2), zl[:, cs, PB:PB + T],
                   ALU.subtract, ["f1", zkey], ["pooledT"])
            else:
                STT("dve", pooledT[:, cs, :T], cur[:, :, PB:PB + T], 1.0 / wnd, zl[:, cs, PB:PB + T], ALU.mult, ALU.subtract,
                    [curkey, zkey], ["pooledT"])
        w, wkey = w_get("pm")
        ob = [nf(), nf()]
        for g in range(4):
            bb = ob[g // 2]
            for cc in range(2):
                MM(pf[bb][:T, (g % 2) * 256:(g % 2) * 256 + 256], pooledT[:, g * 2 + cc, :T], w[:, g * 2 + cc, :],
                   g % 2 == 0 and cc == 0, g % 2 == 1 and cc == 1, ["pooledT", wkey], ["pf%d" % bb])
        w_done()
        DMA(lnG[:T, :], pscale[l:l + 1, :].to_broadcast([T, D]), (), ["lnG"])
        for i in range(2):
            TT("dve", hb[:T, i * 512:(i + 1) * 512], pf[ob[i]][:T, :], lnG[:T, i * 512:(i + 1) * 512], ALU.mult,
               ["pf%d" % ob[i], "lnG"], ["hb"])
        transpose_to(oT, hb, T, 8, "hb", "oT")
        gate_mix(T, 2, False)
        if pooldst is not None:
            src0 = PB + T - 128 if T >= 128 else None
            for half in range(2):
                b = nf()
                for j in range(4):
                    c = half * 4 + j
                    if T >= 128:
                        P.add("pe", lambda e, b=b, j=j, c=c: e.transpose(pf[b][:, j * 128:(j + 1) * 128], zl[:, c, PB + T - 128:PB + T], identf),
                              [zkey, "C"], ["pf%d" % b])
                    else:
                        P.add("pe", lambda e, b=b, j=j, c=c: e.transpose(pf[b][:PB + T, j * 128:(j + 1) * 128], zl[:, c, 0:PB + T], identf),
                              [zkey, "C"], ["pf%d" % b])
                rows = 128 if T >= 128 else PB + T
                CP("act", f1[:rows, half * 512:(half + 1) * 512], pf[b][:rows, :], ["pf%d" % b], ["f1"])
            rows = 128 if T >= 128 else PB + T
            for (ap, wk) in pooldst:
                DMA(ap, f1[rows - PB:rows, :], ["f1"], [wk])
        CP("pool", zl[:, :, 0:PB], zl[:, :, T:T + PB], [zkey], [zkey])

    def out_proj(T, l):
        CP("act", hb[:T, :], mix[:T, :], ["mix"], ["hb"])
        transpose_to(oT, hb, T, 8, "hb", "oT")

        def ev(part, pbank, pkey):
            sl = slice(part * 512, (part + 1) * 512)
            STT("dve", hres[:T, sl], pbank[:T, :], CRES, hres[:T, sl], ALU.mult, ALU.add, [pkey, "hres"], ["hres"])
        proj_tok(T, oT, "oT", "wo", 2, ev)
        layer_norm(T, l, 1)

    def load_rope(T, row0):
        DMA(rc[:T, :], ropec[row0:row0 + T, :], (), ["rc"])
        DMA(rs[:T, :], ropes[row0:row0 + T, :], (), ["rs"])

    def load_tile(T, src):
        DMA(hres[:T, :], src, (), ["hres"])
        CP("dve", hb[:T, :], hres[:T, :], ["hres"], ["hb"])
        transpose_to(hT, hb, T, 8, "hb", "hT")

    def layer(T, l, Sl, Skey, dccol, zl, zkey, is_meta, kdst, vdst, past, pooldst):
        ffn(T, l, 0)
        retention(T, l, Sl, Skey, dccol)
        stick(T, l, kdst, vdst, past)
        pool_mixer(T, l, zl, zkey, is_meta, pooldst)
        out_proj(T, l)
        ffn(T, l, 1)

    ntiles_total = 2 + NSEQ * NT
    seqlist = []
    for _ in range(ntiles_total):
        for l in range(DEPTH):
            for ti in range(len(WT)):
                seqlist.append((l, ti))
    w_start(seqlist)

    T = NMETA
    load_rope(T, 0)
    load_tile(T, meta[:, :])
    for l in range(DEPTH):
        MSET("pool", Sm[l][:, :, :], 0.0, ["Sm%d" % l])
        MSET("pool", zT[l][:, :, :], 0.0, ["zT%d" % l])
        kd = [(nk[l, s, 0:NMETA, :], ("nk", l, s, "m")) for s in range(NSEQ)]
        vd = [(nv[l, s, 0:NMETA, :], ("nv", l, s, "m")) for s in range(NSEQ)]
        layer(T, l, Sm[l], "Sm%d" % l, 272, zT[l], "zT%d" % l, True, kd, vd, [], None)
        CP("pool", zm[l][:, :, :], zT[l][:, :, 0:PB], ["zT%d" % l], ["zm%d" % l])

    T = DEC
    load_rope(T, NMETA + SEQ)
    load_tile(T, xs[:, :])
    for l in range(DEPTH):
        DMA(S[l][:, :, :], sret[l].rearrange("h k v -> k h v"), (), ["S%d" % l])
        DMA(f1[:PB, :], spool[l, :, :], (), ["f1"])
        for half in range(2):
            b = nf()
            for j in range(4):
                c = half * 4 + j
                P.add("pe", lambda e, b=b, j=j, c=c: e.transpose(pf[b][:, j * PB:(j + 1) * PB], f1[:PB, c * 128:(c + 1) * 128], identf[:PB, :PB]),
                      ["f1", "C"], ["pf%d" % b])
            CP("act", zT[l][:, half * 4:half * 4 + 4, 0:PB], pf[b][:, 0:4 * PB].rearrange("p (c t) -> p c t", c=4),
               ["pf%d" % b], ["zT%d" % l])
        nblk = PAST // 128
        past = [(ck[l, bk * 128:(bk + 1) * 128, :], cv[l, bk * 128:(bk + 1) * 128, :], 128, []) for bk in reversed(range(nblk))]
        layer(T, l, S[l], "S%d" % l, 268, zT[l], "zT%d" % l, False, [(sk[l, :, :], ("sk", l))], [(sv[l, :, :], ("sv", l))],
              past, [(spoolo[l, :, :], ("spoolo", l))])
        DMA(sreto[l].rearrange("h k v -> k h v"), S[l][:, :, :], ["S%d" % l], [("sreto", l)])
    DMA(ys[:, :], hres[:T, :], ["hres"], ["ys"])

    T = 128
    for s in range(NSEQ):
        for l in range(DEPTH):
            CP("pool", S[l][:, :, :], Sm[l][:, :, :], ["Sm%d" % l], ["S%d" % l])
            CP("pool", zT[l][:, :, 0:PB], zm[l][:, :, :], ["zm%d" % l], ["zT%d" % l])
        for it in range(NT):
            load_rope(T, NMETA + it * 128)
            load_tile(T, x[s, it * 128:(it + 1) * 128, :])
            for l in range(DEPTH):
                r0 = NMETA + it * 128
                kd = [(nk[l, s, r0:r0 + 128, :], ("nk", l, s, it))]
                vd = [(nv[l, s, r0:r0 + 128, :], ("nv", l, s, it))]
                past = []
                for pt in reversed(range(it)):
                    p0 = NMETA + pt * 128
                    past.append((nk[l, s, p0:p0 + 128, :], nv[l, s, p0:p0 + 128, :], 128, [("nk", l, s, pt), ("nv", l, s, pt)]))
                past.append((nk[l, s, 0:NMETA, :], nv[l, s, 0:NMETA, :], NMETA, [("nk", l, s, "m"), ("nv", l, s, "m")]))
                pd = [(npool[l, s, :, :], ("npool", l, s))] if it == NT - 1 else None
                layer(T, l, S[l], "S%d" % l, 264, zT[l], "zT%d" % l, False, kd, vd, past, pd)
            DMA(y[s, it * 128:(it + 1) * 128, :], hres[:T, :], ["hres"], [("y", s, it)])
        for l in range(DEPTH):
            DMA(nret[l, s].rearrange("h k v -> k h v"), S[l][:, :, :], ["S%d" % l], [("nret", l, s)])

    P.emit(nc, es)
    return nc, es


def make_consts(seq, dec, past):
    import ml_dtypes
    lg = lg_decay()
    cst = np.zeros((128, 640), np.float32)
    cst[:, 0:128] = np.eye(128, dtype=np.float32)
    m = np.arange(128)
    cst[:, 128:256] = (m[None, :] >= m[:, None]).astype(np.float32)
    for h in range(HR):
        cst[:, 256 + h] = np.exp(lg[h] * (m + 1.0))
        cst[:, 260 + h] = (DK ** -0.5) * np.exp(-lg[h] * (m + 1.0))
        cst[:, 264 + h] = np.exp(lg[h] * 128.0)
        cst[:, 268 + h] = np.exp(lg[h] * float(dec))
        cst[:, 272 + h] = np.exp(lg[h] * float(NMETA))
    cb = np.zeros((128, 512), np.float32)
    cb[:, 0:128] = np.eye(128)
    cb[:, 128:256] = (m[:, None] < m[None, :])
    cb[:, 256:384] = -(m[:, None] >= m[None, :]).astype(np.float32)
    cb[:, 384:512] = -1.0
    cstb = cb.astype(ml_dtypes.bfloat16)
    half = 64
    inv = (10000.0 ** (-np.arange(half, dtype=np.float32) / np.float32(half))).astype(np.float32)
    pos = np.concatenate([np.arange(NMETA), NMETA + np.arange(seq), NMETA + past + np.arange(dec)]).astype(np.float32)
    ang = (pos[:, None] * inv[None, :]).astype(np.float32)
    cos = np.cos(ang).astype(np.float32)
    sin = np.sin(ang).astype(np.float32)
    ropec = cos
    ropes = np.concatenate([-sin, sin], axis=1).astype(np.float32)
    rc = np.zeros((128, 8, NMETA), np.float32)
    t = np.arange(NMETA)
    for g, w in enumerate((2, 4, 8, 16)):
        rc[:, 2 * g:2 * g + 2, :] = (1.0 / np.minimum(t + 1, w))[None, None, :]
    return cst, cstb, ropec, ropes, rc.reshape(128, 8 * NMETA)


_CACHE = {}


def run(inputs, cfg_seq=None):
    inp = {k: np.asarray(v) for k, v in inputs.items()}
    B, SEQ, _ = inp["x_prompt"].shape
    DB, DEC, _ = inp["x_sample"].shape
    PAST = inp["cache_sb_k"].shape[2]
    assert B % NCORES == 0 and DB == NCORES
    NSEQ = B // NCORES
    cfg = Cfg(NSEQ, SEQ, PAST, DEC)
    nc, es = build(cfg)
    cst, cstb, ropec, ropes, rcnt = make_consts(SEQ, DEC, PAST)
    wf = [pack_layer(l, inp) for l in range(DEPTH)]
    in_maps = []
    for c in range(NCORES):
        in_maps.append({
            "x": np.ascontiguousarray(inp["x_prompt"][c * NSEQ:(c + 1) * NSEQ]),
            "xs": np.ascontiguousarray(inp["x_sample"][c]),
            "ck": np.ascontiguousarray(inp["cache_sb_k"][:, c].reshape(DEPTH, PAST, D)),
            "cv": np.ascontiguousarray(inp["cache_sb_v"][:, c].reshape(DEPTH, PAST, D)),
            "sret": np.ascontiguousarray(inp["state_ret"][:, c]),
            "spool": np.ascontiguousarray(inp["state_pool"][:, c]),
            "meta": inp["meta_tokens"],
            "wflat0": wf[0], "wflat1": wf[1],
            "lng": np.ascontiguousarray(inp["ln_g"].reshape(DEPTH * 3, D)),
            "lnb": np.ascontiguousarray(inp["ln_b"].reshape(DEPTH * 3, D)),
            "retg": np.ascontiguousarray(inp["ret_norm_g"].reshape(DEPTH, D)),
            "pscale": np.ascontiguousarray(inp["pool_scale"].reshape(DEPTH, D)),
            "ropec": ropec, "ropes": ropes, "cst": cst, "cstb": cstb, "rcnt": rcnt,
        })
    return nc, es, in_maps, cfg


def assemble(results, cfg):
    NSEQ, SEQ, DEC = cfg.nseq, cfg.seq, cfg.dec
    cat = lambda k, ax: np.concatenate([r[k] for r in results], axis=ax)
    y = cat("y", 0)
    ys = np.stack([r["ys"] for r in results], 0)
    nk = cat("nk", 1).reshape(DEPTH, NSEQ * NCORES, NMETA + SEQ, HS, DS)
    nv = cat("nv", 1).reshape(DEPTH, NSEQ * NCORES, NMETA + SEQ, HS, DS)
    nret = cat("nret", 1)
    npool = cat("npool", 1)
    sk = np.stack([r["sk"] for r in results], 1).reshape(DEPTH, NCORES, DEC, HS, DS)
    sv = np.stack([r["sv"] for r in results], 1).reshape(DEPTH, NCORES, DEC, HS, DS)
    sreto = np.stack([r["sreto"] for r in results], 1)
    spoolo = np.stack([r["spoolo"] for r in results], 1)
    return tuple(np.ascontiguousarray(a, dtype=np.float32) for a in (y, ys, nk, nv, nret, npool, sk, sv, sreto, spoolo))


def kernel(**inputs):
    nc, es, in_maps, cfg = run(inputs)
    with es:
        res = run_bass_kernel_spmd(nc, in_maps, core_ids=list(range(NCORES)))
    return assemble(res.results, cfg)
```
